# Optimizing a Trainium2 kernel written in Bass

```python
import math
import jax, jax.numpy as jnp
from jax import lax
import numpy as np

D_MODEL = 1024
BATCH = 8
SEQ = 2048
DEPTH = 2

CONV_WIDTH = 1024
CONV_KERNEL = 31
SSM_WIDTH = 512
SSM_GROUP = 16
SSM_GROUPS = SSM_WIDTH // SSM_GROUP
SSM_STATE = 64
DT_MIN = 1e-3
DT_MAX = 1e-1
EVEN_IN = 3 * CONV_WIDTH + 2 * SSM_WIDTH
EVEN_MIX = CONV_WIDTH + SSM_WIDTH
ATTN_HEADS = 16
ATTN_HEAD_DIM = 64
ATTN_WIDTH = ATTN_HEADS * ATTN_HEAD_DIM
ODD_IN = 4 * ATTN_WIDTH
Q_BLOCK = 128
EPS = 1e-6

kernel_name = "hybrid_conformer_s5_stickbreaking_block"


def rmsnorm(x, g):
    xf = x.astype(jnp.float32)
    y = xf * lax.rsqrt(jnp.mean(xf * xf, axis=-1, keepdims=True) + EPS)
    return (y * g.astype(jnp.float32)).astype(x.dtype)


def layernorm(x, g, b):
    xf = x.astype(jnp.float32)
    mu = jnp.mean(xf, axis=-1, keepdims=True)
    var = jnp.mean(jnp.square(xf - mu), axis=-1, keepdims=True)
    y = (xf - mu) * lax.rsqrt(var + EPS)
    return (y * g.astype(jnp.float32) + b.astype(jnp.float32)).astype(x.dtype)


def adaln(c, w_ada, b_ada):
    mod = jnp.einsum('bd,de->be', jax.nn.silu(c), w_ada) + b_ada
    shift, scale, gate = jnp.split(mod, 3, axis=-1)
    return shift[:, None, :], scale[:, None, :], gate[:, None, :]


def conformer_conv(val, glu_gate, conv_w, conv_b, ln_g, ln_b):
    h = val * jax.nn.sigmoid(glu_gate)
    h = lax.conv_general_dilated(
        h, conv_w[:, None, :], window_strides=(1,),
        padding=[(CONV_KERNEL - 1, 0)],
        dimension_numbers=('NWC', 'WIO', 'NWC'),
        feature_group_count=h.shape[-1]) + conv_b
    return jax.nn.silu(layernorm(h, ln_g, ln_b))


def _ssm_combine(e1, e2):
    a1, b1 = e1
    a2, b2 = e2
    return a1 * a2, a2 * b1 + b2


def s5_branch(u, lam_re, lam_im, log_dt, b_re, b_im, c_re, c_im, d_skip, w_glu, b_glu):
    f32 = jnp.float32
    bsz, seq, _ = u.shape
    lam = lax.complex(lam_re.astype(f32), lam_im.astype(f32))
    dt = jnp.exp(log_dt.astype(f32))[:, None]
    a_bar = jnp.exp(lam * dt)
    bmat = lax.complex(b_re.astype(f32), b_im.astype(f32))
    b_bar = ((a_bar - 1.0) / lam)[..., None] * bmat
    uf = u.astype(f32)
    ug = uf.reshape(bsz, seq, SSM_GROUPS, SSM_GROUP)
    bu = lax.complex(jnp.einsum('blgi,gpi->blgp', ug, jnp.real(b_bar)),
                     jnp.einsum('blgi,gpi->blgp', ug, jnp.imag(b_bar)))
    a = jnp.broadcast_to(a_bar, bu.shape)
    _, state = lax.associative_scan(_ssm_combine, (a, bu), axis=1)
    y = (jnp.einsum('blgp,gip->blgi', jnp.real(state), c_re.astype(f32))
         - jnp.einsum('blgp,gip->blgi', jnp.imag(state), c_im.astype(f32)))
    y = y.reshape(bsz, seq, SSM_WIDTH) + d_skip.astype(f32) * uf
    y = jax.nn.gelu(y)
    y = y * jax.nn.sigmoid(y @ w_glu.astype(f32) + b_glu.astype(f32))
    return y.astype(u.dtype)


def stick_breaking_attention(q, k, v):
    seq = q.shape[2]
    scale = ATTN_HEAD_DIM ** -0.5
    outs = []
    for blk in range(seq // Q_BLOCK):
        q0 = blk * Q_BLOCK
        kl = q0 + Q_BLOCK
        z = jnp.einsum('bhqd,bhkd->bhqk', q[:, :, q0:kl], k[:, :, :kl]).astype(jnp.float32) * scale
        t_idx = q0 + jnp.arange(Q_BLOCK)[:, None]
        s_idx = jnp.arange(kl)[None, :]
        mask = s_idx < t_idx
        log_1m = jnp.where(mask, jax.nn.log_sigmoid(-z), 0.0)
        between = lax.cumsum(log_1m, axis=3, reverse=True) - log_1m
        w = jnp.where(mask, jnp.exp(jax.nn.log_sigmoid(z) + between), 0.0)
        outs.append(jnp.einsum('bhqk,bhkd->bhqd', w.astype(v.dtype), v[:, :, :kl]))
    return jnp.concatenate(outs, axis=2)


def even_layer(x, c, norm_g, w_ada, b_ada, w_in, conv_w, conv_b, conv_ln_g, conv_ln_b,
               lam_re, lam_im, log_dt, b_re, b_im, c_re, c_im, d_skip, w_glu, b_glu, w_out):
    shift, scale, gate = adaln(c, w_ada, b_ada)
    h = rmsnorm(x, norm_g) * (1.0 + scale) + shift
    proj = h @ w_in
    val_a, glu_a, gate_a, u_b, gate_b = jnp.split(
        proj, [CONV_WIDTH, 2 * CONV_WIDTH, 3 * CONV_WIDTH, 3 * CONV_WIDTH + SSM_WIDTH], axis=-1)
    y_a = conformer_conv(val_a, glu_a, conv_w, conv_b, conv_ln_g, conv_ln_b) * jax.nn.silu(gate_a)
    y_b = s5_branch(u_b, lam_re, lam_im, log_dt, b_re, b_im, c_re, c_im, d_skip, w_glu, b_glu) * jax.nn.silu(gate_b)
    y = jnp.concatenate([y_a, y_b], axis=-1) @ w_out
    return x + gate * y


def odd_layer(x, c, norm_g, w_ada, b_ada, w_in, w_out):
    bsz, seq, _ = x.shape
    shift, scale, gate = adaln(c, w_ada, b_ada)
    h = rmsnorm(x, norm_g) * (1.0 + scale) + shift
    q, k, v, g = jnp.split(h @ w_in, 4, axis=-1)
    heads = lambda t: t.reshape(bsz, seq, ATTN_HEADS, ATTN_HEAD_DIM).transpose(0, 2, 1, 3)
    o = stick_breaking_attention(heads(q), heads(k), heads(v))
    o = o.transpose(0, 2, 1, 3).reshape(bsz, seq, ATTN_WIDTH) * jax.nn.silu(g)
    return x + gate * (o @ w_out)


def setup_inputs(seed: int = 0) -> dict:
    key = jax.random.key(seed)
    ks = iter(jax.random.split(key, 32))
    f32 = jnp.float32
    nrm = lambda shape, s: jax.random.normal(next(ks), shape, f32) * s
    d = D_MODEL
    n_idx = jnp.arange(SSM_STATE, dtype=f32)[None, :]
    return {
        "x": nrm((BATCH, SEQ, d), 1.0),
        "c": nrm((BATCH, d), 1.0),
        "l0_norm_g": 1.0 + nrm((d,), 0.02),
        "l0_w_ada": nrm((d, 3 * d), d ** -0.5),
        "l0_b_ada": nrm((3 * d,), 0.02),
        "l0_w_in": nrm((d, EVEN_IN), d ** -0.5),
        "l0_conv_w": nrm((CONV_KERNEL, CONV_WIDTH), CONV_KERNEL ** -0.5),
        "l0_conv_b": nrm((CONV_WIDTH,), 0.02),
        "l0_conv_ln_g": 1.0 + nrm((CONV_WIDTH,), 0.02),
        "l0_conv_ln_b": nrm((CONV_WIDTH,), 0.02),
        "l0_ssm_lam_re": -0.5 + nrm((SSM_GROUPS, SSM_STATE), 0.01),
        "l0_ssm_lam_im": math.pi * n_idx + nrm((SSM_GROUPS, SSM_STATE), 0.01),
        "l0_ssm_log_dt": jax.random.uniform(next(ks), (SSM_GROUPS,), f32,
                                             math.log(DT_MIN), math.log(DT_MAX)),
        "l0_ssm_b_re": nrm((SSM_GROUPS, SSM_STATE, SSM_GROUP), (2 * SSM_GROUP) ** -0.5),
        "l0_ssm_b_im": nrm((SSM_GROUPS, SSM_STATE, SSM_GROUP), (2 * SSM_GROUP) ** -0.5),
        "l0_ssm_c_re": nrm((SSM_GROUPS, SSM_GROUP, SSM_STATE), (2 * SSM_STATE) ** -0.5),
        "l0_ssm_c_im": nrm((SSM_GROUPS, SSM_GROUP, SSM_STATE), (2 * SSM_STATE) ** -0.5),
        "l0_ssm_d": 1.0 + nrm((SSM_WIDTH,), 0.1),
        "l0_ssm_w_glu": nrm((SSM_WIDTH, SSM_WIDTH), SSM_WIDTH ** -0.5),
        "l0_ssm_b_glu": nrm((SSM_WIDTH,), 0.02),
        "l0_w_out": nrm((EVEN_MIX, d), EVEN_MIX ** -0.5),
        "l1_norm_g": 1.0 + nrm((d,), 0.02),
        "l1_w_ada": nrm((d, 3 * d), d ** -0.5),
        "l1_b_ada": nrm((3 * d,), 0.02),
        "l1_w_in": nrm((d, ODD_IN), d ** -0.5),
        "l1_w_out": nrm((ATTN_WIDTH, d), ATTN_WIDTH ** -0.5),
        "final_norm_g": 1.0 + nrm((d,), 0.02),
    }


def reference(x, c, l0_norm_g, l0_w_ada, l0_b_ada, l0_w_in, l0_conv_w, l0_conv_b, l0_conv_ln_g,
              l0_conv_ln_b, l0_ssm_lam_re, l0_ssm_lam_im, l0_ssm_log_dt, l0_ssm_b_re, l0_ssm_b_im,
              l0_ssm_c_re, l0_ssm_c_im, l0_ssm_d, l0_ssm_w_glu, l0_ssm_b_glu, l0_w_out,
              l1_norm_g, l1_w_ada, l1_b_ada, l1_w_in, l1_w_out, final_norm_g):
    even_params = (l0_norm_g, l0_w_ada, l0_b_ada, l0_w_in, l0_conv_w, l0_conv_b, l0_conv_ln_g,
                   l0_conv_ln_b, l0_ssm_lam_re, l0_ssm_lam_im, l0_ssm_log_dt, l0_ssm_b_re,
                   l0_ssm_b_im, l0_ssm_c_re, l0_ssm_c_im, l0_ssm_d, l0_ssm_w_glu, l0_ssm_b_glu,
                   l0_w_out)
    odd_params = (l1_norm_g, l1_w_ada, l1_b_ada, l1_w_in, l1_w_out)
    for layer in range(DEPTH):
        if layer % 2 == 0:
            x = even_layer(x, c, *even_params)
        else:
            x = odd_layer(x, c, *odd_params)
    return rmsnorm(x, final_norm_g)
```

```python
import math
import numpy as np
import concourse.bass as bass
import concourse.mybir as mybir
from concourse.bass_utils import run_bass_kernel_spmd

F32 = mybir.dt.float32
BF16 = mybir.dt.bfloat16
AF = mybir.ActivationFunctionType
ALU = mybir.AluOpType

L = 2048
D = 1024
NB = 4
TB = 512
EPS = 1e-6
PI = math.pi
SAME_ENG_SYNC = True
N_FILL = 0
ADA1_UPFRONT = False
P3ENG = "vector"

_VEC = {}
_off = 0
for _n, _w in [("norm_g0", 8), ("b_ada0", 24), ("conv_b", 8), ("ln_g", 8), ("ln_b", 8), ("ssm_d", 4),
               ("b_glu", 4), ("norm_g1", 8), ("b_ada1", 24), ("final_g", 8), ("cT", 8), ("conv_w", 248),
               ("lamre", 16), ("lamim", 16), ("logdt", 16)]:
    _VEC[_n] = (_off, _w)
    _off += _w
NV = _off


class _Op:
    __slots__ = ("eng", "fn", "deps", "signal", "val", "tag", "is_dma")


class Prog:
    ENG = ("sync", "tensor", "vector", "scalar", "gpsimd")

    def __init__(self, nc):
        self.nc = nc
        self.ops = {e: [] for e in self.ENG}
        self.lastw = {}
        self.readers = {}
        self.pend = {e: [] for e in self.ENG}
        self.dma_count = {}
        self.last_dma = {}

    def _add(self, op, r, w):
        deps = set()
        for k in r:
            o = self.lastw.get(k)
            if o is not None:
                deps.add(o)
        for k in w:
            o = self.lastw.get(k)
            if o is not None:
                deps.add(o)
            for o in self.readers.get(k, ()):
                deps.add(o)
        for o in self.pend[op.eng]:
            deps.add(o)
        self.pend[op.eng] = []
        deps.discard(op)
        op.deps = deps
        for d in deps:
            d.signal = True
        for k in r:
            self.readers.setdefault(k, []).append(op)
        for k in w:
            self.lastw[k] = op
            self.readers[k] = []
        self.ops[op.eng].append(op)

    def op(self, eng, fn, r=(), w=()):
        o = _Op()
        o.eng = eng; o.fn = fn; o.signal = False; o.val = 0; o.tag = None; o.is_dma = False
        self._add(o, r, w)
        return o

    def dma(self, fn, tag, r=(), w=(), q="sync"):
        o = _Op()
        o.eng = q; o.fn = fn; o.signal = True; o.tag = tag; o.is_dma = True
        self.dma_count[tag] = self.dma_count.get(tag, 0) + 1
        o.val = 16 * self.dma_count[tag]
        self.last_dma[tag] = o
        self._add(o, r, w)
        return o

    def barrier(self):
        evs = []
        for e in self.ENG:
            if self.ops[e]:
                o = self.ops[e][-1]
                o.signal = True
                evs.append(o)
        for t, o in self.last_dma.items():
            evs.append(o)
        for e in self.ENG:
            self.pend[e] = list(evs)

    def final_wait(self, tags):
        o = _Op()
        o.eng = "sync"; o.fn = (lambda e: None); o.signal = False; o.val = 0; o.tag = None; o.is_dma = False
        o.deps = set(self.last_dma[t] for t in tags)
        self.ops["sync"].append(o)

    def emit(self):
        nc = self.nc
        for e in self.ENG:
            cnt = 0
            for o in self.ops[e]:
                if o.is_dma:
                    continue
                if o.signal:
                    cnt += 1
                    o.val = cnt
        sems = {e: nc.alloc_semaphore("sem_" + e) for e in self.ENG}
        dsems = {t: nc.alloc_semaphore("dsem_" + t) for t in self.dma_count}
        ops = self.ops
        with nc.Block() as block:
            for e in self.ENG:
                def body(eng, e=e):
                    waited = {}
                    for o in ops[e]:
                        need = {}
                        for d in o.deps:
                            if d.is_dma:
                                nm = "d_" + d.tag; sem = dsems[d.tag]
                            else:
                                if d.eng == e and (e == "tensor" or e == "sync" or not SAME_ENG_SYNC):
                                    continue
                                nm = "e_" + d.eng; sem = sems[d.eng]
                            if d.val > need.get(nm, (None, 0))[1]:
                                need[nm] = (sem, d.val)
                        for nm, (sem, val) in need.items():
                            if waited.get(nm, 0) >= val:
                                continue
                            eng.wait_ge(sem, val)
                            waited[nm] = val
                        ins = o.fn(eng)
                        if ins is None:
                            continue
                        if o.is_dma:
                            ins.then_inc(dsems[o.tag], 16)
                        elif o.signal:
                            ins.then_inc(sems[e], 1)
                getattr(block, e)(body)


class Pool:
    def __init__(self, nc, name, n, shape, dtype, tiles=None):
        if tiles is None:
            self.t = [nc.alloc_sbuf_tensor("%s%d" % (name, i), list(shape), dtype)[:, :] for i in range(n)]
        else:
            self.t = list(tiles)
        self.k = ["%s%d" % (name, i) for i in range(len(self.t))]
        self.i = 0

    def next(self):
        j = self.i % len(self.t)
        self.i += 1
        return self.t[j], self.k[j]


def build_program(do_l0=True, do_l1=True):
    nc = bass.Bass("TRN2", target_bir_lowering=False)
    P = Prog(nc)

    def din(name, shape):
        return nc.dram_tensor(name, list(shape), F32, kind="ExternalInput")

    xT_d = din("xT", [128, 8, L])
    vecs_d = din("vecs", [128, NV])
    wada_d = [din("w_ada0", [8, 128, 3072]), din("w_ada1", [8, 128, 3072])]
    win0_d = din("w_in0", [32, 128, 1024])
    wout0_d = din("w_out0", [8, 128, 1536])
    win1_d = din("w_in1", [32, 128, 1024])
    wout1_d = din("w_out1", [8, 128, 1024])
    wglu_d = din("w_glu", [128, 2048])
    bpre_d = din("bp_re", [128, 2048])
    bpim_d = din("bp_im", [128, 2048])
    cpre_d = din("cp_re", [128, 2048])
    cpim_d = din("cp_im", [128, 2048])
    consts_d = din("consts", [128, 5 * 128 + 512])
    out_d = nc.dram_tensor("out", [L, D], F32, kind="ExternalOutput")
    dgd = nc.dram_tensor("dgd", [8, 128, 31 * 128], BF16)

    sb = nc.alloc_sbuf_tensor
    xT = sb("xT_sb", [128, 8, L], F32)
    vecs = sb("vecs_sb", [128, NV], F32)
    consts = sb("consts_sb", [128, 2 * 128 + 512], F32)
    ident_f = consts[:, 0:128]
    ones_f = consts[:, 128:256]
    iota_f = consts[:, 256:768]
    c16 = sb("c16", [128, 8 * 128], BF16)
    ident16 = c16[:, 0:128]
    negones16 = c16[:, 128:256]
    tri01_16 = c16[:, 256:384]
    nti16 = c16[:, 384:512]
    negbig16 = c16[:, 512:640]
    zeros16 = c16[:, 640:768]
    ones16 = c16[:, 768:896]
    halfid16 = c16[:, 896:1024]
    small = sb("small", [128, 256], F32)
    rsd = sb("rsd", [128, 512], F32)
    lnm = sb("lnm", [128, 512], F32)
    lnr = sb("lnr", [128, 512], F32)

    def V(name, j0=0, j1=None):
        o, w = _VEC[name]
        if j1 is None:
            j1 = w
        return vecs[:, o + j0:o + j1]

    arena = sb("arena", [128, 16640], F32)
    arena16 = arena.bitcast(BF16)

    tmpf = Pool(nc, "tf", 4, [128, 512], F32)
    parena = sb("parena", [128, 9728], F32)
    parena16 = parena.bitcast(BF16)
    plong = Pool(nc, "pl", 10, None, None, tiles=[parena[:, i * 512:(i + 1) * 512] for i in range(10)])
    pshort = Pool(nc, "ps_", 2, None, None, tiles=[parena[:, 5120 + i * 512:5120 + (i + 1) * 512] for i in range(2)])
    pg = Pool(nc, "pg", 2, None, None, tiles=[parena[:, 6144 + i * 512:6144 + (i + 1) * 512] for i in range(2)])
    tmph = Pool(nc, "th", 6, [128, 512], BF16)
    wstT = sb("wstT", [128, 2048], F32)
    wst = Pool(nc, "wst", 2, None, None, tiles=[wstT[:, 0:1024], wstT[:, 1024:2048]])
    wbfT = [sb("wbfT%d" % i, [128, 1024], BF16) for i in range(2)]
    wbf = Pool(nc, "wbf", 2, None, None, tiles=[t_[:, :] for t_ in wbfT])
    wstT16 = wstT.bitcast(BF16)
    wring_f32 = [t_.bitcast(F32)[:, 0:512] for t_ in wbfT] + [wstT[:, i * 512:(i + 1) * 512] for i in range(4)]
    wring = Pool(nc, "wring", 6, None, None, tiles=wbf.t + [wstT16[:, i * 1024:(i + 1) * 1024] for i in range(4)])
    wring.k = list(wbf.k) + ["wrx0", "wrx1", "wrx2", "wrx3"]
    win0b = nc.dram_tensor("win0b", [32, 128, 1024], BF16)
    wout0b = nc.dram_tensor("wout0b", [8, 128, 1536], BF16)
    dgt = parena16[:, 14336:14336 + 3968]
    glup = Pool(nc, "glu", 2, None, None, tiles=[parena16[:, 18304 + i * 544:18304 + (i + 1) * 544] for i in range(2)])

    psb = [nc.alloc_psum_tensor("psb%d" % i, [128, 512], F32) for i in range(8)]
    ps_cnt = {}

    def psum(role, banks):
        i = ps_cnt.get(role, 0)
        ps_cnt[role] = i + 1
        b = banks[i % len(banks)]
        return psb[b][:, :], "ps%d" % b

    def ACT(out, in_, func, r, w, **kw):
        P.op("scalar", lambda e: e.activation(out=out, in_=in_, func=func, **kw), r, w)

    def MM(out, lhsT, rhs, start, stop, r, w, skip=False):
        if skip:
            P.op("tensor", lambda e: e.matmul(out, lhsT, rhs, start=start, stop=stop, skip_group_check=True), r, w)
        else:
            P.op("tensor", lambda e: e.matmul(out, lhsT, rhs, start=start, stop=stop), r, w)

    def TT(eng, out, in0, in1, op, r, w):
        P.op(eng, lambda e: e.tensor_tensor(out=out, in0=in0, in1=in1, op=op), r, w)

    def TS(eng, out, in0, s1, s2, op0, op1, r, w):
        if op1 is None:
            P.op(eng, lambda e: e.tensor_scalar(out=out, in0=in0, scalar1=s1, scalar2=None, op0=op0), r, w)
        else:
            P.op(eng, lambda e: e.tensor_scalar(out=out, in0=in0, scalar1=s1, scalar2=s2, op0=op0, op1=op1), r, w)

    def STT(out, in0, scalar, in1, op0, op1, r, w):
        P.op("vector", lambda e: e.scalar_tensor_tensor(out=out, in0=in0, scalar=scalar, in1=in1,
                                                        op0=op0, op1=op1), r, w)

    def CP(eng, out, in_, r, w):
        P.op(eng, lambda e: e.tensor_copy(out=out, in_=in_), r, w)

    def MEMSET(eng, ap, val, w):
        P.op(eng, lambda e: e.memset(ap, val), (), w)

    def DMA(out, in_, tag, r, w, q="sync"):
        P.dma(lambda e: e.dma_start(out=out, in_=in_), tag, r, w, q=q)

    def RECIP(out, in_, r, w):
        P.op("vector", lambda e: e.reciprocal(out=out, in_=in_), r, w)

    MAGIC = 12582912.0
    PI_LO = 3.1415925
    CW1 = 6.28125
    CW2 = 2.0 * PI - 6.28125

    def sincos(x, sn, cs, kx, ksn, kcs):
        ACT(cs, x, AF.Identity, [kx], [kcs], scale=1.0 / (2.0 * PI), bias=MAGIC)
        ACT(cs, cs, AF.Identity, [kcs], [kcs], bias=-MAGIC)
        STT(sn, cs, -CW1, x, ALU.mult, ALU.add, [kcs, kx], [ksn])
        STT(sn, cs, -CW2, sn, ALU.mult, ALU.add, [kcs, ksn], [ksn])
        TS("vector", sn, sn, -PI_LO, PI_LO, ALU.max, ALU.min, [ksn], [ksn])
        STT(cs, sn, -1.0, sn, ALU.mult, ALU.max, [ksn], [kcs])
        ACT(cs, cs, AF.Sin, [kcs, "small"], [kcs], scale=-1.0, bias=halfpi)
        ACT(sn, sn, AF.Sin, [ksn], [ksn])

    DMA(vecs[:, :], vecs_d[:, :], "vecs", [], ["vecs"])
    DMA(consts[:, 0:256], consts_d[:, 0:256], "consts", [], ["const"])
    DMA(consts[:, 256:768], consts_d[:, 640:1152], "consts", [], ["const"])
    cstage = arena[:, 0:384]
    DMA(cstage, consts_d[:, 256:640], "cstage", [], ["cstage"])
    tri01_f = cstage[:, 0:128]
    nti_f = cstage[:, 128:256]
    negbig_f = cstage[:, 256:384]
    for c in range(8):
        DMA(xT[:, c, :], xT_d[:, c, :], "xload%d" % c, [], ["x%d_%d" % (c, n) for n in range(NB)])
    CP("gpsimd", ident16, ident_f, ["const"], ["c16"])
    TS("gpsimd", negones16, ones_f, -1.0, None, ALU.mult, None, ["const"], ["c16"])
    CP("gpsimd", tri01_16, tri01_f, ["cstage"], ["c16"])
    CP("gpsimd", nti16, nti_f, ["cstage"], ["c16"])
    CP("gpsimd", negbig16, negbig_f, ["cstage"], ["c16"])
    MEMSET("gpsimd", zeros16, 0.0, ["c16"])
    CP("gpsimd", ones16, ones_f, ["const"], ["c16"])
    TS("gpsimd", halfid16, ident_f, 0.5, None, ALU.mult, None, ["const"], ["c16"])
    MEMSET("gpsimd", small[:, :], 0.0, ["small", "e512"])
    sc = small[:, 0:8]
    mods = [small[:, 8:32], small[:, 32:56]]
    gsv = [small[:, 56:64], small[:, 64:72]]
    negpi = small[:, 72:73]
    onec = small[:, 73:74]
    MEMSET("gpsimd", negpi, -PI, ["small"])
    MEMSET("gpsimd", onec, 1.0, ["small"])
    halfpi = small[:, 74:75]
    MEMSET("gpsimd", halfpi, PI / 2.0, ["small"])
    ACT(sc, V("cT"), AF.Silu, ["vecs", "small"], ["small"])

    s5 = sb("s5", [128, 16 * 16], F32)

    def S(i):
        return s5[:, i * 16:(i + 1) * 16]
    dt_, xr, th, rr, asn, acs, sn_, cs_, are, aim, den, cr, ci, t1_, t2_, nci = [S(i) for i in range(16)]
    K5 = ["s5"]
    c512 = small[:, 144:160]
    s512 = small[:, 160:176]
    ns512 = small[:, 176:192]
    tabd = nc.dram_tensor("tabd", [16, 128, 1024], F32)
    if do_l0:
        ACT(dt_, V("logdt"), AF.Exp, ["vecs"], K5)
        TT("vector", xr, V("lamre"), dt_, ALU.mult, K5 + ["vecs"], K5)
        TT("vector", th, V("lamim"), dt_, ALU.mult, K5 + ["vecs"], K5)
        ACT(rr, xr, AF.Exp, K5, K5)
        TS("vector", asn, th, 512.0, None, ALU.mult, None, K5, K5)
        sincos(asn, s512, c512, "s5", "e512", "e512")
        TS("vector", ns512, s512, -1.0, None, ALU.mult, None, ["e512"], ["e512"])

    def gen_table(q):
        if True:
            base, kb_ = pshort.next()
            sn, ksn = plong.next()
            cs, kcs = plong.next()
            ACT(base, iota_f, AF.Identity, ["const", "s5"], [kb_], scale=th[:, q:q + 1])
            sincos(base, sn, cs, kb_, ksn, kcs)
            DMA(tabd[q, :, 0:512], cs, "to" + kcs, [kcs], ["tabd%d" % q], q="gpsimd")
            DMA(tabd[q, :, 512:1024], sn, "to" + ksn, [ksn], ["tabd%d" % q], q="gpsimd")

    sc16 = sb("sc16", [128, 8], BF16)[:, :]
    CP("vector", sc16, sc, ["small"], ["sc16"])
    for l in range(2 if ADA1_UPFRONT else 1):
        for kc in range(8):
            if do_l0:
                gen_table(l * 8 + kc)
            ps, kp = psum("ada", [2])
            for g in range(6):
                st, ks = tmpf.next()
                DMA(st, wada_d[l][kc, :, g * 512:(g + 1) * 512], "d" + ks, [], [ks])
                w16_, k16 = tmph.next()
                CP("vector", w16_, st, [ks], [k16])
                for j in range(4):
                    MM(ps[:, g * 4 + j:g * 4 + j + 1], w16_[:, j * 128:(j + 1) * 128], sc16[:, kc:kc + 1], True, True,
                       [k16, "sc16"], [kp])
            if kc == 0:
                CP("vector", mods[l], ps[:, 0:24], [kp], ["small"])
            else:
                TT("vector", mods[l], mods[l], ps[:, 0:24], ALU.add, [kp, "small"], ["small"])
        bname = "b_ada%d" % l
        TT("vector", mods[l], mods[l], V(bname), ALU.add, ["small", "vecs"], ["small"])
        TS("vector", gsv[l], mods[l][:, 8:16], 1.0, None, ALU.add, None, ["small"], ["small"])
        TT("vector", gsv[l], gsv[l], V("norm_g%d" % l), ALU.mult, ["small", "vecs"], ["small"])
    if do_l0 and not ADA1_UPFRONT:
        for q_ in range(8, 16):
            gen_table(q_)

    def ada1_ring_tasks():
        tasks = []
        psst = {}

        def mk(kc, g):
            def f():
                i_ = wring.i % 6
                _, ks = wring.next()
                st = wring_f32[i_]
                DMA(st, wada_d[1][kc, :, g * 512:(g + 1) * 512], "d" + ks, [], [ks])
                j_ = wring.i % 6
                w16t, k16 = wring.next()
                w16_ = w16t[:, 0:512]
                CP("vector", w16_, st, [ks], [k16])
                ps, kp = psum("misc", [7])
                for j in range(4):
                    MM(ps[:, j:j + 1], w16_[:, j * 128:(j + 1) * 128], sc16[:, kc:kc + 1], True, True,
                       [k16, "sc16"], [kp])
                dst = mods[1][:, g * 4:(g + 1) * 4]
                if kc == 0:
                    CP("vector", dst, ps[:, 0:4], [kp], ["mod1"])
                else:
                    TT("vector", dst, dst, ps[:, 0:4], ALU.add, [kp, "mod1"], ["mod1"])
            return f
        for kc in range(8):
            for g in range(6):
                tasks.append(mk(kc, g))

        def fin():
            TT("vector", mods[1], mods[1], V("b_ada1"), ALU.add, ["mod1", "vecs"], ["mod1"])
            TS("vector", gsv[1], mods[1][:, 8:16], 1.0, None, ALU.add, None, ["mod1"], ["mod1"])
            TT("vector", gsv[1], gsv[1], V("norm_g1"), ALU.mult, ["mod1", "vecs"], ["mod1"])
        tasks.append(fin)
        return tasks

    def ada1_tasks():
        tasks = []

        def mk(kc, g):
            def f():
                st, ks = wst.next()
                DMA(st, wada_d[1][kc, :, g * 1024:(g + 1) * 1024], ks, [], [ks])
                ps, kp = psum("misc", [7])
                for j in range(8):
                    MM(ps[:, j:j + 1], st[:, j * 128:(j + 1) * 128], sc[:, kc:kc + 1], True, True, [ks], [kp])
                dst = mods[1][:, g * 8:(g + 1) * 8]
                if kc == 0:
                    CP("vector", dst, ps[:, 0:8], [kp], ["mod1"])
                else:
                    TT("vector", dst, dst, ps[:, 0:8], ALU.add, [kp, "mod1"], ["mod1"])
            return f
        for kc in range(8):
            for g in range(3):
                tasks.append(mk(kc, g))

        def fin():
            TT("vector", mods[1], mods[1], V("b_ada1"), ALU.add, ["mod1", "vecs"], ["mod1"])
            TS("vector", gsv[1], mods[1][:, 8:16], 1.0, None, ALU.add, None, ["mod1"], ["mod1"])
            TT("vector", gsv[1], gsv[1], V("norm_g1"), ALU.mult, ["mod1", "vecs"], ["mod1"])
        tasks.append(fin)
        return tasks
    shiftv = [mods[0][:, 0:8], mods[1][:, 0:8]]
    gatev = [mods[0][:, 16:24], mods[1][:, 16:24]]

    def rms_rstd(n, tag):
        blk = slice(n * TB, (n + 1) * TB)
        ps, kp = psum("stat", [2, 3])
        for c in range(8):
            sq, ksq = tmpf.next()
            ACT(sq, xT[:, c, blk], AF.Square, ["x%d_%d" % (c, n)], [ksq])
            MM(ps, ones_f, sq, c == 0, c == 7, [ksq, "const"], [kp])
        rs, krs = rsd[:, :], "rsd"
        TS("vector", rs, ps, 1.0 / D, EPS, ALU.mult, ALU.add, [kp], [krs])
        ACT(rs, rs, AF.Sqrt, [krs], [krs])
        RECIP(rs, rs, [krs], [krs])
        return rs, krs

    def load_w(src, width, eng="scalar", scratch=None, skey=None, reload=False):
        if reload:
            wb, kb = wring.next()
            DMA(wb[:, 0:width], scratch, "d" + kb, [skey], [kb])
            return wb, kb
        st, ks = wst.next()
        DMA(st[:, 0:width], src, ks, [], [ks])
        wb, kb = wbf.next()
        if eng == "scalar":
            ACT(wb[:, 0:width], st[:, 0:width], AF.Copy, [ks], [kb])
        else:
            CP(eng, wb[:, 0:width], st[:, 0:width], [ks], [kb])
        if scratch is not None:
            DMA(scratch, wb[:, 0:width], "ws" + kb, [kb], [skey])
        return wb, kb

    if do_l0:
        hT = arena16[:, 0:4096].rearrange("p (c t) -> p c t", c=8)
        convo = arena16[:, 4096:8192].rearrange("p (c t) -> p c t", c=8)
        y16 = arena16[:, 8192:14336].rearrange("p (c t) -> p c t", c=12)
        u16 = arena16[:, 14336:16384].rearrange("p (c t) -> p c t", c=4)
        yg = arena[:, 8192:10240].rearrange("p (c t) -> p c t", c=4)
        yg16 = arena16[:, 20480:22528].rearrange("p (c t) -> p c t", c=4)
        halo = arena16[:, 22528:22784].rearrange("p (c t) -> p c t", c=8)
        bbT = [arena16[:, 22784:24832].rearrange("p (q m) -> p q m", q=16),
               arena16[:, 24832:26880].rearrange("p (q m) -> p q m", q=16)]
        cT16 = [arena16[:, 26880:28928].rearrange("p (q m) -> p q m", q=16),
                arena16[:, 28928:30976].rearrange("p (q m) -> p q m", q=16)]
        wglu16 = arena16[:, 30976:33024]
        stg = [arena[:, 0:2048], arena[:, 2048:4096], arena[:, 4096:6144], arena[:, 6144:8192]]
        DMA(stg[0], bpre_d[:, :], "stg0", [], ["stg0", "cstage"])
        DMA(stg[1], bpim_d[:, :], "stg1", [], ["stg1"])
        sincos(th, sn_, cs_, "s5", "s5", "s5")
        TT("vector", are, rr, cs_, ALU.mult, K5, K5)
        TT("vector", aim, rr, sn_, ALU.mult, K5, K5)
        TS("vector", are, are, -1.0, None, ALU.add, None, K5, K5)
        TT("vector", den, V("lamre"), V("lamre"), ALU.mult, ["vecs"], K5)
        TT("vector", t1_, V("lamim"), V("lamim"), ALU.mult, ["vecs"], K5)
        TT("vector", den, den, t1_, ALU.add, K5, K5)
        RECIP(den, den, K5, K5)
        TT("vector", t1_, are, V("lamre"), ALU.mult, K5 + ["vecs"], K5)
        TT("vector", t2_, aim, V("lamim"), ALU.mult, K5 + ["vecs"], K5)
        TT("vector", t1_, t1_, t2_, ALU.add, K5, K5)
        TT("vector", cr, t1_, den, ALU.mult, K5, K5)
        TT("vector", t1_, aim, V("lamre"), ALU.mult, K5 + ["vecs"], K5)
        TT("vector", t2_, are, V("lamim"), ALU.mult, K5 + ["vecs"], K5)
        TT("vector", t1_, t1_, t2_, ALU.subtract, K5, K5)
        TT("vector", ci, t1_, den, ALU.mult, K5, K5)
        TS("vector", nci, ci, -1.0, None, ALU.mult, None, K5, K5)
        for q in range(16):
            qs = slice(q * 128, (q + 1) * 128)
            t_a, ka = tmpf.next()
            TS("vector", t_a[:, 0:128], stg[0][:, qs], cr[:, q:q + 1], None, ALU.mult, None, ["stg0"] + K5, [ka])
            STT(t_a[:, 0:128], stg[1][:, qs], nci[:, q:q + 1], t_a[:, 0:128], ALU.mult, ALU.add,
                ["stg1", ka] + K5, [ka])
            TS("vector", t_a[:, 128:256], stg[1][:, qs], cr[:, q:q + 1], None, ALU.mult, None, ["stg1"] + K5, [ka])
            STT(t_a[:, 128:256], stg[0][:, qs], ci[:, q:q + 1], t_a[:, 128:256], ALU.mult, ALU.add,
                ["stg0", ka] + K5, [ka])
            ps, kp = psum("misc", [7])
            for comp in range(2):
                src = t_a[:, comp * 128:(comp + 1) * 128]
                dst = ps[:, comp * 128:(comp + 1) * 128]
                P.op("tensor", lambda e, dst=dst, src=src: e.transpose(dst, src, ident_f), [ka, "const"], [kp])
            CP("vector", bbT[0][:, q, :], ps[:, 0:128], [kp], ["bbT"])
            CP("vector", bbT[1][:, q, :], ps[:, 128:256], [kp], ["bbT"])
        DMA(stg[2], cpre_d[:, :], "stg2", [], ["stg2"])
        DMA(stg[3], cpim_d[:, :], "stg3", [], ["stg3"])
        CP("gpsimd", cT16[0].rearrange("p q m -> p (q m)"), stg[2], ["stg2"], ["cT16"])
        TS("gpsimd", cT16[1].rearrange("p q m -> p (q m)"), stg[3], -1.0, None, ALU.mult, None, ["stg3"], ["cT16"])
        DMA(stg[0], wglu_d[:, :], "stg0", [], ["stg0"])
        CP("gpsimd", wglu16, stg[0], ["stg0"], ["wglu16"])
        P.barrier()

        for c in range(8):
            o0 = _VEC["conv_w"][0] + c * 31
            identb = bass.AP(c16, 896, [[8 * 128, 128], [0, 31], [1, 128]])
            wbc = bass.AP(vecs, o0, [[NV, 128], [1, 31], [0, 128]])
            dg3 = dgt.rearrange("p (k m) -> p k m", k=31)
            TT("vector", dg3, identb, wbc, ALU.mult, ["c16", "vecs"], ["dgA", "dgB"])
            DMA(dgd[c, :, :], dgt, "dgout", ["dgA", "dgB"], ["dgd%d" % c])
        def proj(m):
            wb, kb = load_w(win0_d[m, :, :], 1024, scratch=win0b[m, :, :], skey="win0b%d" % m, reload=(curblk["n"] > 0))
            ps, kp = psum("proj", [0, 1])
            for c in range(8):
                MM(ps, wb[:, c * 128:(c + 1) * 128], hT[:, c, :], c == 0, c == 7, [kb, "hT%d" % c], [kp])
            return ps, kp

        def head_ops(n):
            blk = slice(n * TB, (n + 1) * TB)
            ops = []
            st = {}

            def f_rms():
                st["rs"] = rms_rstd(n, "l0")
            ops.append(f_rms)

            def mk_h(c):
                def f():
                    rs, krs = st["rs"]
                    tmp, kt = tmpf.next()
                    STT(tmp, xT[:, c, blk], gsv[0][:, c:c + 1], rs, ALU.mult, ALU.mult,
                        ["x%d_%d" % (c, n), krs, "small"], [kt])
                    ACT(hT[:, c, :], tmp, AF.Identity, [kt, "small"], ["hT%d" % c], bias=shiftv[0][:, c:c + 1])
                return f
            for c in range(8):
                ops.append(mk_h(c))

            def mk_u(cc):
                def f():
                    psu, kpu = proj(24 + cc)
                    ACT(u16[:, cc, :], psu, AF.Identity, [kpu], ["u16_%d" % cc])
                return f
            for cc in range(4):
                ops.append(mk_u(cc))
            return ops

        tail_prev = []
        curblk = {"n": 0}
        ada1_q = ada1_ring_tasks() if (do_l1 and not ADA1_UPFRONT) else []
        for n in range(NB):
            blk = slice(n * TB, (n + 1) * TB)
            if n == 1:
                while tail_prev:
                    tail_prev.pop(0)()
                P.barrier()
            curblk["n"] = n
            hops = head_ops(n)
            while hops or tail_prev:
                if hops:
                    hops.pop(0)()
                if tail_prev:
                    tail_prev.pop(0)()

            cst = {}

            def conv_c1a(c):
                psg, kg = proj(8 + c)
                sig, ksig = tmpf.next()
                ACT(sig, psg, AF.Tanh, [kg], [ksig], scale=0.5)
                psv, kv = proj(c)
                cst[c] = {"sig": sig, "ksig": ksig, "psv": psv, "kv": kv}

            def conv_c1b(c):
                d = cst[c]
                sig, ksig, psv, kv = d["sig"], d["ksig"], d["psv"], d["kv"]
                DMA(dgt[:, 0:2048], dgd[c, :, 0:2048], "dgA", ["dgd%d" % c], ["dgA"])
                DMA(dgt[:, 2048:3968], dgd[c, :, 2048:3968], "dgB", ["dgd%d" % c], ["dgB"])
                glu, kgl = glup.next()
                if n == 0:
                    MEMSET("vector", glu[:, 0:30], 0.0, [kgl])
                else:
                    CP("vector", glu[:, 0:30], halo[:, c, 0:30], ["halo%d" % c], [kgl])
                STT(glu[:, 30:542], sig, 1.0, psv, ALU.add, ALU.mult, [kv, ksig], [kgl])
                if n < NB - 1:
                    CP("vector", halo[:, c, 0:30], glu[:, 512:542], [kgl], ["halo%d" % c])
                d["glu"] = glu; d["kgl"] = kgl

            def conv_c2(c):
                d = cst.pop(c)
                glu, kgl = d["glu"], d["kgl"]
                cps, kcp = psum("conv", [2, 3])
                for k in range(31):
                    MM(cps, dgt[:, k * 128:(k + 1) * 128], glu[:, k:k + 512], k == 0, k == 30,
                       ["dgA" if k < 16 else "dgB", kgl], [kcp])
                ACT(convo[:, c, :], cps, AF.Identity, [kcp, "vecs"], ["convo%d" % c], bias=V("conv_b", c, c + 1))

            pst = {}
            ysta = {}

            def s5_p2a_act(q):
                sn, ksn = plong.next()
                cs, kcs = plong.next()
                DMA(cs, tabd[q, :, 0:512], "d" + kcs, ["tabd%d" % q], [kcs])
                DMA(sn, tabd[q, :, 512:1024], "d" + ksn, ["tabd%d" % q], [ksn])
                pst[q] = {"sn": sn, "ksn": ksn, "cs": cs, "kcs": kcs}

            def s5_p2a_dve(q):
                pass

            def s5_p2a_sin(q):
                pass

            def s5_p1(q):
                cc = q // 4
                bre, kbre = psum("s5b", [4, 5])
                bim, kbim = psum("s5b", [4, 5])
                MM(bre, bbT[0][:, q, :], u16[:, cc, :], True, True, ["bbT", "u16_%d" % cc], [kbre])
                MM(bim, bbT[1][:, q, :], u16[:, cc, :], True, True, ["bbT", "u16_%d" % cc], [kbim])
                pst[q].update({"bre": bre, "kbre": kbre, "bim": bim, "kbim": kbim})

            def s5_p2b(q):
                d = pst[q]
                bre, kbre, bim, kbim = d["bre"], d["kbre"], d["bim"], d["kbim"]
                sn, ksn, cs, kcs = d["sn"], d["ksn"], d["cs"], d["kcs"]
                btr, kbtr = plong.next()
                bti, kbti = plong.next()
                m2, km2 = pshort.next()
                TT("vector", btr, bre, cs, ALU.mult, [kbre, kcs], [kbtr])
                TT("vector", m2, bim, sn, ALU.mult, [kbim, ksn], [km2])
                TT("vector", btr, btr, m2, ALU.add, [kbtr, km2], [kbtr])
                TT("vector", bti, bim, cs, ALU.mult, [kbim, kcs], [kbti])
                TT("vector", m2, bre, sn, ALU.mult, [kbre, ksn, km2], [km2])
                TT("vector", bti, bti, m2, ALU.subtract, [kbti, km2], [kbti])
                rb = bass.AP(s5, 3 * 16 + q, [[256, 128], [0, 512]])
                st_re = small[:, 96 + q:97 + q]
                st_im = small[:, 112 + q:113 + q]
                kst = "st%d" % q
                if n > 0:
                    i_re = small[:, 192 + q:193 + q]
                    i_im = small[:, 208 + q:209 + q]
                    t_a = small[:, 224 + q:225 + q]
                    t_b = small[:, 240 + q:241 + q]
                    TS("vector", t_a, st_re, c512[:, q:q + 1], None, ALU.mult, None, [kst, "e512"], [kst])
                    STT(i_re, st_im, ns512[:, q:q + 1], t_a, ALU.mult, ALU.add, [kst, "e512"], [kst])
                    TS("vector", t_b, st_re, s512[:, q:q + 1], None, ALU.mult, None, [kst, "e512"], [kst])
                    STT(i_im, st_im, c512[:, q:q + 1], t_b, ALU.mult, ALU.add, [kst, "e512"], [kst])
                for (bt, kbt, stv) in ((btr, kbtr, (i_re if n > 0 else None)), (bti, kbti, (i_im if n > 0 else None))):
                    init = 0.0 if n == 0 else stv
                    P.op("vector", lambda e, bt=bt, init=init, rb=rb: e.tensor_tensor_scan(
                        out=bt, data0=rb, data1=bt, initial=init, op0=ALU.mult, op1=ALU.add),
                        [kbt, "s5", kst], [kbt])
                if n < NB - 1:
                    CP("vector", st_re, btr[:, 511:512], [kbtr], [kst])
                    CP("vector", st_im, bti[:, 511:512], [kbti], [kst])
                d.update({"btr": btr, "kbtr": kbtr, "bti": bti, "kbti": kbti})

            def s5_p3(q):
                d = pst[q]
                sn, ksn, cs, kcs = d["sn"], d["ksn"], d["cs"], d["kcs"]
                btr, kbtr, bti, kbti = d["btr"], d["kbtr"], d["bti"], d["kbti"]
                sre, ksre = tmph.next()
                sim, ksim = tmph.next()
                g1, kg1 = pg.next()
                g2, kg2 = pg.next()
                TT(P3ENG, g1, btr, cs, ALU.mult, [kbtr, kcs], [kg1])
                TT(P3ENG, g2, bti, sn, ALU.mult, [kbti, ksn], [kg2])
                TT(P3ENG, sre, g1, g2, ALU.subtract, [kg1, kg2], [ksre])
                TT(P3ENG, g1, btr, sn, ALU.mult, [kbtr, ksn, kg1], [kg1])
                TT(P3ENG, g2, bti, cs, ALU.mult, [kbti, kcs, kg2], [kg2])
                TT(P3ENG, sim, g1, g2, ALU.add, [kg1, kg2], [ksim])
                d.update({"sre": sre, "ksre": ksre, "sim": sim, "ksim": ksim})

            def s5_p4(q):
                cc = q // 4
                d = pst.pop(q)
                sre, ksre, sim, ksim = d["sre"], d["ksre"], d["sim"], d["ksim"]
                if q % 4 == 0:
                    ysta["y"] = psum("s5y", [6])
                y_ps, ky = ysta["y"]
                MM(y_ps, cT16[0][:, q, :], sre, q % 4 == 0, False, ["cT16", ksre], [ky])
                MM(y_ps, cT16[1][:, q, :], sim, False, q % 4 == 3, ["cT16", ksim], [ky])
                if q % 4 == 3:
                    yb, kyb = tmpf.next()
                    STT(yb, u16[:, cc, :], V("ssm_d", cc, cc + 1), y_ps, ALU.mult, ALU.add,
                        ["u16_%d" % cc, "vecs", ky], [kyb])
                    ACT(yg[:, cc, :], yb, AF.Gelu_apprx_tanh, [kyb], ["yg%d" % cc])
                    ACT(yg16[:, cc, :], yg[:, cc, :], AF.Copy, ["yg%d" % cc], ["yg16_%d" % cc])

            lnst = {}

            def ln_stats():
                s_ps, ks_ = psum("stat", [2, 3])
                q_ps, kq_ = psum("stat", [2, 3])
                for c in range(8):
                    MM(s_ps, ones16, convo[:, c, :], c == 0, c == 7, ["convo%d" % c, "c16"], [ks_])
                for c in range(8):
                    sq, ksq = pg.next()
                    ACT(sq, convo[:, c, :], AF.Square, ["convo%d" % c], [ksq])
                    MM(q_ps, ones_f, sq, c == 0, c == 7, [ksq, "const"], [kq_])
                mean, kmean = lnm[:, :], "lnm"
                TS("vector", mean, s_ps, 1.0 / 1024, None, ALU.mult, None, [ks_], [kmean])
                rl, krl = lnr[:, :], "lnr"
                TT("vector", rl, mean, mean, ALU.mult, [kmean], [krl])
                STT(rl, q_ps, 1.0 / 1024, rl, ALU.mult, ALU.subtract, [kq_, krl], [krl])
                TS("vector", rl, rl, EPS, None, ALU.add, None, [krl], [krl])
                ACT(rl, rl, AF.Sqrt, [krl], [krl])
                RECIP(rl, rl, [krl], [krl])

            nst = {}

            def norm_a(c):
                mean, kmean = lnm[:, :], "lnm"
                rl, krl = lnr[:, :], "lnr"
                t1, k1 = tmpf.next()
                TT("vector", t1, convo[:, c, :], mean, ALU.subtract, ["convo%d" % c, kmean], [k1])
                TT("vector", t1, t1, rl, ALU.mult, [k1, krl], [k1])
                ACT(t1, t1, AF.Silu, [k1, "vecs"], [k1], scale=V("ln_g", c, c + 1), bias=V("ln_b", c, c + 1))
                psa, kpa = proj(16 + c)
                sga, ksga = tmpf.next()
                ACT(sga, psa, AF.Silu, [kpa], [ksga])
                nst[c] = (t1, k1, sga, ksga)

            def norm_b(c):
                t1, k1, sga, ksga = nst.pop(c)
                TT("vector", y16[:, c, :], t1, sga, ALU.mult, [k1, ksga], ["y16_%d" % c])

            for k in range(21):
                if n >= 1 and ada1_q:
                    ada1_q.pop(0)()
                if 0 <= k - 2 < 16:
                    s5_p3(k - 2)
                if k < 16:
                    s5_p2a_act(k)
                if 0 <= k - 1 < 16:
                    s5_p2b(k - 1)
                if 0 <= k - 3 < 16:
                    s5_p4(k - 3)
                if k == 11:
                    ln_stats()
                if 0 <= k - 2 < 8:
                    conv_c2(k - 2)
                if k < 16:
                    s5_p1(k)
                    s5_p2a_dve(k)
                    s5_p2a_sin(k)
                if 0 <= k - 1 < 8:
                    conv_c1b(k - 1)
                if k < 8:
                    conv_c1a(k)
                if 8 <= k < 12:
                    mo_ = k - 8
                    psb_, kpb = proj(28 + mo_)
                    ACT(y16[:, 8 + mo_, :], psb_, AF.Silu, [kpb], ["y16_%d" % (8 + mo_)])
                if 0 <= k - 13 < 8:
                    norm_b(k - 13)
                if 0 <= k - 12 < 8:
                    norm_a(k - 12)
            def mk_glu(mo, n=n):
                def f():
                    ps, kp = psum("misc", [7])
                    for k in range(4):
                        MM(ps, wglu16[:, k * 512 + mo * 128:k * 512 + (mo + 1) * 128], yg16[:, k, :], k == 0, k == 3,
                           ["wglu16", "yg16_%d" % k], [kp])
                    sg, ksg = tmpf.next()
                    ACT(sg, ps, AF.Sigmoid, [kp, "vecs"], [ksg], bias=V("b_glu", mo, mo + 1))
                    TT("vector", sg, sg, yg[:, mo, :], ALU.mult, [ksg, "yg%d" % mo], [ksg])
                    ky_ = "y16_%d" % (8 + mo)
                    TT("vector", y16[:, 8 + mo, :], sg, y16[:, 8 + mo, :], ALU.mult, [ksg, ky_], [ky_])
                return f

            def mk_wout(mo, n=n, blk=blk):
                def f():
                    wb, kb = load_w(wout0_d[mo, :, 0:1024], 1024, scratch=wout0b[mo, :, 0:1024],
                                    skey="wout0bA%d" % mo, reload=(n > 0))
                    wb2, kb2 = load_w(wout0_d[mo, :, 1024:1536], 512, scratch=wout0b[mo, :, 1024:1536],
                                      skey="wout0bB%d" % mo, reload=(n > 0))
                    ps, kp = psum("misc", [7])
                    for k in range(8):
                        MM(ps, wb[:, k * 128:(k + 1) * 128], y16[:, k, :], k == 0, False, [kb, "y16_%d" % k], [kp])
                    for k in range(8, 12):
                        MM(ps, wb2[:, (k - 8) * 128:(k - 7) * 128], y16[:, k, :], False, k == 11,
                           [kb2, "y16_%d" % k], [kp])
                    xk = "x%d_%d" % (mo, n)
                    STT(xT[:, mo, blk], ps, gatev[0][:, mo:mo + 1], xT[:, mo, blk], ALU.mult, ALU.add,
                        [kp, "small", xk], [xk])
                return f
            tail_prev = [mk_glu(mo) for mo in range(4)] + [mk_wout(mo) for mo in range(8)]
        for f in tail_prev:
            f()
        while ada1_q:
            ada1_q.pop(0)()
        P.barrier()

    if do_l1:
        hT1 = arena16[:, 0:16384].rearrange("p (c t) -> p c t", c=8)
        def mkset(base16, o):
            return {
                "qT": base16[:, o:o + 2048],
                "kpad": [base16[:, o + 2048:o + 4096], base16[:, o + 4096:o + 6144]],
                "vpad": [base16[:, o + 6144:o + 8192].rearrange("p (t m) -> p t m", t=16),
                         base16[:, o + 8192:o + 10240].rearrange("p (t m) -> p t m", t=16)],
                "gsl": base16[:, o + 10240:o + 12288],
                "opair": base16[:, o + 12288:o + 14336],
            }
        sets = [mkset(arena16, 16384), mkset(parena16, 0)]
        ssum = [[arena16[:, 30720:31232], arena16[:, 31232:31744]],
                [arena16[:, 31744:32256], arena16[:, 32256:32768]]]
        MEMSET("gpsimd", arena16[:, 16384 + 2048:16384 + 10240], 0.0, ["kpad0_0", "kpad1_0", "vpad0_0", "vpad1_0"])
        MEMSET("gpsimd", parena16[:, 2048:10240], 0.0, ["kpad0_1", "kpad1_1", "vpad0_1", "vpad1_1"])
        for n in range(NB):
            blk = slice(n * TB, (n + 1) * TB)
            rs, krs = rms_rstd(n, "l1")
            for c in range(8):
                tmp, kt = tmpf.next()
                STT(tmp, xT[:, c, blk], gsv[1][:, c:c + 1], rs, ALU.mult, ALU.mult,
                    ["x%d_%d" % (c, n), krs, "small"], [kt])
                ACT(hT1[:, c, blk], tmp, AF.Identity, [kt, "small"], ["h1_%d_%d" % (c, n)],
                    bias=shiftv[1][:, c:c + 1])

        ZB = [0, 1, 2, 3]
        OB = [4, 5]

        def proj_tasks(hp):
            S_ = sets[hp % 2]
            sx = hp % 2
            tasks = []
            wref = {}

            def mk_load(sec):
                def t():
                    wref[sec] = load_w(win1_d[sec * 8 + hp, :, :], 1024, "vector")
                return t

            def mk_grp(sec, n):
                def t():
                    wb, kb = wref[sec]
                    ps, kp = psum("proj1", [6])
                    for c in range(8):
                        MM(ps, wb[:, c * 128:(c + 1) * 128], hT1[:, c, n * TB:(n + 1) * TB], c == 0, c == 7,
                           [kb, "h1_%d_%d" % (c, n)], [kp])
                    cols = slice(n * TB, (n + 1) * TB)
                    if sec == 0:
                        TS("vector", S_["qT"][:, cols], ps, 0.125, None, ALU.mult, None, [kp], ["qT_%d" % sx])
                    elif sec == 1:
                        CP("vector", S_["kpad"][0][0:64, cols], ps[0:64, :], [kp], ["kpad0_%d" % sx])
                        CP("vector", S_["kpad"][1][64:128, cols], ps[64:128, :], [kp], ["kpad1_%d" % sx])
                    else:
                        ACT(S_["gsl"][:, cols], ps, AF.Silu, [kp], ["gsl_%d" % sx])
                return t

            def mk_v(tt):
                def t():
                    wb, kb = wref[2]
                    ps, kp = psum("proj1", [6])
                    n = tt // 4
                    for c in range(8):
                        MM(ps[:, 0:128], hT1[:, c, tt * 128:(tt + 1) * 128], wb[:, c * 128:(c + 1) * 128],
                           c == 0, c == 7, [kb, "h1_%d_%d" % (c, n)], [kp])
                    CP("vector", S_["vpad"][0][:, tt, 0:64], ps[:, 0:64], [kp], ["vpad0_%d" % sx])
                    CP("vector", S_["vpad"][1][:, tt, 64:128], ps[:, 64:128], [kp], ["vpad1_%d" % sx])
                return t

            for sec in (0, 1, 3):
                tasks.append(mk_load(sec))
                for n in range(NB):
                    tasks.append(mk_grp(sec, n))
            tasks.append(mk_load(2))
            for tt in range(16):
                tasks.append(mk_v(tt))
            return tasks

        def wout_tasks(hp):
            S_ = sets[hp % 2]
            sx = hp % 2
            tasks = []
            wref = {}

            def ld():
                wref[0] = load_w(wout1_d[hp, :, :], 1024, "vector")
            tasks.append(ld)

            def mk(mo, n):
                def t():
                    wb, kb = wref[0]
                    blk = slice(n * TB, (n + 1) * TB)
                    ps, kp = psum("proj1", [6])
                    MM(ps, wb[:, mo * 128:(mo + 1) * 128], S_["opair"][:, blk], True, True, [kb, "opair_%d" % sx], [kp])
                    xk = "x%d_%d" % (mo, n)
                    STT(xT[:, mo, blk], ps, gatev[1][:, mo:mo + 1], xT[:, mo, blk], ALU.mult, ALU.add,
                        [kp, "small", xk], [xk])
                return t
            for mo in range(8):
                for n in range(NB):
                    tasks.append(mk(mo, n))
            return tasks

        items = []
        for hp in range(8):
            for g4 in range(4):
                nblk = 4 * g4 + 4
                for b in range(nblk - 1, -1, -1):
                    for hi in range(2):
                        items.append((hp, hi, g4, b, nblk))
        info = {}
        hstate = {0: [0, 0], 1: [0, 0]}
        ostate = {}

        def stA1(it):
            hp, hi, g4, b, nblk = it
            S_ = sets[hp % 2]; sx = hp % 2
            T0 = g4 * TB
            c0 = max(0, b * 128 - T0)
            Z, kz = psum("Z", ZB)
            MM(Z[:, c0:TB], S_["kpad"][hi][:, b * 128:(b + 1) * 128], S_["qT"][:, T0 + c0:T0 + TB], True, True,
               ["kpad%d_%d" % (hi, sx), "qT_%d" % sx], [kz])
            info[it] = {"Z": Z, "kz": kz, "c0": c0}

        def stA2(it):
            hp, hi, g4, b, nblk = it
            d = info[it]
            Z, kz, c0 = d["Z"], d["kz"], d["c0"]
            first = (b == nblk - 1)
            e_, ke = psb[7][:, :], "ps7"
            ACT(e_[:, c0:TB], Z[:, c0:TB], AF.Exp, [kz], [ke])
            sp, ksp = tmph.next()
            ACT(sp[:, c0:TB], e_[:, c0:TB], AF.Ln, [ke], [ksp], bias=1.0)
            if b >= 4 * g4:
                TT("vector", sp[:, c0:c0 + 128], sp[:, c0:c0 + 128], tri01_16, ALU.mult, [ksp, "c16"], [ksp])
            if first:
                MEMSET("gpsimd", ssum[hi][0], 0.0, ["ssum%d_0" % hi])
                MEMSET("gpsimd", ssum[hi][1], 0.0, ["ssum%d_1" % hi])
                hstate[hi][0] = 0
            cur = hstate[hi][0]
            d["sp"] = sp; d["ksp"] = ksp; d["cur"] = cur
            if b > 0:
                nxt = 1 - cur
                TT("vector", ssum[hi][nxt][:, c0:TB], ssum[hi][cur][:, c0:TB], sp[:, c0:TB], ALU.add,
                   ["ssum%d_%d" % (hi, cur), ksp], ["ssum%d_%d" % (hi, nxt)])
                hstate[hi][0] = nxt

        def stB1(it):
            hp, hi, g4, b, nblk = it
            d = info[it]
            Z, kz, c0, sp, ksp, cur = d["Z"], d["kz"], d["c0"], d["sp"], d["ksp"], d["cur"]
            first = (b == nblk - 1)
            diag = (b >= 4 * g4)
            kc_ = "ssum%d_%d" % (hi, cur)
            MM(Z[:, c0:TB], nti16, sp[:, c0:TB], False, (first and not diag), [ksp, "c16"], [kz], skip=True)
            if not first:
                MM(Z[:, c0:TB], negones16, ssum[hi][cur][:, c0:TB], False, not diag, [kc_, "c16"], [kz], skip=True)
            if diag:
                MM(Z[:, c0:c0 + 128], ident16, negbig16, False, True, ["c16"], [kz], skip=True)

        def stB2(it):
            d = info[it]
            Z, kz, c0 = d["Z"], d["kz"], d["c0"]
            w16, kw = tmph.next()
            ACT(w16[:, c0:TB], Z[:, c0:TB], AF.Exp, [kz], [kw])
            d["w16"] = w16; d["kw"] = kw

        def stB3(it):
            hp, hi, g4, b, nblk = it
            S_ = sets[hp % 2]; sx = hp % 2
            T0 = g4 * TB
            d = info.pop(it)
            c0, w16, kw = d["c0"], d["w16"], d["kw"]
            if b == nblk - 1:
                O, ko = psum("O%d" % hi, [OB[hi]])
                ostate[hi] = (O, ko)
                MM(O, zeros16, S_["qT"][:, 0:TB], True, False, ["c16", "qT_%d" % sx], [ko])
            O, ko = ostate[hi]
            MM(O[:, c0:TB], S_["vpad"][hi][:, b, :], w16[:, c0:TB], False, b == 0, [kw, "vpad%d_%d" % (hi, sx)], [ko])
            if b == 0:
                rows = slice(hi * 64, (hi + 1) * 64)
                TT("vector", S_["opair"][rows, T0:T0 + TB], O[rows, :], S_["gsl"][rows, T0:T0 + TB], ALU.mult,
                   [ko, "gsl_%d" % sx], ["opair_%d" % sx])

        for t in proj_tasks(0):
            t()
        NI = len(items)
        PER = NI // 8
        queue = []
        for s_ in range(NI + 4):
            p_ = s_ // PER
            r_ = s_ % PER
            if s_ < NI and r_ == 5 and p_ + 1 < 8:
                queue.extend(proj_tasks(p_ + 1))
            if s_ < NI and r_ == 6 and p_ >= 1:
                queue = wout_tasks(p_ - 1) + queue
            if 0 <= s_ - 2 < NI:
                stB1(items[s_ - 2])
            if 0 <= s_ - 4 < NI:
                stB3(items[s_ - 4])
            if s_ < NI:
                if r_ == 0:
                    while queue:
                        queue.pop(0)()
                stA1(items[s_])
            if 0 <= s_ - 1 < NI:
                stA2(items[s_ - 1])
            if 0 <= s_ - 3 < NI:
                stB2(items[s_ - 3])
            if queue:
                left = PER - r_ - 2
                ntask = len(queue) if left <= 0 else -(-len(queue) // left)
                for _ in range(min(ntask, len(queue))):
                    queue.pop(0)()
        while queue:
            queue.pop(0)()
        for t in wout_tasks(7):
            t()
        P.barrier()

    diagG = arena[:, 0:1024].rearrange("p (c m) -> p c m", c=8)
    for c in range(8):
        TS("vector", diagG[:, c, :], ident_f, V("final_g", c, c + 1), None, ALU.mult, None,
           ["const", "vecs"], ["diagG"])
    obuf = [arena[:, 1024:2048], arena[:, 2048:3072]]
    frs = None
    for tt in range(16):
        n = tt // 4
        tsl = slice(tt * 128, (tt + 1) * 128)
        if tt % 4 == 0:
            frs = rms_rstd(n, "fin")
        rs, krs = frs
        ps2, k2 = psum("fin2", [4, 5])
        MM(ps2[:, 0:1], rs[:, (tt % 4) * 128:(tt % 4 + 1) * 128], ident_f[:, 0:1], True, True, [krs, "const"], [k2])
        rt, krt = tmpf.next()
        CP("vector", rt[:, 0:1], ps2[:, 0:1], [k2], [krt])
        ob = obuf[tt % 2]
        kob = "obuf%d" % (tt % 2)
        for half in range(2):
            ps, kp = psum("fin", [0, 1])
            for j in range(4):
                c = half * 4 + j
                MM(ps[:, j * 128:(j + 1) * 128], xT[:, c, tsl], diagG[:, c, :], True, True,
                   ["x%d_%d" % (c, n), "diagG"], [kp])
            TS("vector", ob[:, half * 512:(half + 1) * 512], ps, rt[:, 0:1], None, ALU.mult, None,
               [kp, krt], [kob])
        DMA(out_d[tsl, :], ob, kob + "d", [kob], [kob])
    P.final_wait(["obuf0d", "obuf1d"])
    P.emit()
    return nc


def _host_inputs(inp, b):
    f = np.float32

    def colvec(v):
        v = np.asarray(v, f)
        return v.reshape(-1, 128).T

    vecs = np.zeros((128, NV), f)

    def put(name, arr):
        o, w = _VEC[name]
        assert arr.shape == (128, w), (name, arr.shape)
        vecs[:, o:o + w] = arr
    put("norm_g0", colvec(inp["l0_norm_g"]))
    put("b_ada0", colvec(inp["l0_b_ada"]))
    put("conv_b", colvec(inp["l0_conv_b"]))
    put("ln_g", colvec(inp["l0_conv_ln_g"]))
    put("ln_b", colvec(inp["l0_conv_ln_b"]))
    put("ssm_d", colvec(inp["l0_ssm_d"]))
    put("b_glu", colvec(inp["l0_ssm_b_glu"]))
    put("norm_g1", colvec(inp["l1_norm_g"]))
    put("b_ada1", colvec(inp["l1_b_ada"]))
    put("final_g", colvec(inp["final_norm_g"]))
    put("cT", colvec(inp["c"][b]))
    cw = np.asarray(inp["l0_conv_w"], f)
    put("conv_w", cw.T.reshape(8, 128, 31).transpose(1, 0, 2).reshape(128, 248))

    def pairlay(a):
        return np.asarray(a, f).reshape(16, 2, 64).transpose(1, 2, 0).reshape(128, 16)
    put("lamre", pairlay(inp["l0_ssm_lam_re"]))
    put("lamim", pairlay(inp["l0_ssm_lam_im"]))
    ld = np.asarray(inp["l0_ssm_log_dt"], f)
    put("logdt", pairlay(np.repeat(ld[:, None], 64, axis=1)))

    def kchunks(w, ncol):
        w = np.asarray(w, f)
        K, N = w.shape
        return np.ascontiguousarray(
            w.reshape(K // 128, 128, N // ncol, ncol).transpose(2, 1, 0, 3).reshape(N // ncol, 128, (K // 128) * ncol))

    def bpad(bmat):
        bmat = np.asarray(bmat, f)
        o = np.zeros((2, 64, 16, 128), f)
        for g in range(32):
            q, h = g // 2, g % 2
            o[h, :, q, (g % 8) * 16:(g % 8) * 16 + 16] = bmat[g]
        return o.reshape(128, 2048)

    def cpad(cmat):
        return bpad(np.asarray(cmat, f).transpose(0, 2, 1))

    d = {}
    xb = np.asarray(inp["x"][b], f)
    d["xT"] = np.ascontiguousarray(xb.reshape(L, 8, 128).transpose(2, 1, 0))
    d["vecs"] = vecs
    d["w_ada0"] = np.ascontiguousarray(np.asarray(inp["l0_w_ada"], f).reshape(8, 128, 3072))
    d["w_ada1"] = np.ascontiguousarray(np.asarray(inp["l1_w_ada"], f).reshape(8, 128, 3072))
    d["w_in0"] = kchunks(inp["l0_w_in"], 128)
    d["w_out0"] = kchunks(inp["l0_w_out"], 128)
    d["w_in1"] = kchunks(inp["l1_w_in"], 128)
    w1 = np.asarray(inp["l1_w_out"], f)
    d["w_out1"] = np.ascontiguousarray(w1.reshape(8, 128, 1024))
    wg = np.asarray(inp["l0_ssm_w_glu"], f)
    d["w_glu"] = np.ascontiguousarray(wg.reshape(4, 128, 512).transpose(1, 0, 2).reshape(128, 2048))
    d["bp_re"] = bpad(inp["l0_ssm_b_re"])
    d["bp_im"] = bpad(inp["l0_ssm_b_im"])
    d["cp_re"] = cpad(inp["l0_ssm_c_re"])
    d["cp_im"] = cpad(inp["l0_ssm_c_im"])
    cst = np.zeros((128, 5 * 128 + 512), f)
    i = np.arange(128)
    cst[:, 0:128] = np.eye(128, dtype=f)
    cst[:, 128:256] = 1.0
    cst[:, 256:384] = (i[None, :] > i[:, None]).astype(f)
    cst[:, 384:512] = -(i[:, None] >= i[None, :]).astype(f)
    cst[:, 512:640] = np.where(i[None, :] <= i[:, None], -30000.0, 0.0).astype(f)
    cst[:, 640:1152] = np.arange(512, dtype=f)[None, :]
    d["consts"] = cst
    return d


_NC_CACHE = {}


def kernel(**inputs):
    inp = {k: np.asarray(v) for k, v in inputs.items()}
    if "nc" not in _NC_CACHE:
        _NC_CACHE["nc"] = build_program()
    nc = _NC_CACHE["nc"]
    in_maps = [_host_inputs(inp, b) for b in range(8)]
    res = run_bass_kernel_spmd(nc, in_maps, core_ids=list(range(8)))
    out = np.stack([np.asarray(r["out"], np.float32).reshape(L, D) for r in res.results], axis=0)
    return out
```

```python
import math
import numpy as np
import concourse.bass as bass
import concourse.mybir as mybir
from concourse.bass_utils import run_bass_kernel_spmd

F32 = mybir.dt.float32
BF16 = mybir.dt.bfloat16
AF = mybir.ActivationFunctionType
ALU = mybir.AluOpType

L = 2048
D = 1024
NB = 4
TB = 512
EPS = 1e-6
PI = math.pi
SAME_ENG_SYNC = True
N_FILL = 0
ADA1_UPFRONT = False
P3ENG = "vector"

_VEC = {}
_off = 0
for _n, _w in [("norm_g0", 8), ("b_ada0", 24), ("conv_b", 8), ("ln_g", 8), ("ln_b", 8), ("ssm_d", 4),
               ("b_glu", 4), ("norm_g1", 8), ("b_ada1", 24), ("final_g", 8), ("cT", 8), ("conv_w", 248),
               ("lamre", 16), ("lamim", 16), ("logdt", 16)]:
    _VEC[_n] = (_off, _w)
    _off += _w
NV = _off


class _Op:
    __slots__ = ("eng", "fn", "deps", "signal", "val", "tag", "is_dma")


class Prog:
    ENG = ("sync", "tensor", "vector", "scalar", "gpsimd")

    def __init__(self, nc):
        self.nc = nc
        self.ops = {e: [] for e in self.ENG}
        self.lastw = {}
        self.readers = {}
        self.pend = {e: [] for e in self.ENG}
        self.dma_count = {}
        self.last_dma = {}

    def _add(self, op, r, w):
        deps = set()
        for k in r:
            o = self.lastw.get(k)
            if o is not None:
                deps.add(o)
        for k in w:
            o = self.lastw.get(k)
            if o is not None:
                deps.add(o)
            for o in self.readers.get(k, ()):
                deps.add(o)
        for o in self.pend[op.eng]:
            deps.add(o)
        self.pend[op.eng] = []
        deps.discard(op)
        op.deps = deps
        for d in deps:
            d.signal = True
        for k in r:
            self.readers.setdefault(k, []).append(op)
        for k in w:
            self.lastw[k] = op
            self.readers[k] = []
        self.ops[op.eng].append(op)

    def op(self, eng, fn, r=(), w=()):
        o = _Op()
        o.eng = eng; o.fn = fn; o.signal = False; o.val = 0; o.tag = None; o.is_dma = False
        self._add(o, r, w)
        return o

    def dma(self, fn, tag, r=(), w=(), q="sync"):
        o = _Op()
        o.eng = q; o.fn = fn; o.signal = True; o.tag = tag; o.is_dma = True
        self.dma_count[tag] = self.dma_count.get(tag, 0) + 1
        o.val = 16 * self.dma_count[tag]
        self.last_dma[tag] = o
        self._add(o, r, w)
        return o

    def barrier(self):
        evs = []
        for e in self.ENG:
            if self.ops[e]:
                o = self.ops[e][-1]
                o.signal = True
                evs.append(o)
        for t, o in self.last_dma.items():
            evs.append(o)
        for e in self.ENG:
            self.pend[e] = list(evs)

    def final_wait(self, tags):
        o = _Op()
        o.eng = "sync"; o.fn = (lambda e: None); o.signal = False; o.val = 0; o.tag = None; o.is_dma = False
        o.deps = set(self.last_dma[t] for t in tags)
        self.ops["sync"].append(o)

    def emit(self):
        nc = self.nc
        for e in self.ENG:
            cnt = 0
            for o in self.ops[e]:
                if o.is_dma:
                    continue
                if o.signal:
                    cnt += 1
                    o.val = cnt
        sems = {e: nc.alloc_semaphore("sem_" + e) for e in self.ENG}
        dsems = {t: nc.alloc_semaphore("dsem_" + t) for t in self.dma_count}
        ops = self.ops
        with nc.Block() as block:
            for e in self.ENG:
                def body(eng, e=e):
                    waited = {}
                    for o in ops[e]:
                        need = {}
                        for d in o.deps:
                            if d.is_dma:
                                nm = "d_" + d.tag; sem = dsems[d.tag]
                            else:
                                if d.eng == e and (e == "tensor" or e == "sync" or not SAME_ENG_SYNC):
                                    continue
                                nm = "e_" + d.eng; sem = sems[d.eng]
                            if d.val > need.get(nm, (None, 0))[1]:
                                need[nm] = (sem, d.val)
                        for nm, (sem, val) in need.items():
                            if waited.get(nm, 0) >= val:
                                continue
                            eng.wait_ge(sem, val)
                            waited[nm] = val
                        ins = o.fn(eng)
                        if ins is None:
                            continue
                        if o.is_dma:
                            ins.then_inc(dsems[o.tag], 16)
                        elif o.signal:
                            ins.then_inc(sems[e], 1)
                getattr(block, e)(body)


class Pool:
    def __init__(self, nc, name, n, shape, dtype, tiles=None):
        if tiles is None:
            self.t = [nc.alloc_sbuf_tensor("%s%d" % (name, i), list(shape), dtype)[:, :] for i in range(n)]
        else:
            self.t = list(tiles)
        self.k = ["%s%d" % (name, i) for i in range(len(self.t))]
        self.i = 0

    def next(self):
        j = self.i % len(self.t)
        self.i += 1
        return self.t[j], self.k[j]


def build_program(do_l0=True, do_l1=True):
    nc = bass.Bass("TRN2", target_bir_lowering=False)
    P = Prog(nc)

    def din(name, shape):
        return nc.dram_tensor(name, list(shape), F32, kind="ExternalInput")

    xT_d = din("xT", [128, 8, L])
    vecs_d = din("vecs", [128, NV])
    wada_d = [din("w_ada0", [8, 128, 3072]), din("w_ada1", [8, 128, 3072])]
    win0_d = din("w_in0", [32, 128, 1024])
    wout0_d = din("w_out0", [8, 128, 1536])
    win1_d = din("w_in1", [32, 128, 1024])
    wout1_d = din("w_out1", [8, 128, 1024])
    wglu_d = din("w_glu", [128, 2048])
    bpre_d = din("bp_re", [128, 2048])
    bpim_d = din("bp_im", [128, 2048])
    cpre_d = din("cp_re", [128, 2048])
    cpim_d = din("cp_im", [128, 2048])
    consts_d = din("consts", [128, 5 * 128 + 512])
    out_d = nc.dram_tensor("out", [L, D], F32, kind="ExternalOutput")
    dgd = nc.dram_tensor("dgd", [8, 128, 31 * 128], BF16)

    sb = nc.alloc_sbuf_tensor
    xT = sb("xT_sb", [128, 8, L], F32)
    vecs = sb("vecs_sb", [128, NV], F32)
    consts = sb("consts_sb", [128, 2 * 128 + 512], F32)
    ident_f = consts[:, 0:128]
    ones_f = consts[:, 128:256]
    iota_f = consts[:, 256:768]
    c16 = sb("c16", [128, 8 * 128], BF16)
    ident16 = c16[:, 0:128]
    negones16 = c16[:, 128:256]
    tri01_16 = c16[:, 256:384]
    nti16 = c16[:, 384:512]
    negbig16 = c16[:, 512:640]
    zeros16 = c16[:, 640:768]
    ones16 = c16[:, 768:896]
    halfid16 = c16[:, 896:1024]
    small = sb("small", [128, 256], F32)
    rsd = sb("rsd", [128, 512], F32)
    lnm = sb("lnm", [128, 512], F32)
    lnr = sb("lnr", [128, 512], F32)

    def V(name, j0=0, j1=None):
        o, w = _VEC[name]
        if j1 is None:
            j1 = w
        return vecs[:, o + j0:o + j1]

    arena = sb("arena", [128, 16640], F32)
    arena16 = arena.bitcast(BF16)

    tmpf = Pool(nc, "tf", 4, [128, 512], F32)
    parena = sb("parena", [128, 9728], F32)
    parena16 = parena.bitcast(BF16)
    plong = Pool(nc, "pl", 10, None, None, tiles=[parena[:, i * 512:(i + 1) * 512] for i in range(10)])
    pshort = Pool(nc, "ps_", 2, None, None, tiles=[parena[:, 5120 + i * 512:5120 + (i + 1) * 512] for i in range(2)])
    pg = Pool(nc, "pg", 2, None, None, tiles=[parena[:, 6144 + i * 512:6144 + (i + 1) * 512] for i in range(2)])
    tmph = Pool(nc, "th", 6, [128, 512], BF16)
    wstT = sb("wstT", [128, 2048], F32)
    wst = Pool(nc, "wst", 2, None, None, tiles=[wstT[:, 0:1024], wstT[:, 1024:2048]])
    wbfT = [sb("wbfT%d" % i, [128, 1024], BF16) for i in range(2)]
    wbf = Pool(nc, "wbf", 2, None, None, tiles=[t_[:, :] for t_ in wbfT])
    wstT16 = wstT.bitcast(BF16)
    wring_f32 = [t_.bitcast(F32)[:, 0:512] for t_ in wbfT] + [wstT[:, i * 512:(i + 1) * 512] for i in range(4)]
    wring = Pool(nc, "wring", 6, None, None, tiles=wbf.t + [wstT16[:, i * 1024:(i + 1) * 1024] for i in range(4)])
    wring.k = list(wbf.k) + ["wrx0", "wrx1", "wrx2", "wrx3"]
    win0b = nc.dram_tensor("win0b", [32, 128, 1024], BF16)
    wout0b = nc.dram_tensor("wout0b", [8, 128, 1536], BF16)
    dgt = parena16[:, 14336:14336 + 3968]
    glup = Pool(nc, "glu", 2, None, None, tiles=[parena16[:, 18304 + i * 544:18304 + (i + 1) * 544] for i in range(2)])

    psb = [nc.alloc_psum_tensor("psb%d" % i, [128, 512], F32) for i in range(8)]
    ps_cnt = {}

    def psum(role, banks):
        i = ps_cnt.get(role, 0)
        ps_cnt[role] = i + 1
        b = banks[i % len(banks)]
        return psb[b][:, :], "ps%d" % b

    def ACT(out, in_, func, r, w, **kw):
        P.op("scalar", lambda e: e.activation(out=out, in_=in_, func=func, **kw), r, w)

    def MM(out, lhsT, rhs, start, stop, r, w, skip=False):
        if skip:
            P.op("tensor", lambda e: e.matmul(out, lhsT, rhs, start=start, stop=stop, skip_group_check=True), r, w)
        else:
            P.op("tensor", lambda e: e.matmul(out, lhsT, rhs, start=start, stop=stop), r, w)

    def TT(eng, out, in0, in1, op, r, w):
        P.op(eng, lambda e: e.tensor_tensor(out=out, in0=in0, in1=in1, op=op), r, w)

    def TS(eng, out, in0, s1, s2, op0, op1, r, w):
        if op1 is None:
            P.op(eng, lambda e: e.tensor_scalar(out=out, in0=in0, scalar1=s1, scalar2=None, op0=op0), r, w)
        else:
            P.op(eng, lambda e: e.tensor_scalar(out=out, in0=in0, scalar1=s1, scalar2=s2, op0=op0, op1=op1), r, w)

    def STT(out, in0, scalar, in1, op0, op1, r, w):
        P.op("vector", lambda e: e.scalar_tensor_tensor(out=out, in0=in0, scalar=scalar, in1=in1,
                                                        op0=op0, op1=op1), r, w)

    def CP(eng, out, in_, r, w):
        P.op(eng, lambda e: e.tensor_copy(out=out, in_=in_), r, w)

    def MEMSET(eng, ap, val, w):
        P.op(eng, lambda e: e.memset(ap, val), (), w)

    def DMA(out, in_, tag, r, w, q="sync"):
        P.dma(lambda e: e.dma_start(out=out, in_=in_), tag, r, w, q=q)

    def RECIP(out, in_, r, w):
        P.op("vector", lambda e: e.reciprocal(out=out, in_=in_), r, w)

    MAGIC = 12582912.0
    PI_LO = 3.1415925
    CW1 = 6.28125
    CW2 = 2.0 * PI - 6.28125

    def sincos(x, sn, cs, kx, ksn, kcs):
        ACT(cs, x, AF.Identity, [kx], [kcs], scale=1.0 / (2.0 * PI), bias=MAGIC)
        ACT(cs, cs, AF.Identity, [kcs], [kcs], bias=-MAGIC)
        STT(sn, cs, -CW1, x, ALU.mult, ALU.add, [kcs, kx], [ksn])
        STT(sn, cs, -CW2, sn, ALU.mult, ALU.add, [kcs, ksn], [ksn])
        TS("vector", sn, sn, -PI_LO, PI_LO, ALU.max, ALU.min, [ksn], [ksn])
        STT(cs, sn, -1.0, sn, ALU.mult, ALU.max, [ksn], [kcs])
        ACT(cs, cs, AF.Sin, [kcs, "small"], [kcs], scale=-1.0, bias=halfpi)
        ACT(sn, sn, AF.Sin, [ksn], [ksn])

    DMA(vecs[:, :], vecs_d[:, :], "vecs", [], ["vecs"])
    DMA(consts[:, 0:256], consts_d[:, 0:256], "consts", [], ["const"])
    DMA(consts[:, 256:768], consts_d[:, 640:1152], "consts", [], ["const"])
    cstage = arena[:, 0:384]
    DMA(cstage, consts_d[:, 256:640], "cstage", [], ["cstage"])
    tri01_f = cstage[:, 0:128]
    nti_f = cstage[:, 128:256]
    negbig_f = cstage[:, 256:384]
    for c in range(8):
        DMA(xT[:, c, :], xT_d[:, c, :], "xload%d" % c, [], ["x%d_%d" % (c, n) for n in range(NB)])
    CP("gpsimd", ident16, ident_f, ["const"], ["c16"])
    TS("gpsimd", negones16, ones_f, -1.0, None, ALU.mult, None, ["const"], ["c16"])
    CP("gpsimd", tri01_16, tri01_f, ["cstage"], ["c16"])
    CP("gpsimd", nti16, nti_f, ["cstage"], ["c16"])
    CP("gpsimd", negbig16, negbig_f, ["cstage"], ["c16"])
    MEMSET("gpsimd", zeros16, 0.0, ["c16"])
    CP("gpsimd", ones16, ones_f, ["const"], ["c16"])
    TS("gpsimd", halfid16, ident_f, 0.5, None, ALU.mult, None, ["const"], ["c16"])
    MEMSET("gpsimd", small[:, :], 0.0, ["small", "e512"])
    sc = small[:, 0:8]
    mods = [small[:, 8:32], small[:, 32:56]]
    gsv = [small[:, 56:64], small[:, 64:72]]
    negpi = small[:, 72:73]
    onec = small[:, 73:74]
    MEMSET("gpsimd", negpi, -PI, ["small"])
    MEMSET("gpsimd", onec, 1.0, ["small"])
    halfpi = small[:, 74:75]
    MEMSET("gpsimd", halfpi, PI / 2.0, ["small"])
    ACT(sc, V("cT"), AF.Silu, ["vecs", "small"], ["small"])

    s5 = sb("s5", [128, 16 * 16], F32)

    def S(i):
        return s5[:, i * 16:(i + 1) * 16]
    dt_, xr, th, rr, asn, acs, sn_, cs_, are, aim, den, cr, ci, t1_, t2_, nci = [S(i) for i in range(16)]
    K5 = ["s5"]
    c512 = small[:, 144:160]
    s512 = small[:, 160:176]
    ns512 = small[:, 176:192]
    tabd = nc.dram_tensor("tabd", [16, 128, 1024], F32)
    if do_l0:
        ACT(dt_, V("logdt"), AF.Exp, ["vecs"], K5)
        TT("vector", xr, V("lamre"), dt_, ALU.mult, K5 + ["vecs"], K5)
        TT("vector", th, V("lamim"), dt_, ALU.mult, K5 + ["vecs"], K5)
        ACT(rr, xr, AF.Exp, K5, K5)
        TS("vector", asn, th, 512.0, None, ALU.mult, None, K5, K5)
        sincos(asn, s512, c512, "s5", "e512", "e512")
        TS("vector", ns512, s512, -1.0, None, ALU.mult, None, ["e512"], ["e512"])

    def gen_table(q):
        if True:
            base, kb_ = pshort.next()
            sn, ksn = plong.next()
            cs, kcs = plong.next()
            ACT(base, iota_f, AF.Identity, ["const", "s5"], [kb_], scale=th[:, q:q + 1])
            sincos(base, sn, cs, kb_, ksn, kcs)
            DMA(tabd[q, :, 0:512], cs, "to" + kcs, [kcs], ["tabd%d" % q], q="gpsimd")
            DMA(tabd[q, :, 512:1024], sn, "to" + ksn, [ksn], ["tabd%d" % q], q="gpsimd")

    sc16 = sb("sc16", [128, 8], BF16)[:, :]
    CP("vector", sc16, sc, ["small"], ["sc16"])
    for l in range(2 if ADA1_UPFRONT else 1):
        for kc in range(8):
            if do_l0:
                gen_table(l * 8 + kc)
            ps, kp = psum("ada", [2])
            for g in range(6):
                st, ks = tmpf.next()
                DMA(st, wada_d[l][kc, :, g * 512:(g + 1) * 512], "d" + ks, [], [ks])
                w16_, k16 = tmph.next()
                CP("vector", w16_, st, [ks], [k16])
                for j in range(4):
                    MM(ps[:, g * 4 + j:g * 4 + j + 1], w16_[:, j * 128:(j + 1) * 128], sc16[:, kc:kc + 1], True, True,
                       [k16, "sc16"], [kp])
            if kc == 0:
                CP("vector", mods[l], ps[:, 0:24], [kp], ["small"])
            else:
                TT("vector", mods[l], mods[l], ps[:, 0:24], ALU.add, [kp, "small"], ["small"])
        bname = "b_ada%d" % l
        TT("vector", mods[l], mods[l], V(bname), ALU.add, ["small", "vecs"], ["small"])
        TS("vector", gsv[l], mods[l][:, 8:16], 1.0, None, ALU.add, None, ["small"], ["small"])
        TT("vector", gsv[l], gsv[l], V("norm_g%d" % l), ALU.mult, ["small", "vecs"], ["small"])
    if do_l0 and not ADA1_UPFRONT:
        for q_ in range(8, 16):
            gen_table(q_)

    def ada1_ring_tasks():
        tasks = []
        psst = {}

        def mk(kc, g):
            def f():
                i_ = wring.i % 6
                _, ks = wring.next()
                st = wring_f32[i_]
                DMA(st, wada_d[1][kc, :, g * 512:(g + 1) * 512], "d" + ks, [], [ks])
                j_ = wring.i % 6
                w16t, k16 = wring.next()
                w16_ = w16t[:, 0:512]
                CP("vector", w16_, st, [ks], [k16])
                ps, kp = psum("misc", [7])
                for j in range(4):
                    MM(ps[:, j:j + 1], w16_[:, j * 128:(j + 1) * 128], sc16[:, kc:kc + 1], True, True,
                       [k16, "sc16"], [kp])
                dst = mods[1][:, g * 4:(g + 1) * 4]
                if kc == 0:
                    CP("vector", dst, ps[:, 0:4], [kp], ["mod1"])
                else:
                    TT("vector", dst, dst, ps[:, 0:4], ALU.add, [kp, "mod1"], ["mod1"])
            return f
        for kc in range(8):
            for g in range(6):
                tasks.append(mk(kc, g))

        def fin():
            TT("vector", mods[1], mods[1], V("b_ada1"), ALU.add, ["mod1", "vecs"], ["mod1"])
            TS("vector", gsv[1], mods[1][:, 8:16], 1.0, None, ALU.add, None, ["mod1"], ["mod1"])
            TT("vector", gsv[1], gsv[1], V("norm_g1"), ALU.mult, ["mod1", "vecs"], ["mod1"])
        tasks.append(fin)
        return tasks

    def ada1_tasks():
        tasks = []

        def mk(kc, g):
            def f():
                st, ks = wst.next()
                DMA(st, wada_d[1][kc, :, g * 1024:(g + 1) * 1024], ks, [], [ks])
                ps, kp = psum("misc", [7])
                for j in range(8):
                    MM(ps[:, j:j + 1], st[:, j * 128:(j + 1) * 128], sc[:, kc:kc + 1], True, True, [ks], [kp])
                dst = mods[1][:, g * 8:(g + 1) * 8]
                if kc == 0:
                    CP("vector", dst, ps[:, 0:8], [kp], ["mod1"])
                else:
                    TT("vector", dst, dst, ps[:, 0:8], ALU.add, [kp, "mod1"], ["mod1"])
            return f
        for kc in range(8):
            for g in range(3):
                tasks.append(mk(kc, g))

        def fin():
            TT("vector", mods[1], mods[1], V("b_ada1"), ALU.add, ["mod1", "vecs"], ["mod1"])
            TS("vector", gsv[1], mods[1][:, 8:16], 1.0, None, ALU.add, None, ["mod1"], ["mod1"])
            TT("vector", gsv[1], gsv[1], V("norm_g1"), ALU.mult, ["mod1", "vecs"], ["mod1"])
        tasks.append(fin)
        return tasks
    shiftv = [mods[0][:, 0:8], mods[1][:, 0:8]]
    gatev = [mods[0][:, 16:24], mods[1][:, 16:24]]

    def rms_rstd(n, tag, pool=None):
        blk = slice(n * TB, (n + 1) * TB)
        pool = pool or tmpf
        ps, kp = psum("stat", [2, 3])
        for c in range(8):
            sq, ksq = pool.next()
            ACT(sq, xT[:, c, blk], AF.Square, ["x%d_%d" % (c, n)], [ksq])
            MM(ps, ones_f, sq, c == 0, c == 7, [ksq, "const"], [kp])
        rs, krs = rsd[:, :], "rsd"
        TS("vector", rs, ps, 1.0 / D, EPS, ALU.mult, ALU.add, [kp], [krs])
        ACT(rs, rs, AF.Sqrt, [krs], [krs])
        RECIP(rs, rs, [krs], [krs])
        return rs, krs

    def load_w(src, width, eng="scalar", scratch=None, skey=None, reload=False):
        if reload:
            wb, kb = wring.next()
            DMA(wb[:, 0:width], scratch, "d" + kb, [skey], [kb])
            return wb, kb
        st, ks = wst.next()
        DMA(st[:, 0:width], src, ks, [], [ks])
        wb, kb = wbf.next()
        if eng == "scalar":
            ACT(wb[:, 0:width], st[:, 0:width], AF.Copy, [ks], [kb])
        else:
            CP(eng, wb[:, 0:width], st[:, 0:width], [ks], [kb])
        if scratch is not None:
            DMA(scratch, wb[:, 0:width], "ws" + kb, [kb], [skey])
        return wb, kb

    if do_l0:
        hT = arena16[:, 0:4096].rearrange("p (c t) -> p c t", c=8)
        convo = arena16[:, 4096:8192].rearrange("p (c t) -> p c t", c=8)
        y16 = arena16[:, 8192:14336].rearrange("p (c t) -> p c t", c=12)
        u16 = arena16[:, 14336:16384].rearrange("p (c t) -> p c t", c=4)
        hT2 = arena16[:, 16384:20480].rearrange("p (c t) -> p c t", c=8)
        hTb = [hT, hT2]
        yg16 = arena16[:, 20480:22528].rearrange("p (c t) -> p c t", c=4)
        halo = arena16[:, 22528:22784].rearrange("p (c t) -> p c t", c=8)
        bbT = [arena16[:, 22784:24832].rearrange("p (q m) -> p q m", q=16),
               arena16[:, 24832:26880].rearrange("p (q m) -> p q m", q=16)]
        cT16 = [arena16[:, 26880:28928].rearrange("p (q m) -> p q m", q=16),
                arena16[:, 28928:30976].rearrange("p (q m) -> p q m", q=16)]
        wglu16 = arena16[:, 30976:33024]
        stg = [arena[:, 0:2048], arena[:, 2048:4096], arena[:, 4096:6144], arena[:, 6144:8192]]
        DMA(stg[0], bpre_d[:, :], "stg0", [], ["stg0", "cstage"])
        DMA(stg[1], bpim_d[:, :], "stg1", [], ["stg1"])
        sincos(th, sn_, cs_, "s5", "s5", "s5")
        TT("vector", are, rr, cs_, ALU.mult, K5, K5)
        TT("vector", aim, rr, sn_, ALU.mult, K5, K5)
        TS("vector", are, are, -1.0, None, ALU.add, None, K5, K5)
        TT("vector", den, V("lamre"), V("lamre"), ALU.mult, ["vecs"], K5)
        TT("vector", t1_, V("lamim"), V("lamim"), ALU.mult, ["vecs"], K5)
        TT("vector", den, den, t1_, ALU.add, K5, K5)
        RECIP(den, den, K5, K5)
        TT("vector", t1_, are, V("lamre"), ALU.mult, K5 + ["vecs"], K5)
        TT("vector", t2_, aim, V("lamim"), ALU.mult, K5 + ["vecs"], K5)
        TT("vector", t1_, t1_, t2_, ALU.add, K5, K5)
        TT("vector", cr, t1_, den, ALU.mult, K5, K5)
        TT("vector", t1_, aim, V("lamre"), ALU.mult, K5 + ["vecs"], K5)
        TT("vector", t2_, are, V("lamim"), ALU.mult, K5 + ["vecs"], K5)
        TT("vector", t1_, t1_, t2_, ALU.subtract, K5, K5)
        TT("vector", ci, t1_, den, ALU.mult, K5, K5)
        TS("vector", nci, ci, -1.0, None, ALU.mult, None, K5, K5)
        for q in range(16):
            qs = slice(q * 128, (q + 1) * 128)
            t_a, ka = tmpf.next()
            TS("vector", t_a[:, 0:128], stg[0][:, qs], cr[:, q:q + 1], None, ALU.mult, None, ["stg0"] + K5, [ka])
            STT(t_a[:, 0:128], stg[1][:, qs], nci[:, q:q + 1], t_a[:, 0:128], ALU.mult, ALU.add,
                ["stg1", ka] + K5, [ka])
            TS("vector", t_a[:, 128:256], stg[1][:, qs], cr[:, q:q + 1], None, ALU.mult, None, ["stg1"] + K5, [ka])
            STT(t_a[:, 128:256], stg[0][:, qs], ci[:, q:q + 1], t_a[:, 128:256], ALU.mult, ALU.add,
                ["stg0", ka] + K5, [ka])
            ps, kp = psum("misc", [7])
            for comp in range(2):
                src = t_a[:, comp * 128:(comp + 1) * 128]
                dst = ps[:, comp * 128:(comp + 1) * 128]
                P.op("tensor", lambda e, dst=dst, src=src: e.transpose(dst, src, ident_f), [ka, "const"], [kp])
            CP("vector", bbT[0][:, q, :], ps[:, 0:128], [kp], ["bbT"])
            CP("vector", bbT[1][:, q, :], ps[:, 128:256], [kp], ["bbT"])
        DMA(stg[2], cpre_d[:, :], "stg2", [], ["stg2"])
        DMA(stg[3], cpim_d[:, :], "stg3", [], ["stg3"])
        CP("gpsimd", cT16[0].rearrange("p q m -> p (q m)"), stg[2], ["stg2"], ["cT16"])
        TS("gpsimd", cT16[1].rearrange("p q m -> p (q m)"), stg[3], -1.0, None, ALU.mult, None, ["stg3"], ["cT16"])
        DMA(stg[0], wglu_d[:, :], "stg0", [], ["stg0"])
        CP("gpsimd", wglu16, stg[0], ["stg0"], ["wglu16"])
        P.barrier()

        for c in range(8):
            o0 = _VEC["conv_w"][0] + c * 31
            identb = bass.AP(c16, 896, [[8 * 128, 128], [0, 31], [1, 128]])
            wbc = bass.AP(vecs, o0, [[NV, 128], [1, 31], [0, 128]])
            dg3 = dgt.rearrange("p (k m) -> p k m", k=31)
            TT("vector", dg3, identb, wbc, ALU.mult, ["c16", "vecs"], ["dgA", "dgB"])
            DMA(dgd[c, :, :], dgt, "dgout", ["dgA", "dgB"], ["dgd%d" % c])
        def proj(m, nb=None):
            if nb is None:
                nb = curblk["n"]
            hb = hTb[nb % 2]
            wb, kb = load_w(win0_d[m, :, :], 1024, scratch=win0b[m, :, :], skey="win0b%d" % m, reload=(nb > 0))
            ps, kp = psum("proj", [0, 1])
            for c in range(8):
                MM(ps, wb[:, c * 128:(c + 1) * 128], hb[:, c, :], c == 0, c == 7, [kb, "hT%d_%d" % (c, nb % 2)], [kp])
            return ps, kp

        def head_ops(n):
            blk = slice(n * TB, (n + 1) * TB)
            ops = []
            st = {}

            def f_rms():
                st["rs"] = rms_rstd(n, "l0", pool=pg)
            ops.append(f_rms)

            def mk_h(c):
                def f():
                    rs, krs = st["rs"]
                    tmp, kt = pg.next()
                    STT(tmp, xT[:, c, blk], gsv[0][:, c:c + 1], rs, ALU.mult, ALU.mult,
                        ["x%d_%d" % (c, n), krs, "small"], [kt])
                    ACT(hTb[n % 2][:, c, :], tmp, AF.Identity, [kt, "small"], ["hT%d_%d" % (c, n % 2)],
                        bias=shiftv[0][:, c:c + 1])
                return f
            for c in range(8):
                ops.append(mk_h(c))

            def mk_u(cc):
                def f():
                    psu, kpu = proj(24 + cc, n)
                    ACT(u16[:, cc, :], psu, AF.Identity, [kpu], ["u16_%d" % cc])
                return f
            for cc in range(4):
                ops.append(mk_u(cc))
            return ops

        tail_prev = []
        curblk = {"n": 0}
        ada1_q = ada1_ring_tasks() if (do_l1 and not ADA1_UPFRONT) else []
        for n in range(NB):
            blk = slice(n * TB, (n + 1) * TB)
            if n == 1:
                while tail_prev:
                    tail_prev.pop(0)()
                P.barrier()
            curblk["n"] = n
            if n <= 1:
                for f in head_ops(n):
                    f()
            hq = head_ops(n + 1) if (n >= 1 and n + 1 < NB) else []
            tq = tail_prev
            tail_prev = []

            cst = {}

            def conv_c1a(c):
                psg, kg = proj(8 + c)
                sig, ksig = tmpf.next()
                ACT(sig, psg, AF.Tanh, [kg], [ksig], scale=0.5)
                psv, kv = proj(c)
                cst[c] = {"sig": sig, "ksig": ksig, "psv": psv, "kv": kv}

            def conv_c1b(c):
                d = cst[c]
                sig, ksig, psv, kv = d["sig"], d["ksig"], d["psv"], d["kv"]
                DMA(dgt[:, 0:2048], dgd[c, :, 0:2048], "dgA", ["dgd%d" % c], ["dgA"])
                DMA(dgt[:, 2048:3968], dgd[c, :, 2048:3968], "dgB", ["dgd%d" % c], ["dgB"])
                glu, kgl = glup.next()
                if n == 0:
                    MEMSET("vector", glu[:, 0:30], 0.0, [kgl])
                else:
                    CP("vector", glu[:, 0:30], halo[:, c, 0:30], ["halo%d" % c], [kgl])
                STT(glu[:, 30:542], sig, 1.0, psv, ALU.add, ALU.mult, [kv, ksig], [kgl])
                if n < NB - 1:
                    CP("vector", halo[:, c, 0:30], glu[:, 512:542], [kgl], ["halo%d" % c])
                d["glu"] = glu; d["kgl"] = kgl

            def conv_c2(c):
                d = cst.pop(c)
                glu, kgl = d["glu"], d["kgl"]
                cps, kcp = psum("conv", [2, 3])
                for k in range(31):
                    MM(cps, dgt[:, k * 128:(k + 1) * 128], glu[:, k:k + 512], k == 0, k == 30,
                       ["dgA" if k < 16 else "dgB", kgl], [kcp])
                ACT(convo[:, c, :], cps, AF.Identity, [kcp, "vecs"], ["convo%d" % c], bias=V("conv_b", c, c + 1))

            pst = {}
            ysta = {}

            def s5_p2a_act(q):
                sn, ksn = plong.next()
                cs, kcs = plong.next()
                DMA(cs, tabd[q, :, 0:512], "d" + kcs, ["tabd%d" % q], [kcs])
                DMA(sn, tabd[q, :, 512:1024], "d" + ksn, ["tabd%d" % q], [ksn])
                pst[q] = {"sn": sn, "ksn": ksn, "cs": cs, "kcs": kcs}

            def s5_p2a_dve(q):
                pass

            def s5_p2a_sin(q):
                pass

            def s5_p1(q):
                cc = q // 4
                bre, kbre = psum("s5b", [4, 5])
                bim, kbim = psum("s5b", [4, 5])
                MM(bre, bbT[0][:, q, :], u16[:, cc, :], True, True, ["bbT", "u16_%d" % cc], [kbre])
                MM(bim, bbT[1][:, q, :], u16[:, cc, :], True, True, ["bbT", "u16_%d" % cc], [kbim])
                pst[q].update({"bre": bre, "kbre": kbre, "bim": bim, "kbim": kbim})

            def s5_p2b(q):
                d = pst[q]
                bre, kbre, bim, kbim = d["bre"], d["kbre"], d["bim"], d["kbim"]
                sn, ksn, cs, kcs = d["sn"], d["ksn"], d["cs"], d["kcs"]
                btr, kbtr = plong.next()
                bti, kbti = plong.next()
                m2, km2 = pshort.next()
                TT("vector", btr, bre, cs, ALU.mult, [kbre, kcs], [kbtr])
                TT("vector", m2, bim, sn, ALU.mult, [kbim, ksn], [km2])
                TT("vector", btr, btr, m2, ALU.add, [kbtr, km2], [kbtr])
                TT("vector", bti, bim, cs, ALU.mult, [kbim, kcs], [kbti])
                TT("vector", m2, bre, sn, ALU.mult, [kbre, ksn, km2], [km2])
                TT("vector", bti, bti, m2, ALU.subtract, [kbti, km2], [kbti])
                rb = bass.AP(s5, 3 * 16 + q, [[256, 128], [0, 512]])
                st_re = small[:, 96 + q:97 + q]
                st_im = small[:, 112 + q:113 + q]
                kst = "st%d" % q
                if n > 0:
                    i_re = small[:, 192 + q:193 + q]
                    i_im = small[:, 208 + q:209 + q]
                    t_a = small[:, 224 + q:225 + q]
                    t_b = small[:, 240 + q:241 + q]
                    TS("vector", t_a, st_re, c512[:, q:q + 1], None, ALU.mult, None, [kst, "e512"], [kst])
                    STT(i_re, st_im, ns512[:, q:q + 1], t_a, ALU.mult, ALU.add, [kst, "e512"], [kst])
                    TS("vector", t_b, st_re, s512[:, q:q + 1], None, ALU.mult, None, [kst, "e512"], [kst])
                    STT(i_im, st_im, c512[:, q:q + 1], t_b, ALU.mult, ALU.add, [kst, "e512"], [kst])
                for (bt, kbt, stv) in ((btr, kbtr, (i_re if n > 0 else None)), (bti, kbti, (i_im if n > 0 else None))):
                    init = 0.0 if n == 0 else stv
                    P.op("vector", lambda e, bt=bt, init=init, rb=rb: e.tensor_tensor_scan(
                        out=bt, data0=rb, data1=bt, initial=init, op0=ALU.mult, op1=ALU.add),
                        [kbt, "s5", kst], [kbt])
                if n < NB - 1:
                    CP("vector", st_re, btr[:, 511:512], [kbtr], [kst])
                    CP("vector", st_im, bti[:, 511:512], [kbti], [kst])
                d.update({"btr": btr, "kbtr": kbtr, "bti": bti, "kbti": kbti})

            def s5_p3(q):
                d = pst[q]
                sn, ksn, cs, kcs = d["sn"], d["ksn"], d["cs"], d["kcs"]
                btr, kbtr, bti, kbti = d["btr"], d["kbtr"], d["bti"], d["kbti"]
                sre, ksre = tmph.next()
                sim, ksim = tmph.next()
                g1, kg1 = pg.next()
                g2, kg2 = pg.next()
                TT(P3ENG, g1, btr, cs, ALU.mult, [kbtr, kcs], [kg1])
                TT(P3ENG, g2, bti, sn, ALU.mult, [kbti, ksn], [kg2])
                TT(P3ENG, sre, g1, g2, ALU.subtract, [kg1, kg2], [ksre])
                TT(P3ENG, g1, btr, sn, ALU.mult, [kbtr, ksn, kg1], [kg1])
                TT(P3ENG, g2, bti, cs, ALU.mult, [kbti, kcs, kg2], [kg2])
                TT(P3ENG, sim, g1, g2, ALU.add, [kg1, kg2], [ksim])
                d.update({"sre": sre, "ksre": ksre, "sim": sim, "ksim": ksim})

            def s5_p4(q):
                cc = q // 4
                d = pst.pop(q)
                sre, ksre, sim, ksim = d["sre"], d["ksre"], d["sim"], d["ksim"]
                if q % 4 == 0:
                    ysta["y"] = psum("s5y", [6])
                y_ps, ky = ysta["y"]
                MM(y_ps, cT16[0][:, q, :], sre, q % 4 == 0, False, ["cT16", ksre], [ky])
                MM(y_ps, cT16[1][:, q, :], sim, False, q % 4 == 3, ["cT16", ksim], [ky])
                if q % 4 == 3:
                    yb, kyb = tmpf.next()
                    STT(yb, u16[:, cc, :], V("ssm_d", cc, cc + 1), y_ps, ALU.mult, ALU.add,
                        ["u16_%d" % cc, "vecs", ky], [kyb])
                    ACT(yg16[:, cc, :], yb, AF.Gelu_apprx_tanh, [kyb], ["yg16_%d" % cc])

            lnst = {}

            def ln_stats():
                s_ps, ks_ = psum("stat", [2, 3])
                q_ps, kq_ = psum("stat", [2, 3])
                for c in range(8):
                    MM(s_ps, ones16, convo[:, c, :], c == 0, c == 7, ["convo%d" % c, "c16"], [ks_])
                for c in range(8):
                    sq, ksq = pg.next()
                    ACT(sq, convo[:, c, :], AF.Square, ["convo%d" % c], [ksq])
                    MM(q_ps, ones_f, sq, c == 0, c == 7, [ksq, "const"], [kq_])
                mean, kmean = lnm[:, :], "lnm"
                TS("vector", mean, s_ps, 1.0 / 1024, None, ALU.mult, None, [ks_], [kmean])
                rl, krl = lnr[:, :], "lnr"
                TT("vector", rl, mean, mean, ALU.mult, [kmean], [krl])
                STT(rl, q_ps, 1.0 / 1024, rl, ALU.mult, ALU.subtract, [kq_, krl], [krl])
                TS("vector", rl, rl, EPS, None, ALU.add, None, [krl], [krl])
                ACT(rl, rl, AF.Sqrt, [krl], [krl])
                RECIP(rl, rl, [krl], [krl])

            nst = {}

            def norm_a(c):
                mean, kmean = lnm[:, :], "lnm"
                rl, krl = lnr[:, :], "lnr"
                t1, k1 = tmpf.next()
                TT("vector", t1, convo[:, c, :], mean, ALU.subtract, ["convo%d" % c, kmean], [k1])
                TT("vector", t1, t1, rl, ALU.mult, [k1, krl], [k1])
                ACT(t1, t1, AF.Silu, [k1, "vecs"], [k1], scale=V("ln_g", c, c + 1), bias=V("ln_b", c, c + 1))
                psa, kpa = proj(16 + c)
                sga, ksga = tmpf.next()
                ACT(sga, psa, AF.Silu, [kpa], [ksga])
                nst[c] = (t1, k1, sga, ksga)

            def norm_b(c):
                t1, k1, sga, ksga = nst.pop(c)
                TT("vector", y16[:, c, :], t1, sga, ALU.mult, [k1, ksga], ["y16_%d" % c])

            for k in range(21):
                if n >= 1 and ada1_q:
                    ada1_q.pop(0)()
                for _ in range(2):
                    if tq and k < 6:
                        tq.pop(0)()
                if hq:
                    if k == 10:
                        hq.pop(0)()
                    elif 11 <= k <= 14:
                        hq.pop(0)(); hq.pop(0)()
                    elif k >= 19:
                        hq.pop(0)(); hq.pop(0)()
                if 0 <= k - 2 < 16:
                    s5_p3(k - 2)
                if k < 16:
                    s5_p2a_act(k)
                if 0 <= k - 1 < 16:
                    s5_p2b(k - 1)
                if 0 <= k - 3 < 16:
                    s5_p4(k - 3)
                if k == 11:
                    ln_stats()
                if 0 <= k - 2 < 8:
                    conv_c2(k - 2)
                if k < 16:
                    s5_p1(k)
                    s5_p2a_dve(k)
                    s5_p2a_sin(k)
                if 0 <= k - 1 < 8:
                    conv_c1b(k - 1)
                if k < 8:
                    conv_c1a(k)
                if 8 <= k < 12:
                    mo_ = k - 8
                    psb_, kpb = proj(28 + mo_)
                    ACT(y16[:, 8 + mo_, :], psb_, AF.Silu, [kpb], ["y16_%d" % (8 + mo_)])
                if 0 <= k - 13 < 8:
                    norm_b(k - 13)
                if 0 <= k - 12 < 8:
                    norm_a(k - 12)
            def mk_glu(mo, n=n):
                def f():
                    ps, kp = psum("misc", [7])
                    for k in range(4):
                        MM(ps, wglu16[:, k * 512 + mo * 128:k * 512 + (mo + 1) * 128], yg16[:, k, :], k == 0, k == 3,
                           ["wglu16", "yg16_%d" % k], [kp])
                    sg, ksg = tmpf.next()
                    ACT(sg, ps, AF.Sigmoid, [kp, "vecs"], [ksg], bias=V("b_glu", mo, mo + 1))
                    TT("vector", sg, sg, yg16[:, mo, :], ALU.mult, [ksg, "yg16_%d" % mo], [ksg])
                    ky_ = "y16_%d" % (8 + mo)
                    TT("vector", y16[:, 8 + mo, :], sg, y16[:, 8 + mo, :], ALU.mult, [ksg, ky_], [ky_])
                return f

            def mk_wout(mo, n=n, blk=blk):
                def f():
                    wb, kb = load_w(wout0_d[mo, :, 0:1024], 1024, scratch=wout0b[mo, :, 0:1024],
                                    skey="wout0bA%d" % mo, reload=(n > 0))
                    wb2, kb2 = load_w(wout0_d[mo, :, 1024:1536], 512, scratch=wout0b[mo, :, 1024:1536],
                                      skey="wout0bB%d" % mo, reload=(n > 0))
                    ps, kp = psum("misc", [7])
                    for k in range(8):
                        MM(ps, wb[:, k * 128:(k + 1) * 128], y16[:, k, :], k == 0, False, [kb, "y16_%d" % k], [kp])
                    for k in range(8, 12):
                        MM(ps, wb2[:, (k - 8) * 128:(k - 7) * 128], y16[:, k, :], False, k == 11,
                           [kb2, "y16_%d" % k], [kp])
                    xk = "x%d_%d" % (mo, n)
                    STT(xT[:, mo, blk], ps, gatev[0][:, mo:mo + 1], xT[:, mo, blk], ALU.mult, ALU.add,
                        [kp, "small", xk], [xk])
                return f
            tail_prev = [mk_glu(mo) for mo in range(4)] + [mk_wout(mo) for mo in range(8)]
        for f in tail_prev:
            f()
        while ada1_q:
            ada1_q.pop(0)()
        P.barrier()

    if do_l1:
        hT1 = arena16[:, 0:16384].rearrange("p (c t) -> p c t", c=8)
        def mkset(base16, o):
            return {
                "qT": base16[:, o:o + 2048],
                "kpad": [base16[:, o + 2048:o + 4096], base16[:, o + 4096:o + 6144]],
                "vpad": [base16[:, o + 6144:o + 8192].rearrange("p (t m) -> p t m", t=16),
                         base16[:, o + 8192:o + 10240].rearrange("p (t m) -> p t m", t=16)],
                "gsl": base16[:, o + 10240:o + 12288],
                "opair": base16[:, o + 12288:o + 14336],
            }
        sets = [mkset(arena16, 16384), mkset(parena16, 0)]
        ssum = [[arena16[:, 30720:31232], arena16[:, 31232:31744]],
                [arena16[:, 31744:32256], arena16[:, 32256:32768]]]
        MEMSET("gpsimd", arena16[:, 16384 + 2048:16384 + 10240], 0.0, ["kpad0_0", "kpad1_0", "vpad0_0", "vpad1_0"])
        MEMSET("gpsimd", parena16[:, 2048:10240], 0.0, ["kpad0_1", "kpad1_1", "vpad0_1", "vpad1_1"])
        for n in range(NB):
            blk = slice(n * TB, (n + 1) * TB)
            rs, krs = rms_rstd(n, "l1")
            for c in range(8):
                tmp, kt = tmpf.next()
                STT(tmp, xT[:, c, blk], gsv[1][:, c:c + 1], rs, ALU.mult, ALU.mult,
                    ["x%d_%d" % (c, n), krs, "small"], [kt])
                ACT(hT1[:, c, blk], tmp, AF.Identity, [kt, "small"], ["h1_%d_%d" % (c, n)],
                    bias=shiftv[1][:, c:c + 1])

        ZB = [0, 1, 2, 3]
        OB = [4, 5]

        def proj_tasks(hp):
            S_ = sets[hp % 2]
            sx = hp % 2
            tasks = []
            wref = {}

            def mk_load(sec):
                def t():
                    wref[sec] = load_w(win1_d[sec * 8 + hp, :, :], 1024, "vector")
                return t

            def mk_grp(sec, n):
                def t():
                    wb, kb = wref[sec]
                    ps, kp = psum("proj1", [6, 7])
                    for c in range(8):
                        MM(ps, wb[:, c * 128:(c + 1) * 128], hT1[:, c, n * TB:(n + 1) * TB], c == 0, c == 7,
                           [kb, "h1_%d_%d" % (c, n)], [kp])
                    cols = slice(n * TB, (n + 1) * TB)
                    if sec == 0:
                        TS("vector", S_["qT"][:, cols], ps, 0.125, None, ALU.mult, None, [kp], ["qT_%d" % sx])
                    elif sec == 1:
                        CP("vector", S_["kpad"][0][0:64, cols], ps[0:64, :], [kp], ["kpad0_%d" % sx])
                        CP("vector", S_["kpad"][1][64:128, cols], ps[64:128, :], [kp], ["kpad1_%d" % sx])
                    else:
                        ACT(S_["gsl"][:, cols], ps, AF.Silu, [kp], ["gsl_%d" % sx])
                return t

            def mk_v(tt):
                def t():
                    wb, kb = wref[2]
                    ps, kp = psum("proj1", [6, 7])
                    n = tt // 4
                    for c in range(8):
                        MM(ps[:, 0:128], hT1[:, c, tt * 128:(tt + 1) * 128], wb[:, c * 128:(c + 1) * 128],
                           c == 0, c == 7, [kb, "h1_%d_%d" % (c, n)], [kp])
                    CP("vector", S_["vpad"][0][:, tt, 0:64], ps[:, 0:64], [kp], ["vpad0_%d" % sx])
                    CP("vector", S_["vpad"][1][:, tt, 64:128], ps[:, 64:128], [kp], ["vpad1_%d" % sx])
                return t

            for sec in (0, 1, 3):
                tasks.append(mk_load(sec))
                for n in range(NB):
                    tasks.append(mk_grp(sec, n))
            tasks.append(mk_load(2))
            for tt in range(16):
                tasks.append(mk_v(tt))
            return tasks

        def wout_tasks(hp):
            S_ = sets[hp % 2]
            sx = hp % 2
            tasks = []
            wref = {}

            def ld():
                wref[0] = load_w(wout1_d[hp, :, :], 1024, "vector")
            tasks.append(ld)

            def mk(mo, n):
                def t():
                    wb, kb = wref[0]
                    blk = slice(n * TB, (n + 1) * TB)
                    ps, kp = psum("proj1", [6, 7])
                    MM(ps, wb[:, mo * 128:(mo + 1) * 128], S_["opair"][:, blk], True, True, [kb, "opair_%d" % sx], [kp])
                    xk = "x%d_%d" % (mo, n)
                    STT(xT[:, mo, blk], ps, gatev[1][:, mo:mo + 1], xT[:, mo, blk], ALU.mult, ALU.add,
                        [kp, "small", xk], [xk])
                return t
            for mo in range(8):
                for n in range(NB):
                    tasks.append(mk(mo, n))
            return tasks

        items = []
        for hp in range(8):
            for g4 in range(4):
                nblk = 4 * g4 + 4
                for b in range(nblk - 1, -1, -1):
                    for hi in range(2):
                        items.append((hp, hi, g4, b, nblk))
        info = {}
        hstate = {0: [0, 0], 1: [0, 0]}
        ostate = {}

        def stA1(it):
            hp, hi, g4, b, nblk = it
            S_ = sets[hp % 2]; sx = hp % 2
            T0 = g4 * TB
            c0 = max(0, b * 128 - T0)
            Z, kz = psum("Z", ZB)
            MM(Z[:, c0:TB], S_["kpad"][hi][:, b * 128:(b + 1) * 128], S_["qT"][:, T0 + c0:T0 + TB], True, True,
               ["kpad%d_%d" % (hi, sx), "qT_%d" % sx], [kz])
            info[it] = {"Z": Z, "kz": kz, "c0": c0}

        def stA2(it):
            hp, hi, g4, b, nblk = it
            d = info[it]
            Z, kz, c0 = d["Z"], d["kz"], d["c0"]
            first = (b == nblk - 1)
            e_, ke = tmpf.next()
            ACT(e_[:, c0:TB], Z[:, c0:TB], AF.Exp, [kz], [ke])
            sp, ksp = tmph.next()
            ACT(sp[:, c0:TB], e_[:, c0:TB], AF.Ln, [ke], [ksp], bias=1.0)
            if b >= 4 * g4:
                TT("vector", sp[:, c0:c0 + 128], sp[:, c0:c0 + 128], tri01_16, ALU.mult, [ksp, "c16"], [ksp])
            if first:
                MEMSET("gpsimd", ssum[hi][0], 0.0, ["ssum%d_0" % hi])
                MEMSET("gpsimd", ssum[hi][1], 0.0, ["ssum%d_1" % hi])
                hstate[hi][0] = 0
            cur = hstate[hi][0]
            d["sp"] = sp; d["ksp"] = ksp; d["cur"] = cur
            if b > 0:
                nxt = 1 - cur
                TT("vector", ssum[hi][nxt][:, c0:TB], ssum[hi][cur][:, c0:TB], sp[:, c0:TB], ALU.add,
                   ["ssum%d_%d" % (hi, cur), ksp], ["ssum%d_%d" % (hi, nxt)])
                hstate[hi][0] = nxt

        def stB1(it):
            hp, hi, g4, b, nblk = it
            d = info[it]
            Z, kz, c0, sp, ksp, cur = d["Z"], d["kz"], d["c0"], d["sp"], d["ksp"], d["cur"]
            first = (b == nblk - 1)
            diag = (b >= 4 * g4)
            kc_ = "ssum%d_%d" % (hi, cur)
            MM(Z[:, c0:TB], nti16, sp[:, c0:TB], False, (first and not diag), [ksp, "c16"], [kz], skip=True)
            if not first:
                MM(Z[:, c0:TB], negones16, ssum[hi][cur][:, c0:TB], False, not diag, [kc_, "c16"], [kz], skip=True)
            if diag:
                MM(Z[:, c0:c0 + 128], ident16, negbig16, False, True, ["c16"], [kz], skip=True)

        def stB2(it):
            d = info[it]
            Z, kz, c0 = d["Z"], d["kz"], d["c0"]
            w16, kw = tmph.next()
            ACT(w16[:, c0:TB], Z[:, c0:TB], AF.Exp, [kz], [kw])
            d["w16"] = w16; d["kw"] = kw

        def stB3(it):
            hp, hi, g4, b, nblk = it
            S_ = sets[hp % 2]; sx = hp % 2
            T0 = g4 * TB
            d = info.pop(it)
            c0, w16, kw = d["c0"], d["w16"], d["kw"]
            if b == nblk - 1:
                O, ko = psum("O%d" % hi, [OB[hi]])
                ostate[hi] = (O, ko)
                MM(O, zeros16, S_["qT"][:, 0:TB], True, False, ["c16", "qT_%d" % sx], [ko])
            O, ko = ostate[hi]
            MM(O[:, c0:TB], S_["vpad"][hi][:, b, :], w16[:, c0:TB], False, b == 0, [kw, "vpad%d_%d" % (hi, sx)], [ko])
            if b == 0:
                rows = slice(hi * 64, (hi + 1) * 64)
                TT("vector", S_["opair"][rows, T0:T0 + TB], O[rows, :], S_["gsl"][rows, T0:T0 + TB], ALU.mult,
                   [ko, "gsl_%d" % sx], ["opair_%d" % sx])

        for t in proj_tasks(0):
            t()
        NI = len(items)
        PER = NI // 8
        queue = []
        for s_ in range(NI + 4):
            p_ = s_ // PER
            r_ = s_ % PER
            if s_ < NI and r_ == 5 and p_ + 1 < 8:
                queue.extend(proj_tasks(p_ + 1))
            if s_ < NI and r_ == 6 and p_ >= 1:
                queue = wout_tasks(p_ - 1) + queue
            if 0 <= s_ - 2 < NI:
                stB1(items[s_ - 2])
            if 0 <= s_ - 4 < NI:
                stB3(items[s_ - 4])
            if s_ < NI:
                if r_ == 0:
                    while queue:
                        queue.pop(0)()
                stA1(items[s_])
            if 0 <= s_ - 1 < NI:
                stA2(items[s_ - 1])
            if 0 <= s_ - 3 < NI:
                stB2(items[s_ - 3])
            if queue:
                left = PER - r_ - 2
                ntask = len(queue) if left <= 0 else -(-len(queue) // left)
                for _ in range(min(ntask, len(queue))):
                    queue.pop(0)()
        while queue:
            queue.pop(0)()
        for t in wout_tasks(7):
            t()
        P.barrier()

    diagG = arena[:, 0:1024].rearrange("p (c m) -> p c m", c=8)
    for c in range(8):
        TS("vector", diagG[:, c, :], ident_f, V("final_g", c, c + 1), None, ALU.mult, None,
           ["const", "vecs"], ["diagG"])
    obuf = [arena[:, 1024:2048], arena[:, 2048:3072]]
    frs = None
    for tt in range(16):
        n = tt // 4
        tsl = slice(tt * 128, (tt + 1) * 128)
        if tt % 4 == 0:
            frs = rms_rstd(n, "fin")
        rs, krs = frs
        ps2, k2 = psum("fin2", [4, 5])
        MM(ps2[:, 0:1], rs[:, (tt % 4) * 128:(tt % 4 + 1) * 128], ident_f[:, 0:1], True, True, [krs, "const"], [k2])
        rt, krt = tmpf.next()
        CP("vector", rt[:, 0:1], ps2[:, 0:1], [k2], [krt])
        ob = obuf[tt % 2]
        kob = "obuf%d" % (tt % 2)
        for half in range(2):
            ps, kp = psum("fin", [0, 1])
            for j in range(4):
                c = half * 4 + j
                MM(ps[:, j * 128:(j + 1) * 128], xT[:, c, tsl], diagG[:, c, :], True, True,
                   ["x%d_%d" % (c, n), "diagG"], [kp])
            TS("vector", ob[:, half * 512:(half + 1) * 512], ps, rt[:, 0:1], None, ALU.mult, None,
               [kp, krt], [kob])
        DMA(out_d[tsl, :], ob, kob + "d", [kob], [kob])
    P.final_wait(["obuf0d", "obuf1d"])
    P.emit()
    return nc


def _host_inputs(inp, b):
    f = np.float32

    def colvec(v):
        v = np.asarray(v, f)
        return v.reshape(-1, 128).T

    vecs = np.zeros((128, NV), f)

    def put(name, arr):
        o, w = _VEC[name]
        assert arr.shape == (128, w), (name, arr.shape)
        vecs[:, o:o + w] = arr
    put("norm_g0", colvec(inp["l0_norm_g"]))
    put("b_ada0", colvec(inp["l0_b_ada"]))
    put("conv_b", colvec(inp["l0_conv_b"]))
    put("ln_g", colvec(inp["l0_conv_ln_g"]))
    put("ln_b", colvec(inp["l0_conv_ln_b"]))
    put("ssm_d", colvec(inp["l0_ssm_d"]))
    put("b_glu", colvec(inp["l0_ssm_b_glu"]))
    put("norm_g1", colvec(inp["l1_norm_g"]))
    put("b_ada1", colvec(inp["l1_b_ada"]))
    put("final_g", colvec(inp["final_norm_g"]))
    put("cT", colvec(inp["c"][b]))
    cw = np.asarray(inp["l0_conv_w"], f)
    put("conv_w", cw.T.reshape(8, 128, 31).transpose(1, 0, 2).reshape(128, 248))

    def pairlay(a):
        return np.asarray(a, f).reshape(16, 2, 64).transpose(1, 2, 0).reshape(128, 16)
    put("lamre", pairlay(inp["l0_ssm_lam_re"]))
    put("lamim", pairlay(inp["l0_ssm_lam_im"]))
    ld = np.asarray(inp["l0_ssm_log_dt"], f)
    put("logdt", pairlay(np.repeat(ld[:, None], 64, axis=1)))

    def kchunks(w, ncol):
        w = np.asarray(w, f)
        K, N = w.shape
        return np.ascontiguousarray(
            w.reshape(K // 128, 128, N // ncol, ncol).transpose(2, 1, 0, 3).reshape(N // ncol, 128, (K // 128) * ncol))

    def bpad(bmat):
        bmat = np.asarray(bmat, f)
        o = np.zeros((2, 64, 16, 128), f)
        for g in range(32):
            q, h = g // 2, g % 2
            o[h, :, q, (g % 8) * 16:(g % 8) * 16 + 16] = bmat[g]
        return o.reshape(128, 2048)

    def cpad(cmat):
        return bpad(np.asarray(cmat, f).transpose(0, 2, 1))

    d = {}
    xb = np.asarray(inp["x"][b], f)
    d["xT"] = np.ascontiguousarray(xb.reshape(L, 8, 128).transpose(2, 1, 0))
    d["vecs"] = vecs
    d["w_ada0"] = np.ascontiguousarray(np.asarray(inp["l0_w_ada"], f).reshape(8, 128, 3072))
    d["w_ada1"] = np.ascontiguousarray(np.asarray(inp["l1_w_ada"], f).reshape(8, 128, 3072))
    d["w_in0"] = kchunks(inp["l0_w_in"], 128)
    d["w_out0"] = kchunks(inp["l0_w_out"], 128)
    d["w_in1"] = kchunks(inp["l1_w_in"], 128)
    w1 = np.asarray(inp["l1_w_out"], f)
    d["w_out1"] = np.ascontiguousarray(w1.reshape(8, 128, 1024))
    wg = np.asarray(inp["l0_ssm_w_glu"], f)
    d["w_glu"] = np.ascontiguousarray(wg.reshape(4, 128, 512).transpose(1, 0, 2).reshape(128, 2048))
    d["bp_re"] = bpad(inp["l0_ssm_b_re"])
    d["bp_im"] = bpad(inp["l0_ssm_b_im"])
    d["cp_re"] = cpad(inp["l0_ssm_c_re"])
    d["cp_im"] = cpad(inp["l0_ssm_c_im"])
    cst = np.zeros((128, 5 * 128 + 512), f)
    i = np.arange(128)
    cst[:, 0:128] = np.eye(128, dtype=f)
    cst[:, 128:256] = 1.0
    cst[:, 256:384] = (i[None, :] > i[:, None]).astype(f)
    cst[:, 384:512] = -(i[:, None] >= i[None, :]).astype(f)
    cst[:, 512:640] = np.where(i[None, :] <= i[:, None], -30000.0, 0.0).astype(f)
    cst[:, 640:1152] = np.arange(512, dtype=f)[None, :]
    d["consts"] = cst
    return d


_NC_CACHE = {}


def kernel(**inputs):
    inp = {k: np.asarray(v) for k, v in inputs.items()}
    if "nc" not in _NC_CACHE:
        _NC_CACHE["nc"] = build_program()
    nc = _NC_CACHE["nc"]
    in_maps = [_host_inputs(inp, b) for b in range(8)]
    res = run_bass_kernel_spmd(nc, in_maps, core_ids=list(range(8)))
    out = np.stack([np.asarray(r["out"], np.float32).reshape(L, D) for r in res.results], axis=0)
    return out
```

```python
import math
import numpy as np
import concourse.bass as bass
import concourse.mybir as mybir
from concourse.bass_utils import run_bass_kernel_spmd

F32 = mybir.dt.float32
BF16 = mybir.dt.bfloat16
AF = mybir.ActivationFunctionType
ALU = mybir.AluOpType

L = 2048
D = 1024
NB = 4
TB = 512
EPS = 1e-6
PI = math.pi
SAME_ENG_SYNC = True
N_FILL = 0
ADA1_UPFRONT = False
P3ENG = "vector"

_VEC = {}
_off = 0
for _n, _w in [("norm_g0", 8), ("b_ada0", 24), ("conv_b", 8), ("ln_g", 8), ("ln_b", 8), ("ssm_d", 4),
               ("b_glu", 4), ("norm_g1", 8), ("b_ada1", 24), ("final_g", 8), ("cT", 8), ("conv_w", 248),
               ("lamre", 16), ("lamim", 16), ("logdt", 16)]:
    _VEC[_n] = (_off, _w)
    _off += _w
NV = _off


class _Op:
    __slots__ = ("eng", "fn", "deps", "signal", "val", "tag", "is_dma")


class Prog:
    ENG = ("sync", "tensor", "vector", "scalar", "gpsimd")

    def __init__(self, nc):
        self.nc = nc
        self.ops = {e: [] for e in self.ENG}
        self.lastw = {}
        self.readers = {}
        self.pend = {e: [] for e in self.ENG}
        self.dma_count = {}
        self.last_dma = {}

    def _add(self, op, r, w):
        deps = set()
        for k in r:
            o = self.lastw.get(k)
            if o is not None:
                deps.add(o)
        for k in w:
            o = self.lastw.get(k)
            if o is not None:
                deps.add(o)
            for o in self.readers.get(k, ()):
                deps.add(o)
        for o in self.pend[op.eng]:
            deps.add(o)
        self.pend[op.eng] = []
        deps.discard(op)
        op.deps = deps
        for d in deps:
            d.signal = True
        for k in r:
            self.readers.setdefault(k, []).append(op)
        for k in w:
            self.lastw[k] = op
            self.readers[k] = []
        self.ops[op.eng].append(op)

    def op(self, eng, fn, r=(), w=()):
        o = _Op()
        o.eng = eng; o.fn = fn; o.signal = False; o.val = 0; o.tag = None; o.is_dma = False
        self._add(o, r, w)
        return o

    def dma(self, fn, tag, r=(), w=(), q="sync"):
        o = _Op()
        o.eng = q; o.fn = fn; o.signal = True; o.tag = tag; o.is_dma = True
        self.dma_count[tag] = self.dma_count.get(tag, 0) + 1
        o.val = 16 * self.dma_count[tag]
        self.last_dma[tag] = o
        self._add(o, r, w)
        return o

    def barrier(self):
        evs = []
        for e in self.ENG:
            if self.ops[e]:
                o = self.ops[e][-1]
                o.signal = True
                evs.append(o)
        for t, o in self.last_dma.items():
            evs.append(o)
        for e in self.ENG:
            self.pend[e] = list(evs)

    def final_wait(self, tags):
        o = _Op()
        o.eng = "sync"; o.fn = (lambda e: None); o.signal = False; o.val = 0; o.tag = None; o.is_dma = False
        o.deps = set(self.last_dma[t] for t in tags)
        self.ops["sync"].append(o)

    def emit(self):
        nc = self.nc
        for e in self.ENG:
            cnt = 0
            for o in self.ops[e]:
                if o.is_dma:
                    continue
                if o.signal:
                    cnt += 1
                    o.val = cnt
        sems = {e: nc.alloc_semaphore("sem_" + e) for e in self.ENG}
        dsems = {t: nc.alloc_semaphore("dsem_" + t) for t in self.dma_count}
        ops = self.ops
        with nc.Block() as block:
            for e in self.ENG:
                def body(eng, e=e):
                    waited = {}
                    for o in ops[e]:
                        need = {}
                        for d in o.deps:
                            if d.is_dma:
                                nm = "d_" + d.tag; sem = dsems[d.tag]
                            else:
                                if d.eng == e and (e == "tensor" or e == "sync" or not SAME_ENG_SYNC):
                                    continue
                                nm = "e_" + d.eng; sem = sems[d.eng]
                            if d.val > need.get(nm, (None, 0))[1]:
                                need[nm] = (sem, d.val)
                        for nm, (sem, val) in need.items():
                            if waited.get(nm, 0) >= val:
                                continue
                            eng.wait_ge(sem, val)
                            waited[nm] = val
                        ins = o.fn(eng)
                        if ins is None:
                            continue
                        if o.is_dma:
                            ins.then_inc(dsems[o.tag], 16)
                        elif o.signal:
                            ins.then_inc(sems[e], 1)
                getattr(block, e)(body)


class Pool:
    def __init__(self, nc, name, n, shape, dtype, tiles=None):
        if tiles is None:
            self.t = [nc.alloc_sbuf_tensor("%s%d" % (name, i), list(shape), dtype)[:, :] for i in range(n)]
        else:
            self.t = list(tiles)
        self.k = ["%s%d" % (name, i) for i in range(len(self.t))]
        self.i = 0

    def next(self):
        j = self.i % len(self.t)
        self.i += 1
        return self.t[j], self.k[j]


def build_program(do_l0=True, do_l1=True):
    nc = bass.Bass("TRN2", target_bir_lowering=False)
    P = Prog(nc)

    def din(name, shape):
        return nc.dram_tensor(name, list(shape), F32, kind="ExternalInput")

    xT_d = din("xT", [128, 8, L])
    vecs_d = din("vecs", [128, NV])
    wada_d = [din("w_ada0", [8, 128, 3072]), din("w_ada1", [8, 128, 3072])]
    win0_d = din("w_in0", [32, 128, 1024])
    wout0_d = din("w_out0", [8, 128, 1536])
    win1_d = din("w_in1", [32, 128, 1024])
    wout1_d = din("w_out1", [8, 128, 1024])
    wglu_d = din("w_glu", [128, 2048])
    bpre_d = din("bp_re", [128, 2048])
    bpim_d = din("bp_im", [128, 2048])
    cpre_d = din("cp_re", [128, 2048])
    cpim_d = din("cp_im", [128, 2048])
    consts_d = din("consts", [128, 5 * 128 + 512])
    out_d = nc.dram_tensor("out", [L, D], F32, kind="ExternalOutput")
    dgd = nc.dram_tensor("dgd", [8, 128, 31 * 128], BF16)

    sb = nc.alloc_sbuf_tensor
    xT = sb("xT_sb", [128, 8, L], F32)
    vecs = sb("vecs_sb", [128, NV], F32)
    consts = sb("consts_sb", [128, 2 * 128 + 512], F32)
    ident_f = consts[:, 0:128]
    ones_f = consts[:, 128:256]
    iota_f = consts[:, 256:768]
    c16 = sb("c16", [128, 8 * 128], BF16)
    ident16 = c16[:, 0:128]
    negones16 = c16[:, 128:256]
    tri01_16 = c16[:, 256:384]
    nti16 = c16[:, 384:512]
    negbig16 = c16[:, 512:640]
    zeros16 = c16[:, 640:768]
    ones16 = c16[:, 768:896]
    halfid16 = c16[:, 896:1024]
    small = sb("small", [128, 256], F32)
    rsd = sb("rsd", [128, 512], F32)
    lnm = sb("lnm", [128, 512], F32)
    lnr = sb("lnr", [128, 512], F32)

    def V(name, j0=0, j1=None):
        o, w = _VEC[name]
        if j1 is None:
            j1 = w
        return vecs[:, o + j0:o + j1]

    arena = sb("arena", [128, 16640], F32)
    arena16 = arena.bitcast(BF16)

    tmpf = Pool(nc, "tf", 4, [128, 512], F32)
    parena = sb("parena", [128, 9728], F32)
    parena16 = parena.bitcast(BF16)
    plong = Pool(nc, "pl", 10, None, None, tiles=[parena[:, i * 512:(i + 1) * 512] for i in range(10)])
    pshort = Pool(nc, "ps_", 2, None, None, tiles=[parena[:, 5120 + i * 512:5120 + (i + 1) * 512] for i in range(2)])
    pg = Pool(nc, "pg", 2, None, None, tiles=[parena[:, 6144 + i * 512:6144 + (i + 1) * 512] for i in range(2)])
    tmph = Pool(nc, "th", 6, [128, 512], BF16)
    wstT = sb("wstT", [128, 2048], F32)
    wst = Pool(nc, "wst", 2, None, None, tiles=[wstT[:, 0:1024], wstT[:, 1024:2048]])
    wbfT = [sb("wbfT%d" % i, [128, 1024], BF16) for i in range(2)]
    wbf = Pool(nc, "wbf", 2, None, None, tiles=[t_[:, :] for t_ in wbfT])
    wstT16 = wstT.bitcast(BF16)
    wring_f32 = [t_.bitcast(F32)[:, 0:512] for t_ in wbfT] + [wstT[:, i * 512:(i + 1) * 512] for i in range(4)]
    wring = Pool(nc, "wring", 6, None, None, tiles=wbf.t + [wstT16[:, i * 1024:(i + 1) * 1024] for i in range(4)])
    wring.k = list(wbf.k) + ["wrx0", "wrx1", "wrx2", "wrx3"]
    win0b = nc.dram_tensor("win0b", [32, 128, 1024], BF16)
    wout0b = nc.dram_tensor("wout0b", [8, 128, 1536], BF16)
    dgt = parena16[:, 14336:14336 + 3968]
    glup = Pool(nc, "glu", 2, None, None, tiles=[parena16[:, 18304 + i * 544:18304 + (i + 1) * 544] for i in range(2)])

    psb = [nc.alloc_psum_tensor("psb%d" % i, [128, 512], F32) for i in range(8)]
    ps_cnt = {}

    def psum(role, banks):
        i = ps_cnt.get(role, 0)
        ps_cnt[role] = i + 1
        b = banks[i % len(banks)]
        return psb[b][:, :], "ps%d" % b

    def ACT(out, in_, func, r, w, **kw):
        P.op("scalar", lambda e: e.activation(out=out, in_=in_, func=func, **kw), r, w)

    def MM(out, lhsT, rhs, start, stop, r, w, skip=False):
        if skip:
            P.op("tensor", lambda e: e.matmul(out, lhsT, rhs, start=start, stop=stop, skip_group_check=True), r, w)
        else:
            P.op("tensor", lambda e: e.matmul(out, lhsT, rhs, start=start, stop=stop), r, w)

    def TT(eng, out, in0, in1, op, r, w):
        P.op(eng, lambda e: e.tensor_tensor(out=out, in0=in0, in1=in1, op=op), r, w)

    def TS(eng, out, in0, s1, s2, op0, op1, r, w):
        if op1 is None:
            P.op(eng, lambda e: e.tensor_scalar(out=out, in0=in0, scalar1=s1, scalar2=None, op0=op0), r, w)
        else:
            P.op(eng, lambda e: e.tensor_scalar(out=out, in0=in0, scalar1=s1, scalar2=s2, op0=op0, op1=op1), r, w)

    def STT(out, in0, scalar, in1, op0, op1, r, w):
        P.op("vector", lambda e: e.scalar_tensor_tensor(out=out, in0=in0, scalar=scalar, in1=in1,
                                                        op0=op0, op1=op1), r, w)

    def CP(eng, out, in_, r, w):
        P.op(eng, lambda e: e.tensor_copy(out=out, in_=in_), r, w)

    def MEMSET(eng, ap, val, w):
        P.op(eng, lambda e: e.memset(ap, val), (), w)

    def DMA(out, in_, tag, r, w, q="sync"):
        P.dma(lambda e: e.dma_start(out=out, in_=in_), tag, r, w, q=q)

    def RECIP(out, in_, r, w):
        P.op("vector", lambda e: e.reciprocal(out=out, in_=in_), r, w)

    MAGIC = 12582912.0
    PI_LO = 3.1415925
    CW1 = 6.28125
    CW2 = 2.0 * PI - 6.28125

    def sincos(x, sn, cs, kx, ksn, kcs):
        ACT(cs, x, AF.Identity, [kx], [kcs], scale=1.0 / (2.0 * PI), bias=MAGIC)
        ACT(cs, cs, AF.Identity, [kcs], [kcs], bias=-MAGIC)
        STT(sn, cs, -CW1, x, ALU.mult, ALU.add, [kcs, kx], [ksn])
        STT(sn, cs, -CW2, sn, ALU.mult, ALU.add, [kcs, ksn], [ksn])
        TS("vector", sn, sn, -PI_LO, PI_LO, ALU.max, ALU.min, [ksn], [ksn])
        STT(cs, sn, -1.0, sn, ALU.mult, ALU.max, [ksn], [kcs])
        ACT(cs, cs, AF.Sin, [kcs, "small"], [kcs], scale=-1.0, bias=halfpi)
        ACT(sn, sn, AF.Sin, [ksn], [ksn])

    DMA(vecs[:, :], vecs_d[:, :], "vecs", [], ["vecs"])
    DMA(consts[:, 0:256], consts_d[:, 0:256], "consts", [], ["const"])
    DMA(consts[:, 256:768], consts_d[:, 640:1152], "consts", [], ["const"])
    cstage = arena[:, 0:384]
    DMA(cstage, consts_d[:, 256:640], "cstage", [], ["cstage"])
    tri01_f = cstage[:, 0:128]
    nti_f = cstage[:, 128:256]
    negbig_f = cstage[:, 256:384]
    def load_x(n):
        for c in range(8):
            DMA(xT[:, c, n * TB:(n + 1) * TB], xT_d[:, c, n * TB:(n + 1) * TB], "xl%d_%d" % (c, n), [],
                ["x%d_%d" % (c, n)])
    load_x(0)
    CP("gpsimd", ident16, ident_f, ["const"], ["c16"])
    TS("gpsimd", negones16, ones_f, -1.0, None, ALU.mult, None, ["const"], ["c16"])
    CP("gpsimd", tri01_16, tri01_f, ["cstage"], ["c16"])
    CP("gpsimd", nti16, nti_f, ["cstage"], ["c16"])
    CP("gpsimd", negbig16, negbig_f, ["cstage"], ["c16"])
    MEMSET("gpsimd", zeros16, 0.0, ["c16"])
    CP("gpsimd", ones16, ones_f, ["const"], ["c16"])
    TS("gpsimd", halfid16, ident_f, 0.5, None, ALU.mult, None, ["const"], ["c16"])
    MEMSET("gpsimd", small[:, :], 0.0, ["small", "e512"])
    sc = small[:, 0:8]
    mods = [small[:, 8:32], small[:, 32:56]]
    gsv = [small[:, 56:64], small[:, 64:72]]
    negpi = small[:, 72:73]
    onec = small[:, 73:74]
    MEMSET("gpsimd", negpi, -PI, ["small"])
    MEMSET("gpsimd", onec, 1.0, ["small"])
    halfpi = small[:, 74:75]
    MEMSET("gpsimd", halfpi, PI / 2.0, ["small"])
    ACT(sc, V("cT"), AF.Silu, ["vecs", "small"], ["small"])

    s5 = sb("s5", [128, 16 * 16], F32)

    def S(i):
        return s5[:, i * 16:(i + 1) * 16]
    dt_, xr, th, rr, asn, acs, sn_, cs_, are, aim, den, cr, ci, t1_, t2_, nci = [S(i) for i in range(16)]
    K5 = ["s5"]
    c512 = small[:, 144:160]
    s512 = small[:, 160:176]
    ns512 = small[:, 176:192]
    tabd = nc.dram_tensor("tabd", [16, 128, 1024], F32)
    if do_l0:
        ACT(dt_, V("logdt"), AF.Exp, ["vecs"], K5)
        TT("vector", xr, V("lamre"), dt_, ALU.mult, K5 + ["vecs"], K5)
        TT("vector", th, V("lamim"), dt_, ALU.mult, K5 + ["vecs"], K5)
        ACT(rr, xr, AF.Exp, K5, K5)
        TS("vector", asn, th, 512.0, None, ALU.mult, None, K5, K5)
        sincos(asn, s512, c512, "s5", "e512", "e512")
        TS("vector", ns512, s512, -1.0, None, ALU.mult, None, ["e512"], ["e512"])

    def build_dg(c):
        o0 = _VEC["conv_w"][0] + c * 31
        idA = bass.AP(c16, 896, [[8 * 128, 128], [0, 16], [1, 128]])
        idB = bass.AP(c16, 896, [[8 * 128, 128], [0, 15], [1, 128]])
        wA = bass.AP(vecs, o0, [[NV, 128], [1, 16], [0, 128]])
        wB = bass.AP(vecs, o0 + 16, [[NV, 128], [1, 15], [0, 128]])
        TT("vector", dgt[:, 0:2048].rearrange("p (k m) -> p k m", k=16), idA, wA, ALU.mult, ["c16", "vecs"], ["dgA"])
        DMA(dgd[c, :, 0:2048], dgt[:, 0:2048], "dgoutA", ["dgA"], ["dgd%d" % c], q="gpsimd")
        TT("vector", dgt[:, 2048:3968].rearrange("p (k m) -> p k m", k=15), idB, wB, ALU.mult, ["c16", "vecs"], ["dgB"])
        DMA(dgd[c, :, 2048:3968], dgt[:, 2048:3968], "dgoutB", ["dgB"], ["dgd%d" % c], q="gpsimd")

    def gen_table(q):
        if True:
            base, kb_ = pshort.next()
            sn, ksn = plong.next()
            cs, kcs = plong.next()
            ACT(base, iota_f, AF.Identity, ["const", "s5"], [kb_], scale=th[:, q:q + 1])
            sincos(base, sn, cs, kb_, ksn, kcs)
            DMA(tabd[q, :, 0:512], cs, "to" + kcs, [kcs], ["tabd%d" % q], q="gpsimd")
            DMA(tabd[q, :, 512:1024], sn, "to" + ksn, [ksn], ["tabd%d" % q], q="gpsimd")

    sc16 = sb("sc16", [128, 8], BF16)[:, :]
    CP("vector", sc16, sc, ["small"], ["sc16"])
    for l in range(2 if ADA1_UPFRONT else 1):
        for kc in range(8):
            if do_l0:
                gen_table(l * 8 + kc)
                if l == 0:
                    build_dg(kc)
            ps, kp = psum("ada", [2])
            for g in range(6):
                st, ks = tmpf.next()
                DMA(st, wada_d[l][kc, :, g * 512:(g + 1) * 512], "d" + ks, [], [ks])
                w16_, k16 = tmph.next()
                CP("vector", w16_, st, [ks], [k16])
                for j in range(4):
                    MM(ps[:, g * 4 + j:g * 4 + j + 1], w16_[:, j * 128:(j + 1) * 128], sc16[:, kc:kc + 1], True, True,
                       [k16, "sc16"], [kp])
            if kc == 0:
                CP("vector", mods[l], ps[:, 0:24], [kp], ["small"])
            else:
                TT("vector", mods[l], mods[l], ps[:, 0:24], ALU.add, [kp, "small"], ["small"])
        bname = "b_ada%d" % l
        TT("vector", mods[l], mods[l], V(bname), ALU.add, ["small", "vecs"], ["small"])
        TS("vector", gsv[l], mods[l][:, 8:16], 1.0, None, ALU.add, None, ["small"], ["small"])
        TT("vector", gsv[l], gsv[l], V("norm_g%d" % l), ALU.mult, ["small", "vecs"], ["small"])
    if do_l0 and not ADA1_UPFRONT:
        for q_ in range(8, 16):
            gen_table(q_)
    for n_ in range(1, NB):
        load_x(n_)

    def ada1_ring_tasks():
        tasks = []
        psst = {}

        def mk(kc, g):
            def f():
                i_ = wring.i % 6
                _, ks = wring.next()
                st = wring_f32[i_]
                DMA(st, wada_d[1][kc, :, g * 512:(g + 1) * 512], "d" + ks, [], [ks])
                j_ = wring.i % 6
                w16t, k16 = wring.next()
                w16_ = w16t[:, 0:512]
                CP("vector", w16_, st, [ks], [k16])
                ps, kp = psum("misc", [7])
                for j in range(4):
                    MM(ps[:, j:j + 1], w16_[:, j * 128:(j + 1) * 128], sc16[:, kc:kc + 1], True, True,
                       [k16, "sc16"], [kp])
                dst = mods[1][:, g * 4:(g + 1) * 4]
                if kc == 0:
                    CP("vector", dst, ps[:, 0:4], [kp], ["mod1"])
                else:
                    TT("vector", dst, dst, ps[:, 0:4], ALU.add, [kp, "mod1"], ["mod1"])
            return f
        for kc in range(8):
            for g in range(6):
                tasks.append(mk(kc, g))

        def fin():
            TT("vector", mods[1], mods[1], V("b_ada1"), ALU.add, ["mod1", "vecs"], ["mod1"])
            TS("vector", gsv[1], mods[1][:, 8:16], 1.0, None, ALU.add, None, ["mod1"], ["mod1"])
            TT("vector", gsv[1], gsv[1], V("norm_g1"), ALU.mult, ["mod1", "vecs"], ["mod1"])
        tasks.append(fin)
        return tasks

    def ada1_tasks():
        tasks = []

        def mk(kc, g):
            def f():
                st, ks = wst.next()
                DMA(st, wada_d[1][kc, :, g * 1024:(g + 1) * 1024], ks, [], [ks])
                ps, kp = psum("misc", [7])
                for j in range(8):
                    MM(ps[:, j:j + 1], st[:, j * 128:(j + 1) * 128], sc[:, kc:kc + 1], True, True, [ks], [kp])
                dst = mods[1][:, g * 8:(g + 1) * 8]
                if kc == 0:
                    CP("vector", dst, ps[:, 0:8], [kp], ["mod1"])
                else:
                    TT("vector", dst, dst, ps[:, 0:8], ALU.add, [kp, "mod1"], ["mod1"])
            return f
        for kc in range(8):
            for g in range(3):
                tasks.append(mk(kc, g))

        def fin():
            TT("vector", mods[1], mods[1], V("b_ada1"), ALU.add, ["mod1", "vecs"], ["mod1"])
            TS("vector", gsv[1], mods[1][:, 8:16], 1.0, None, ALU.add, None, ["mod1"], ["mod1"])
            TT("vector", gsv[1], gsv[1], V("norm_g1"), ALU.mult, ["mod1", "vecs"], ["mod1"])
        tasks.append(fin)
        return tasks
    shiftv = [mods[0][:, 0:8], mods[1][:, 0:8]]
    gatev = [mods[0][:, 16:24], mods[1][:, 16:24]]

    def rms_rstd(n, tag):
        blk = slice(n * TB, (n + 1) * TB)
        ps, kp = psum("stat", [2, 3])
        for c in range(8):
            sq, ksq = tmpf.next()
            ACT(sq, xT[:, c, blk], AF.Square, ["x%d_%d" % (c, n)], [ksq])
            MM(ps, ones_f, sq, c == 0, c == 7, [ksq, "const"], [kp])
        rs, krs = rsd[:, :], "rsd"
        TS("vector", rs, ps, 1.0 / D, EPS, ALU.mult, ALU.add, [kp], [krs])
        ACT(rs, rs, AF.Sqrt, [krs], [krs])
        RECIP(rs, rs, [krs], [krs])
        return rs, krs

    def load_w(src, width, eng="scalar", scratch=None, skey=None, reload=False):
        if reload:
            wb, kb = wring.next()
            DMA(wb[:, 0:width], scratch, "d" + kb, [skey], [kb])
            return wb, kb
        st, ks = wst.next()
        DMA(st[:, 0:width], src, ks, [], [ks])
        wb, kb = wbf.next()
        if eng == "scalar":
            ACT(wb[:, 0:width], st[:, 0:width], AF.Copy, [ks], [kb])
        else:
            CP(eng, wb[:, 0:width], st[:, 0:width], [ks], [kb])
        if scratch is not None:
            DMA(scratch, wb[:, 0:width], "ws" + kb, [kb], [skey])
        return wb, kb

    if do_l0:
        hT = arena16[:, 0:4096].rearrange("p (c t) -> p c t", c=8)
        convo = arena16[:, 4096:8192].rearrange("p (c t) -> p c t", c=8)
        y16 = arena16[:, 8192:14336].rearrange("p (c t) -> p c t", c=12)
        u16 = arena16[:, 14336:16384].rearrange("p (c t) -> p c t", c=4)
        yg = arena[:, 8192:10240].rearrange("p (c t) -> p c t", c=4)
        yg16 = arena16[:, 20480:22528].rearrange("p (c t) -> p c t", c=4)
        halo = arena16[:, 22528:22784].rearrange("p (c t) -> p c t", c=8)
        bbT = [arena16[:, 22784:24832].rearrange("p (q m) -> p q m", q=16),
               arena16[:, 24832:26880].rearrange("p (q m) -> p q m", q=16)]
        cT16 = [arena16[:, 26880:28928].rearrange("p (q m) -> p q m", q=16),
                arena16[:, 28928:30976].rearrange("p (q m) -> p q m", q=16)]
        wglu16 = arena16[:, 30976:33024]
        stg = [arena[:, 0:2048], arena[:, 2048:4096], arena[:, 4096:6144], arena[:, 6144:8192]]
        DMA(stg[0], bpre_d[:, :], "stg0", [], ["stg0", "cstage"])
        DMA(stg[1], bpim_d[:, :], "stg1", [], ["stg1"])
        sincos(th, sn_, cs_, "s5", "s5", "s5")
        TT("vector", are, rr, cs_, ALU.mult, K5, K5)
        TT("vector", aim, rr, sn_, ALU.mult, K5, K5)
        TS("vector", are, are, -1.0, None, ALU.add, None, K5, K5)
        TT("vector", den, V("lamre"), V("lamre"), ALU.mult, ["vecs"], K5)
        TT("vector", t1_, V("lamim"), V("lamim"), ALU.mult, ["vecs"], K5)
        TT("vector", den, den, t1_, ALU.add, K5, K5)
        RECIP(den, den, K5, K5)
        TT("vector", t1_, are, V("lamre"), ALU.mult, K5 + ["vecs"], K5)
        TT("vector", t2_, aim, V("lamim"), ALU.mult, K5 + ["vecs"], K5)
        TT("vector", t1_, t1_, t2_, ALU.add, K5, K5)
        TT("vector", cr, t1_, den, ALU.mult, K5, K5)
        TT("vector", t1_, aim, V("lamre"), ALU.mult, K5 + ["vecs"], K5)
        TT("vector", t2_, are, V("lamim"), ALU.mult, K5 + ["vecs"], K5)
        TT("vector", t1_, t1_, t2_, ALU.subtract, K5, K5)
        TT("vector", ci, t1_, den, ALU.mult, K5, K5)
        TS("vector", nci, ci, -1.0, None, ALU.mult, None, K5, K5)
        for q in range(16):
            qs = slice(q * 128, (q + 1) * 128)
            t_a, ka = tmpf.next()
            TS("vector", t_a[:, 0:128], stg[0][:, qs], cr[:, q:q + 1], None, ALU.mult, None, ["stg0"] + K5, [ka])
            STT(t_a[:, 0:128], stg[1][:, qs], nci[:, q:q + 1], t_a[:, 0:128], ALU.mult, ALU.add,
                ["stg1", ka] + K5, [ka])
            TS("vector", t_a[:, 128:256], stg[1][:, qs], cr[:, q:q + 1], None, ALU.mult, None, ["stg1"] + K5, [ka])
            STT(t_a[:, 128:256], stg[0][:, qs], ci[:, q:q + 1], t_a[:, 128:256], ALU.mult, ALU.add,
                ["stg0", ka] + K5, [ka])
            ps, kp = psum("misc", [7])
            for comp in range(2):
                src = t_a[:, comp * 128:(comp + 1) * 128]
                dst = ps[:, comp * 128:(comp + 1) * 128]
                P.op("tensor", lambda e, dst=dst, src=src: e.transpose(dst, src, ident_f), [ka, "const"], [kp])
            CP("vector", bbT[0][:, q, :], ps[:, 0:128], [kp], ["bbT"])
            CP("vector", bbT[1][:, q, :], ps[:, 128:256], [kp], ["bbT"])
        DMA(stg[2], cpre_d[:, :], "stg2", [], ["stg2"])
        DMA(stg[3], cpim_d[:, :], "stg3", [], ["stg3"])
        CP("gpsimd", cT16[0].rearrange("p q m -> p (q m)"), stg[2], ["stg2"], ["cT16"])
        TS("gpsimd", cT16[1].rearrange("p q m -> p (q m)"), stg[3], -1.0, None, ALU.mult, None, ["stg3"], ["cT16"])
        DMA(stg[0], wglu_d[:, :], "stg0", [], ["stg0"])
        CP("gpsimd", wglu16, stg[0], ["stg0"], ["wglu16"])
        P.barrier()

        def proj(m):
            wb, kb = load_w(win0_d[m, :, :], 1024, scratch=win0b[m, :, :], skey="win0b%d" % m, reload=(curblk["n"] > 0))
            ps, kp = psum("proj", [0, 1])
            for c in range(8):
                MM(ps, wb[:, c * 128:(c + 1) * 128], hT[:, c, :], c == 0, c == 7, [kb, "hT%d" % c], [kp])
            return ps, kp

        def head_ops(n):
            blk = slice(n * TB, (n + 1) * TB)
            ops = []
            st = {}

            def f_rms():
                st["rs"] = rms_rstd(n, "l0")
            ops.append(f_rms)

            def mk_h(c):
                def f():
                    rs, krs = st["rs"]
                    tmp, kt = tmpf.next()
                    STT(tmp, xT[:, c, blk], gsv[0][:, c:c + 1], rs, ALU.mult, ALU.mult,
                        ["x%d_%d" % (c, n), krs, "small"], [kt])
                    ACT(hT[:, c, :], tmp, AF.Identity, [kt, "small"], ["hT%d" % c], bias=shiftv[0][:, c:c + 1])
                return f
            for c in range(8):
                ops.append(mk_h(c))

            def mk_u(cc):
                def f():
                    psu, kpu = proj(24 + cc)
                    ACT(u16[:, cc, :], psu, AF.Identity, [kpu], ["u16_%d" % cc])
                return f
            for cc in range(4):
                ops.append(mk_u(cc))
            return ops

        tail_prev = []
        curblk = {"n": 0}
        ada1_q = ada1_ring_tasks() if (do_l1 and not ADA1_UPFRONT) else []
        for n in range(NB):
            blk = slice(n * TB, (n + 1) * TB)
            if n == 1:
                while tail_prev:
                    tail_prev.pop(0)()
                P.barrier()
            curblk["n"] = n
            hops = head_ops(n)
            while hops or tail_prev:
                if hops:
                    hops.pop(0)()
                if tail_prev:
                    tail_prev.pop(0)()

            cst = {}

            def conv_c1a(c):
                psg, kg = proj(8 + c)
                sig, ksig = tmpf.next()
                ACT(sig, psg, AF.Tanh, [kg], [ksig], scale=0.5)
                psv, kv = proj(c)
                cst[c] = {"sig": sig, "ksig": ksig, "psv": psv, "kv": kv}

            def conv_c1b(c):
                d = cst[c]
                sig, ksig, psv, kv = d["sig"], d["ksig"], d["psv"], d["kv"]
                DMA(dgt[:, 0:2048], dgd[c, :, 0:2048], "dgA", ["dgd%d" % c], ["dgA"])
                DMA(dgt[:, 2048:3968], dgd[c, :, 2048:3968], "dgB", ["dgd%d" % c], ["dgB"])
                glu, kgl = glup.next()
                if n == 0:
                    MEMSET("vector", glu[:, 0:30], 0.0, [kgl])
                else:
                    CP("vector", glu[:, 0:30], halo[:, c, 0:30], ["halo%d" % c], [kgl])
                STT(glu[:, 30:542], sig, 1.0, psv, ALU.add, ALU.mult, [kv, ksig], [kgl])
                if n < NB - 1:
                    CP("vector", halo[:, c, 0:30], glu[:, 512:542], [kgl], ["halo%d" % c])
                d["glu"] = glu; d["kgl"] = kgl

            def conv_c2(c):
                d = cst.pop(c)
                glu, kgl = d["glu"], d["kgl"]
                cps, kcp = psum("conv", [2, 3])
                for k in range(31):
                    MM(cps, dgt[:, k * 128:(k + 1) * 128], glu[:, k:k + 512], k == 0, k == 30,
                       ["dgA" if k < 16 else "dgB", kgl], [kcp])
                ACT(convo[:, c, :], cps, AF.Identity, [kcp, "vecs"], ["convo%d" % c], bias=V("conv_b", c, c + 1))

            pst = {}
            ysta = {}

            def s5_p2a_act(q):
                sn, ksn = plong.next()
                cs, kcs = plong.next()
                DMA(cs, tabd[q, :, 0:512], "d" + kcs, ["tabd%d" % q], [kcs])
                DMA(sn, tabd[q, :, 512:1024], "d" + ksn, ["tabd%d" % q], [ksn])
                pst[q] = {"sn": sn, "ksn": ksn, "cs": cs, "kcs": kcs}

            def s5_p2a_dve(q):
                pass

            def s5_p2a_sin(q):
                pass

            def s5_p1(q):
                cc = q // 4
                bre, kbre = psum("s5b", [4, 5])
                bim, kbim = psum("s5b", [4, 5])
                MM(bre, bbT[0][:, q, :], u16[:, cc, :], True, True, ["bbT", "u16_%d" % cc], [kbre])
                MM(bim, bbT[1][:, q, :], u16[:, cc, :], True, True, ["bbT", "u16_%d" % cc], [kbim])
                pst[q].update({"bre": bre, "kbre": kbre, "bim": bim, "kbim": kbim})

            def s5_p2b(q):
                d = pst[q]
                bre, kbre, bim, kbim = d["bre"], d["kbre"], d["bim"], d["kbim"]
                sn, ksn, cs, kcs = d["sn"], d["ksn"], d["cs"], d["kcs"]
                btr, kbtr = plong.next()
                bti, kbti = plong.next()
                m2, km2 = pshort.next()
                TT("vector", btr, bre, cs, ALU.mult, [kbre, kcs], [kbtr])
                TT("vector", m2, bim, sn, ALU.mult, [kbim, ksn], [km2])
                TT("vector", btr, btr, m2, ALU.add, [kbtr, km2], [kbtr])
                TT("vector", bti, bim, cs, ALU.mult, [kbim, kcs], [kbti])
                TT("vector", m2, bre, sn, ALU.mult, [kbre, ksn, km2], [km2])
                TT("vector", bti, bti, m2, ALU.subtract, [kbti, km2], [kbti])
                rb = bass.AP(s5, 3 * 16 + q, [[256, 128], [0, 512]])
                st_re = small[:, 96 + q:97 + q]
                st_im = small[:, 112 + q:113 + q]
                kst = "st%d" % q
                if n > 0:
                    i_re = small[:, 192 + q:193 + q]
                    i_im = small[:, 208 + q:209 + q]
                    t_a = small[:, 224 + q:225 + q]
                    t_b = small[:, 240 + q:241 + q]
                    TS("vector", t_a, st_re, c512[:, q:q + 1], None, ALU.mult, None, [kst, "e512"], [kst])
                    STT(i_re, st_im, ns512[:, q:q + 1], t_a, ALU.mult, ALU.add, [kst, "e512"], [kst])
                    TS("vector", t_b, st_re, s512[:, q:q + 1], None, ALU.mult, None, [kst, "e512"], [kst])
                    STT(i_im, st_im, c512[:, q:q + 1], t_b, ALU.mult, ALU.add, [kst, "e512"], [kst])
                for (bt, kbt, stv) in ((btr, kbtr, (i_re if n > 0 else None)), (bti, kbti, (i_im if n > 0 else None))):
                    init = 0.0 if n == 0 else stv
                    P.op("vector", lambda e, bt=bt, init=init, rb=rb: e.tensor_tensor_scan(
                        out=bt, data0=rb, data1=bt, initial=init, op0=ALU.mult, op1=ALU.add),
                        [kbt, "s5", kst], [kbt])
                if n < NB - 1:
                    CP("vector", st_re, btr[:, 511:512], [kbtr], [kst])
                    CP("vector", st_im, bti[:, 511:512], [kbti], [kst])
                d.update({"btr": btr, "kbtr": kbtr, "bti": bti, "kbti": kbti})

            def s5_p3(q):
                d = pst[q]
                sn, ksn, cs, kcs = d["sn"], d["ksn"], d["cs"], d["kcs"]
                btr, kbtr, bti, kbti = d["btr"], d["kbtr"], d["bti"], d["kbti"]
                sre, ksre = tmph.next()
                sim, ksim = tmph.next()
                g1, kg1 = pg.next()
                g2, kg2 = pg.next()
                TT(P3ENG, g1, btr, cs, ALU.mult, [kbtr, kcs], [kg1])
                TT(P3ENG, g2, bti, sn, ALU.mult, [kbti, ksn], [kg2])
                TT(P3ENG, sre, g1, g2, ALU.subtract, [kg1, kg2], [ksre])
                TT(P3ENG, g1, btr, sn, ALU.mult, [kbtr, ksn, kg1], [kg1])
                TT(P3ENG, g2, bti, cs, ALU.mult, [kbti, kcs, kg2], [kg2])
                TT(P3ENG, sim, g1, g2, ALU.add, [kg1, kg2], [ksim])
                d.update({"sre": sre, "ksre": ksre, "sim": sim, "ksim": ksim})

            def s5_p4(q):
                cc = q // 4
                d = pst.pop(q)
                sre, ksre, sim, ksim = d["sre"], d["ksre"], d["sim"], d["ksim"]
                if q % 4 == 0:
                    ysta["y"] = psum("s5y", [6])
                y_ps, ky = ysta["y"]
                MM(y_ps, cT16[0][:, q, :], sre, q % 4 == 0, False, ["cT16", ksre], [ky])
                MM(y_ps, cT16[1][:, q, :], sim, False, q % 4 == 3, ["cT16", ksim], [ky])
                if q % 4 == 3:
                    yb, kyb = tmpf.next()
                    STT(yb, u16[:, cc, :], V("ssm_d", cc, cc + 1), y_ps, ALU.mult, ALU.add,
                        ["u16_%d" % cc, "vecs", ky], [kyb])
                    ACT(yg[:, cc, :], yb, AF.Gelu_apprx_tanh, [kyb], ["yg%d" % cc])
                    ACT(yg16[:, cc, :], yg[:, cc, :], AF.Copy, ["yg%d" % cc], ["yg16_%d" % cc])

            lnst = {}

            def ln_stats():
                s_ps, ks_ = psum("stat", [2, 3])
                q_ps, kq_ = psum("stat", [2, 3])
                for c in range(8):
                    MM(s_ps, ones16, convo[:, c, :], c == 0, c == 7, ["convo%d" % c, "c16"], [ks_])
                for c in range(8):
                    sq, ksq = pg.next()
                    ACT(sq, convo[:, c, :], AF.Square, ["convo%d" % c], [ksq])
                    MM(q_ps, ones_f, sq, c == 0, c == 7, [ksq, "const"], [kq_])
                mean, kmean = lnm[:, :], "lnm"
                TS("vector", mean, s_ps, 1.0 / 1024, None, ALU.mult, None, [ks_], [kmean])
                rl, krl = lnr[:, :], "lnr"
                TT("vector", rl, mean, mean, ALU.mult, [kmean], [krl])
                STT(rl, q_ps, 1.0 / 1024, rl, ALU.mult, ALU.subtract, [kq_, krl], [krl])
                TS("vector", rl, rl, EPS, None, ALU.add, None, [krl], [krl])
                ACT(rl, rl, AF.Sqrt, [krl], [krl])
                RECIP(rl, rl, [krl], [krl])

            nst = {}

            def norm_a(c):
                mean, kmean = lnm[:, :], "lnm"
                rl, krl = lnr[:, :], "lnr"
                t1, k1 = tmpf.next()
                TT("vector", t1, convo[:, c, :], mean, ALU.subtract, ["convo%d" % c, kmean], [k1])
                TT("vector", t1, t1, rl, ALU.mult, [k1, krl], [k1])
                ACT(t1, t1, AF.Silu, [k1, "vecs"], [k1], scale=V("ln_g", c, c + 1), bias=V("ln_b", c, c + 1))
                psa, kpa = proj(16 + c)
                sga, ksga = tmpf.next()
                ACT(sga, psa, AF.Silu, [kpa], [ksga])
                nst[c] = (t1, k1, sga, ksga)

            def norm_b(c):
                t1, k1, sga, ksga = nst.pop(c)
                TT("vector", y16[:, c, :], t1, sga, ALU.mult, [k1, ksga], ["y16_%d" % c])

            for k in range(21):
                if n >= 1 and ada1_q:
                    ada1_q.pop(0)()
                if 0 <= k - 2 < 16:
                    s5_p3(k - 2)
                if k < 16:
                    s5_p2a_act(k)
                if 0 <= k - 1 < 16:
                    s5_p2b(k - 1)
                if 0 <= k - 3 < 16:
                    s5_p4(k - 3)
                if k == 11:
                    ln_stats()
                if 0 <= k - 2 < 8:
                    conv_c2(k - 2)
                if k < 16:
                    s5_p1(k)
                    s5_p2a_dve(k)
                    s5_p2a_sin(k)
                if 0 <= k - 1 < 8:
                    conv_c1b(k - 1)
                if k < 8:
                    conv_c1a(k)
                if 8 <= k < 12:
                    mo_ = k - 8
                    psb_, kpb = proj(28 + mo_)
                    ACT(y16[:, 8 + mo_, :], psb_, AF.Silu, [kpb], ["y16_%d" % (8 + mo_)])
                if 0 <= k - 13 < 8:
                    norm_b(k - 13)
                if 0 <= k - 12 < 8:
                    norm_a(k - 12)
            def mk_glu(mo, n=n):
                def f():
                    ps, kp = psum("misc", [7])
                    for k in range(4):
                        MM(ps, wglu16[:, k * 512 + mo * 128:k * 512 + (mo + 1) * 128], yg16[:, k, :], k == 0, k == 3,
                           ["wglu16", "yg16_%d" % k], [kp])
                    sg, ksg = tmpf.next()
                    ACT(sg, ps, AF.Sigmoid, [kp, "vecs"], [ksg], bias=V("b_glu", mo, mo + 1))
                    TT("vector", sg, sg, yg[:, mo, :], ALU.mult, [ksg, "yg%d" % mo], [ksg])
                    ky_ = "y16_%d" % (8 + mo)
                    TT("vector", y16[:, 8 + mo, :], sg, y16[:, 8 + mo, :], ALU.mult, [ksg, ky_], [ky_])
                return f

            def mk_wout(mo, n=n, blk=blk):
                def f():
                    wb, kb = load_w(wout0_d[mo, :, 0:1024], 1024, scratch=wout0b[mo, :, 0:1024],
                                    skey="wout0bA%d" % mo, reload=(n > 0))
                    wb2, kb2 = load_w(wout0_d[mo, :, 1024:1536], 512, scratch=wout0b[mo, :, 1024:1536],
                                      skey="wout0bB%d" % mo, reload=(n > 0))
                    ps, kp = psum("misc", [7])
                    for k in range(8):
                        MM(ps, wb[:, k * 128:(k + 1) * 128], y16[:, k, :], k == 0, False, [kb, "y16_%d" % k], [kp])
                    for k in range(8, 12):
                        MM(ps, wb2[:, (k - 8) * 128:(k - 7) * 128], y16[:, k, :], False, k == 11,
                           [kb2, "y16_%d" % k], [kp])
                    xk = "x%d_%d" % (mo, n)
                    STT(xT[:, mo, blk], ps, gatev[0][:, mo:mo + 1], xT[:, mo, blk], ALU.mult, ALU.add,
                        [kp, "small", xk], [xk])
                return f
            tail_prev = [mk_glu(mo) for mo in range(4)] + [mk_wout(mo) for mo in range(8)]
        for f in tail_prev:
            f()
        while ada1_q:
            ada1_q.pop(0)()
        P.barrier()

    if do_l1:
        hT1 = arena16[:, 0:16384].rearrange("p (c t) -> p c t", c=8)
        def mkset(base16, o):
            return {
                "qT": base16[:, o:o + 2048],
                "kpad": [base16[:, o + 2048:o + 4096], base16[:, o + 4096:o + 6144]],
                "vpad": [base16[:, o + 6144:o + 8192].rearrange("p (t m) -> p t m", t=16),
                         base16[:, o + 8192:o + 10240].rearrange("p (t m) -> p t m", t=16)],
                "gsl": base16[:, o + 10240:o + 12288],
                "opair": base16[:, o + 12288:o + 14336],
            }
        sets = [mkset(arena16, 16384), mkset(parena16, 0)]
        ssum = [[arena16[:, 30720:31232], arena16[:, 31232:31744]],
                [arena16[:, 31744:32256], arena16[:, 32256:32768]]]
        MEMSET("gpsimd", arena16[:, 16384 + 2048:16384 + 10240], 0.0, ["kpad0_0", "kpad1_0", "vpad0_0", "vpad1_0"])
        MEMSET("gpsimd", parena16[:, 2048:10240], 0.0, ["kpad0_1", "kpad1_1", "vpad0_1", "vpad1_1"])
        for n in range(NB):
            blk = slice(n * TB, (n + 1) * TB)
            rs, krs = rms_rstd(n, "l1")
            for c in range(8):
                tmp, kt = tmpf.next()
                STT(tmp, xT[:, c, blk], gsv[1][:, c:c + 1], rs, ALU.mult, ALU.mult,
                    ["x%d_%d" % (c, n), krs, "small"], [kt])
                ACT(hT1[:, c, blk], tmp, AF.Identity, [kt, "small"], ["h1_%d_%d" % (c, n)],
                    bias=shiftv[1][:, c:c + 1])

        ZB = [0, 1, 2, 3]
        OB = [4, 5]

        def proj_tasks(hp):
            S_ = sets[hp % 2]
            sx = hp % 2
            tasks = []
            wref = {}

            def mk_load(sec):
                def t():
                    wref[sec] = load_w(win1_d[sec * 8 + hp, :, :], 1024, "vector")
                return t

            def mk_grp(sec, n):
                def t():
                    wb, kb = wref[sec]
                    ps, kp = psum("proj1", [6, 7])
                    for c in range(8):
                        MM(ps, wb[:, c * 128:(c + 1) * 128], hT1[:, c, n * TB:(n + 1) * TB], c == 0, c == 7,
                           [kb, "h1_%d_%d" % (c, n)], [kp])
                    cols = slice(n * TB, (n + 1) * TB)
                    if sec == 0:
                        TS("vector", S_["qT"][:, cols], ps, 0.125, None, ALU.mult, None, [kp], ["qT_%d" % sx])
                    elif sec == 1:
                        CP("vector", S_["kpad"][0][0:64, cols], ps[0:64, :], [kp], ["kpad0_%d" % sx])
                        CP("vector", S_["kpad"][1][64:128, cols], ps[64:128, :], [kp], ["kpad1_%d" % sx])
                    else:
                        ACT(S_["gsl"][:, cols], ps, AF.Silu, [kp], ["gsl_%d" % sx])
                return t

            def mk_v(tt):
                def t():
                    wb, kb = wref[2]
                    ps, kp = psum("proj1", [6, 7])
                    n = tt // 4
                    for c in range(8):
                        MM(ps[:, 0:128], hT1[:, c, tt * 128:(tt + 1) * 128], wb[:, c * 128:(c + 1) * 128],
                           c == 0, c == 7, [kb, "h1_%d_%d" % (c, n)], [kp])
                    CP("vector", S_["vpad"][0][:, tt, 0:64], ps[:, 0:64], [kp], ["vpad0_%d" % sx])
                    CP("vector", S_["vpad"][1][:, tt, 64:128], ps[:, 64:128], [kp], ["vpad1_%d" % sx])
                return t

            for sec in (0, 1, 3):
                tasks.append(mk_load(sec))
                for n in range(NB):
                    tasks.append(mk_grp(sec, n))
            tasks.append(mk_load(2))
            for tt in range(16):
                tasks.append(mk_v(tt))
            return tasks

        def wout_tasks(hp):
            S_ = sets[hp % 2]
            sx = hp % 2
            tasks = []
            wref = {}

            def ld():
                wref[0] = load_w(wout1_d[hp, :, :], 1024, "vector")
            tasks.append(ld)

            def mk(mo, n):
                def t():
                    wb, kb = wref[0]
                    blk = slice(n * TB, (n + 1) * TB)
                    ps, kp = psum("proj1", [6, 7])
                    MM(ps, wb[:, mo * 128:(mo + 1) * 128], S_["opair"][:, blk], True, True, [kb, "opair_%d" % sx], [kp])
                    xk = "x%d_%d" % (mo, n)
                    STT(xT[:, mo, blk], ps, gatev[1][:, mo:mo + 1], xT[:, mo, blk], ALU.mult, ALU.add,
                        [kp, "small", xk], [xk])
                return t
            for mo in range(8):
                for n in range(NB):
                    tasks.append(mk(mo, n))
            return tasks

        items = []
        for hp in range(8):
            for g4 in range(4):
                nblk = 4 * g4 + 4
                for b in range(nblk - 1, -1, -1):
                    for hi in range(2):
                        items.append((hp, hi, g4, b, nblk))
        info = {}
        hstate = {0: [0, 0], 1: [0, 0]}
        ostate = {}

        def stA1(it):
            hp, hi, g4, b, nblk = it
            S_ = sets[hp % 2]; sx = hp % 2
            T0 = g4 * TB
            c0 = max(0, b * 128 - T0)
            Z, kz = psum("Z", ZB)
            MM(Z[:, c0:TB], S_["kpad"][hi][:, b * 128:(b + 1) * 128], S_["qT"][:, T0 + c0:T0 + TB], True, True,
               ["kpad%d_%d" % (hi, sx), "qT_%d" % sx], [kz])
            info[it] = {"Z": Z, "kz": kz, "c0": c0}

        def stA2(it):
            hp, hi, g4, b, nblk = it
            d = info[it]
            Z, kz, c0 = d["Z"], d["kz"], d["c0"]
            first = (b == nblk - 1)
            e_, ke = tmpf.next()
            ACT(e_[:, c0:TB], Z[:, c0:TB], AF.Exp, [kz], [ke])
            sp, ksp = tmph.next()
            ACT(sp[:, c0:TB], e_[:, c0:TB], AF.Ln, [ke], [ksp], bias=1.0)
            if b >= 4 * g4:
                TT("vector", sp[:, c0:c0 + 128], sp[:, c0:c0 + 128], tri01_16, ALU.mult, [ksp, "c16"], [ksp])
            if first:
                MEMSET("gpsimd", ssum[hi][0], 0.0, ["ssum%d_0" % hi])
                MEMSET("gpsimd", ssum[hi][1], 0.0, ["ssum%d_1" % hi])
                hstate[hi][0] = 0
            cur = hstate[hi][0]
            d["sp"] = sp; d["ksp"] = ksp; d["cur"] = cur
            if b > 0:
                nxt = 1 - cur
                TT("vector", ssum[hi][nxt][:, c0:TB], ssum[hi][cur][:, c0:TB], sp[:, c0:TB], ALU.add,
                   ["ssum%d_%d" % (hi, cur), ksp], ["ssum%d_%d" % (hi, nxt)])
                hstate[hi][0] = nxt

        def stB1(it):
            hp, hi, g4, b, nblk = it
            d = info[it]
            Z, kz, c0, sp, ksp, cur = d["Z"], d["kz"], d["c0"], d["sp"], d["ksp"], d["cur"]
            first = (b == nblk - 1)
            diag = (b >= 4 * g4)
            kc_ = "ssum%d_%d" % (hi, cur)
            MM(Z[:, c0:TB], nti16, sp[:, c0:TB], False, (first and not diag), [ksp, "c16"], [kz], skip=True)
            if not first:
                MM(Z[:, c0:TB], negones16, ssum[hi][cur][:, c0:TB], False, not diag, [kc_, "c16"], [kz], skip=True)
            if diag:
                MM(Z[:, c0:c0 + 128], ident16, negbig16, False, True, ["c16"], [kz], skip=True)

        def stB2(it):
            d = info[it]
            Z, kz, c0 = d["Z"], d["kz"], d["c0"]
            w16, kw = tmph.next()
            ACT(w16[:, c0:TB], Z[:, c0:TB], AF.Exp, [kz], [kw])
            d["w16"] = w16; d["kw"] = kw

        def stB3(it):
            hp, hi, g4, b, nblk = it
            S_ = sets[hp % 2]; sx = hp % 2
            T0 = g4 * TB
            d = info.pop(it)
            c0, w16, kw = d["c0"], d["w16"], d["kw"]
            if b == nblk - 1:
                O, ko = psum("O%d" % hi, [OB[hi]])
                ostate[hi] = (O, ko)
                MM(O, zeros16, S_["qT"][:, 0:TB], True, False, ["c16", "qT_%d" % sx], [ko])
            O, ko = ostate[hi]
            MM(O[:, c0:TB], S_["vpad"][hi][:, b, :], w16[:, c0:TB], False, b == 0, [kw, "vpad%d_%d" % (hi, sx)], [ko])
            if b == 0:
                rows = slice(hi * 64, (hi + 1) * 64)
                TT("vector", S_["opair"][rows, T0:T0 + TB], O[rows, :], S_["gsl"][rows, T0:T0 + TB], ALU.mult,
                   [ko, "gsl_%d" % sx], ["opair_%d" % sx])

        for t in proj_tasks(0):
            t()
        NI = len(items)
        PER = NI // 8
        queue = []
        for s_ in range(NI + 4):
            p_ = s_ // PER
            r_ = s_ % PER
            if s_ < NI and r_ == 5 and p_ + 1 < 8:
                queue.extend(proj_tasks(p_ + 1))
            if s_ < NI and r_ == 6 and p_ >= 1:
                queue = wout_tasks(p_ - 1) + queue
            if 0 <= s_ - 2 < NI:
                stB1(items[s_ - 2])
            if 0 <= s_ - 4 < NI:
                stB3(items[s_ - 4])
            if s_ < NI:
                if r_ == 0:
                    while queue:
                        queue.pop(0)()
                stA1(items[s_])
            if 0 <= s_ - 1 < NI:
                stA2(items[s_ - 1])
            if 0 <= s_ - 3 < NI:
                stB2(items[s_ - 3])
            if queue:
                left = PER - r_ - 2
                ntask = len(queue) if left <= 0 else -(-len(queue) // left)
                for _ in range(min(ntask, len(queue))):
                    queue.pop(0)()
        while queue:
            queue.pop(0)()
        for t in wout_tasks(7):
            t()
        P.barrier()

    diagG = arena[:, 0:1024].rearrange("p (c m) -> p c m", c=8)
    for c in range(8):
        TS("vector", diagG[:, c, :], ident_f, V("final_g", c, c + 1), None, ALU.mult, None,
           ["const", "vecs"], ["diagG"])
    obuf = [arena[:, 1024:2048], arena[:, 2048:3072]]
    frs = None
    for tt in range(16):
        n = tt // 4
        tsl = slice(tt * 128, (tt + 1) * 128)
        if tt % 4 == 0:
            frs = rms_rstd(n, "fin")
        rs, krs = frs
        ps2, k2 = psum("fin2", [4, 5])
        MM(ps2[:, 0:1], rs[:, (tt % 4) * 128:(tt % 4 + 1) * 128], ident_f[:, 0:1], True, True, [krs, "const"], [k2])
        rt, krt = tmpf.next()
        CP("vector", rt[:, 0:1], ps2[:, 0:1], [k2], [krt])
        ob = obuf[tt % 2]
        kob = "obuf%d" % (tt % 2)
        for half in range(2):
            ps, kp = psum("fin", [0, 1])
            for j in range(4):
                c = half * 4 + j
                MM(ps[:, j * 128:(j + 1) * 128], xT[:, c, tsl], diagG[:, c, :], True, True,
                   ["x%d_%d" % (c, n), "diagG"], [kp])
            TS("vector", ob[:, half * 512:(half + 1) * 512], ps, rt[:, 0:1], None, ALU.mult, None,
               [kp, krt], [kob])
        DMA(out_d[tsl, :], ob, kob + "d", [kob], [kob])
    P.final_wait(["obuf0d", "obuf1d"])
    P.emit()
    return nc


def _host_inputs(inp, b):
    f = np.float32

    def colvec(v):
        v = np.asarray(v, f)
        return v.reshape(-1, 128).T

    vecs = np.zeros((128, NV), f)

    def put(name, arr):
        o, w = _VEC[name]
        assert arr.shape == (128, w), (name, arr.shape)
        vecs[:, o:o + w] = arr
    put("norm_g0", colvec(inp["l0_norm_g"]))
    put("b_ada0", colvec(inp["l0_b_ada"]))
    put("conv_b", colvec(inp["l0_conv_b"]))
    put("ln_g", colvec(inp["l0_conv_ln_g"]))
    put("ln_b", colvec(inp["l0_conv_ln_b"]))
    put("ssm_d", colvec(inp["l0_ssm_d"]))
    put("b_glu", colvec(inp["l0_ssm_b_glu"]))
    put("norm_g1", colvec(inp["l1_norm_g"]))
    put("b_ada1", colvec(inp["l1_b_ada"]))
    put("final_g", colvec(inp["final_norm_g"]))
    put("cT", colvec(inp["c"][b]))
    cw = np.asarray(inp["l0_conv_w"], f)
    put("conv_w", cw.T.reshape(8, 128, 31).transpose(1, 0, 2).reshape(128, 248))

    def pairlay(a):
        return np.asarray(a, f).reshape(16, 2, 64).transpose(1, 2, 0).reshape(128, 16)
    put("lamre", pairlay(inp["l0_ssm_lam_re"]))
    put("lamim", pairlay(inp["l0_ssm_lam_im"]))
    ld = np.asarray(inp["l0_ssm_log_dt"], f)
    put("logdt", pairlay(np.repeat(ld[:, None], 64, axis=1)))

    def kchunks(w, ncol):
        w = np.asarray(w, f)
        K, N = w.shape
        return np.ascontiguousarray(
            w.reshape(K // 128, 128, N // ncol, ncol).transpose(2, 1, 0, 3).reshape(N // ncol, 128, (K // 128) * ncol))

    def bpad(bmat):
        bmat = np.asarray(bmat, f)
        o = np.zeros((2, 64, 16, 128), f)
        for g in range(32):
            q, h = g // 2, g % 2
            o[h, :, q, (g % 8) * 16:(g % 8) * 16 + 16] = bmat[g]
        return o.reshape(128, 2048)

    def cpad(cmat):
        return bpad(np.asarray(cmat, f).transpose(0, 2, 1))

    d = {}
    xb = np.asarray(inp["x"][b], f)
    d["xT"] = np.ascontiguousarray(xb.reshape(L, 8, 128).transpose(2, 1, 0))
    d["vecs"] = vecs
    d["w_ada0"] = np.ascontiguousarray(np.asarray(inp["l0_w_ada"], f).reshape(8, 128, 3072))
    d["w_ada1"] = np.ascontiguousarray(np.asarray(inp["l1_w_ada"], f).reshape(8, 128, 3072))
    d["w_in0"] = kchunks(inp["l0_w_in"], 128)
    d["w_out0"] = kchunks(inp["l0_w_out"], 128)
    d["w_in1"] = kchunks(inp["l1_w_in"], 128)
    w1 = np.asarray(inp["l1_w_out"], f)
    d["w_out1"] = np.ascontiguousarray(w1.reshape(8, 128, 1024))
    wg = np.asarray(inp["l0_ssm_w_glu"], f)
    d["w_glu"] = np.ascontiguousarray(wg.reshape(4, 128, 512).transpose(1, 0, 2).reshape(128, 2048))
    d["bp_re"] = bpad(inp["l0_ssm_b_re"])
    d["bp_im"] = bpad(inp["l0_ssm_b_im"])
    d["cp_re"] = cpad(inp["l0_ssm_c_re"])
    d["cp_im"] = cpad(inp["l0_ssm_c_im"])
    cst = np.zeros((128, 5 * 128 + 512), f)
    i = np.arange(128)
    cst[:, 0:128] = np.eye(128, dtype=f)
    cst[:, 128:256] = 1.0
    cst[:, 256:384] = (i[None, :] > i[:, None]).astype(f)
    cst[:, 384:512] = -(i[:, None] >= i[None, :]).astype(f)
    cst[:, 512:640] = np.where(i[None, :] <= i[:, None], -30000.0, 0.0).astype(f)
    cst[:, 640:1152] = np.arange(512, dtype=f)[None, :]
    d["consts"] = cst
    return d


_NC_CACHE = {}


def kernel(**inputs):
    inp = {k: np.asarray(v) for k, v in inputs.items()}
    if "nc" not in _NC_CACHE:
        _NC_CACHE["nc"] = build_program()
    nc = _NC_CACHE["nc"]
    in_maps = [_host_inputs(inp, b) for b in range(8)]
    res = run_bass_kernel_spmd(nc, in_maps, core_ids=list(range(8)))
    out = np.stack([np.asarray(r["out"], np.float32).reshape(L, D) for r in res.results], axis=0)
    return out
```

```python
import math
import numpy as np
import concourse.bass as bass
import concourse.mybir as mybir
from concourse.bass_utils import run_bass_kernel_spmd

F32 = mybir.dt.float32
BF16 = mybir.dt.bfloat16
AF = mybir.ActivationFunctionType
ALU = mybir.AluOpType

L = 2048
D = 1024
NB = 4
TB = 512
EPS = 1e-6
PI = math.pi
SAME_ENG_SYNC = True
N_FILL = 0
ADA1_UPFRONT = False
P3ENG = "vector"

_VEC = {}
_off = 0
for _n, _w in [("norm_g0", 8), ("b_ada0", 24), ("conv_b", 8), ("ln_g", 8), ("ln_b", 8), ("ssm_d", 4),
               ("b_glu", 4), ("norm_g1", 8), ("b_ada1", 24), ("final_g", 8), ("cT", 8), ("conv_w", 248),
               ("lamre", 16), ("lamim", 16), ("logdt", 16)]:
    _VEC[_n] = (_off, _w)
    _off += _w
NV = _off


class _Op:
    __slots__ = ("eng", "fn", "deps", "signal", "val", "tag", "is_dma")


class Prog:
    ENG = ("sync", "tensor", "vector", "scalar", "gpsimd")

    def __init__(self, nc):
        self.nc = nc
        self.ops = {e: [] for e in self.ENG}
        self.lastw = {}
        self.readers = {}
        self.pend = {e: [] for e in self.ENG}
        self.dma_count = {}
        self.last_dma = {}

    def _add(self, op, r, w):
        deps = set()
        for k in r:
            o = self.lastw.get(k)
            if o is not None:
                deps.add(o)
        for k in w:
            o = self.lastw.get(k)
            if o is not None:
                deps.add(o)
            for o in self.readers.get(k, ()):
                deps.add(o)
        for o in self.pend[op.eng]:
            deps.add(o)
        self.pend[op.eng] = []
        deps.discard(op)
        op.deps = deps
        for d in deps:
            d.signal = True
        for k in r:
            self.readers.setdefault(k, []).append(op)
        for k in w:
            self.lastw[k] = op
            self.readers[k] = []
        self.ops[op.eng].append(op)

    def op(self, eng, fn, r=(), w=()):
        o = _Op()
        o.eng = eng; o.fn = fn; o.signal = False; o.val = 0; o.tag = None; o.is_dma = False
        self._add(o, r, w)
        return o

    def dma(self, fn, tag, r=(), w=(), q="sync"):
        o = _Op()
        o.eng = q; o.fn = fn; o.signal = True; o.tag = tag; o.is_dma = True
        self.dma_count[tag] = self.dma_count.get(tag, 0) + 1
        o.val = 16 * self.dma_count[tag]
        self.last_dma[tag] = o
        self._add(o, r, w)
        return o

    def barrier(self):
        evs = []
        for e in self.ENG:
            if self.ops[e]:
                o = self.ops[e][-1]
                o.signal = True
                evs.append(o)
        for t, o in self.last_dma.items():
            evs.append(o)
        for e in self.ENG:
            self.pend[e] = list(evs)

    def final_wait(self, tags):
        o = _Op()
        o.eng = "sync"; o.fn = (lambda e: None); o.signal = False; o.val = 0; o.tag = None; o.is_dma = False
        o.deps = set(self.last_dma[t] for t in tags)
        self.ops["sync"].append(o)

    def emit(self):
        nc = self.nc
        for e in self.ENG:
            cnt = 0
            for o in self.ops[e]:
                if o.is_dma:
                    continue
                if o.signal:
                    cnt += 1
                    o.val = cnt
        sems = {e: nc.alloc_semaphore("sem_" + e) for e in self.ENG}
        dsems = {t: nc.alloc_semaphore("dsem_" + t) for t in self.dma_count}
        ops = self.ops
        with nc.Block() as block:
            for e in self.ENG:
                def body(eng, e=e):
                    waited = {}
                    for o in ops[e]:
                        need = {}
                        for d in o.deps:
                            if d.is_dma:
                                nm = "d_" + d.tag; sem = dsems[d.tag]
                            else:
                                if d.eng == e and (e == "tensor" or e == "sync" or not SAME_ENG_SYNC):
                                    continue
                                nm = "e_" + d.eng; sem = sems[d.eng]
                            if d.val > need.get(nm, (None, 0))[1]:
                                need[nm] = (sem, d.val)
                        for nm, (sem, val) in need.items():
                            if waited.get(nm, 0) >= val:
                                continue
                            eng.wait_ge(sem, val)
                            waited[nm] = val
                        ins = o.fn(eng)
                        if ins is None:
                            continue
                        if o.is_dma:
                            ins.then_inc(dsems[o.tag], 16)
                        elif o.signal:
                            ins.then_inc(sems[e], 1)
                getattr(block, e)(body)


class Pool:
    def __init__(self, nc, name, n, shape, dtype, tiles=None):
        if tiles is None:
            self.t = [nc.alloc_sbuf_tensor("%s%d" % (name, i), list(shape), dtype)[:, :] for i in range(n)]
        else:
            self.t = list(tiles)
        self.k = ["%s%d" % (name, i) for i in range(len(self.t))]
        self.i = 0

    def next(self):
        j = self.i % len(self.t)
        self.i += 1
        return self.t[j], self.k[j]


def build_program(do_l0=True, do_l1=True):
    nc = bass.Bass("TRN2", target_bir_lowering=False)
    P = Prog(nc)

    def din(name, shape):
        return nc.dram_tensor(name, list(shape), F32, kind="ExternalInput")

    xT_d = din("xT", [128, 8, L])
    vecs_d = din("vecs", [128, NV])
    wada_d = [din("w_ada0", [8, 128, 3072]), din("w_ada1", [8, 128, 3072])]
    win0_d = din("w_in0", [32, 128, 1024])
    wout0_d = din("w_out0", [8, 128, 1536])
    win1_d = din("w_in1", [32, 128, 1024])
    wout1_d = din("w_out1", [8, 128, 1024])
    wglu_d = din("w_glu", [128, 2048])
    bpre_d = din("bp_re", [128, 2048])
    bpim_d = din("bp_im", [128, 2048])
    cpre_d = din("cp_re", [128, 2048])
    cpim_d = din("cp_im", [128, 2048])
    consts_d = din("consts", [128, 5 * 128 + 512])
    out_d = nc.dram_tensor("out", [L, D], F32, kind="ExternalOutput")
    dgd = nc.dram_tensor("dgd", [8, 128, 31 * 128], BF16)

    sb = nc.alloc_sbuf_tensor
    xT = sb("xT_sb", [128, 8, L], F32)
    vecs = sb("vecs_sb", [128, NV], F32)
    consts = sb("consts_sb", [128, 2 * 128 + 512], F32)
    ident_f = consts[:, 0:128]
    ones_f = consts[:, 128:256]
    iota_f = consts[:, 256:768]
    c16 = sb("c16", [128, 8 * 128], BF16)
    ident16 = c16[:, 0:128]
    negones16 = c16[:, 128:256]
    tri01_16 = c16[:, 256:384]
    nti16 = c16[:, 384:512]
    negbig16 = c16[:, 512:640]
    zeros16 = c16[:, 640:768]
    ones16 = c16[:, 768:896]
    halfid16 = c16[:, 896:1024]
    small = sb("small", [128, 256], F32)
    rsd = sb("rsd", [128, 512], F32)
    lnm = sb("lnm", [128, 512], F32)
    lnr = sb("lnr", [128, 512], F32)

    def V(name, j0=0, j1=None):
        o, w = _VEC[name]
        if j1 is None:
            j1 = w
        return vecs[:, o + j0:o + j1]

    arena = sb("arena", [128, 16640], F32)
    arena16 = arena.bitcast(BF16)

    tmpf = Pool(nc, "tf", 4, [128, 512], F32)
    parena = sb("parena", [128, 9728], F32)
    parena16 = parena.bitcast(BF16)
    plong = Pool(nc, "pl", 10, None, None, tiles=[parena[:, i * 512:(i + 1) * 512] for i in range(10)])
    pshort = Pool(nc, "ps_", 2, None, None, tiles=[parena[:, 5120 + i * 512:5120 + (i + 1) * 512] for i in range(2)])
    pg = Pool(nc, "pg", 2, None, None, tiles=[parena[:, 6144 + i * 512:6144 + (i + 1) * 512] for i in range(2)])
    tmph = Pool(nc, "th", 6, [128, 512], BF16)
    wstT = sb("wstT", [128, 2048], F32)
    wst = Pool(nc, "wst", 2, None, None, tiles=[wstT[:, 0:1024], wstT[:, 1024:2048]])
    wbfT = [sb("wbfT%d" % i, [128, 1024], BF16) for i in range(2)]
    wbf = Pool(nc, "wbf", 2, None, None, tiles=[t_[:, :] for t_ in wbfT])
    wstT16 = wstT.bitcast(BF16)
    wring_f32 = [t_.bitcast(F32)[:, 0:512] for t_ in wbfT] + [wstT[:, i * 512:(i + 1) * 512] for i in range(4)]
    wring = Pool(nc, "wring", 6, None, None, tiles=wbf.t + [wstT16[:, i * 1024:(i + 1) * 1024] for i in range(4)])
    wring.k = list(wbf.k) + ["wrx0", "wrx1", "wrx2", "wrx3"]
    win0b = nc.dram_tensor("win0b", [32, 128, 1024], BF16)
    wout0b = nc.dram_tensor("wout0b", [8, 128, 1536], BF16)
    dgt = parena16[:, 14336:14336 + 3968]
    glup = Pool(nc, "glu", 2, None, None, tiles=[parena16[:, 18304 + i * 544:18304 + (i + 1) * 544] for i in range(2)])

    psb = [nc.alloc_psum_tensor("psb%d" % i, [128, 512], F32) for i in range(8)]
    ps_cnt = {}

    def psum(role, banks):
        i = ps_cnt.get(role, 0)
        ps_cnt[role] = i + 1
        b = banks[i % len(banks)]
        return psb[b][:, :], "ps%d" % b

    def ACT(out, in_, func, r, w, **kw):
        P.op("scalar", lambda e: e.activation(out=out, in_=in_, func=func, **kw), r, w)

    def MM(out, lhsT, rhs, start, stop, r, w, skip=False):
        if skip:
            P.op("tensor", lambda e: e.matmul(out, lhsT, rhs, start=start, stop=stop, skip_group_check=True), r, w)
        else:
            P.op("tensor", lambda e: e.matmul(out, lhsT, rhs, start=start, stop=stop), r, w)

    def TT(eng, out, in0, in1, op, r, w):
        P.op(eng, lambda e: e.tensor_tensor(out=out, in0=in0, in1=in1, op=op), r, w)

    def TS(eng, out, in0, s1, s2, op0, op1, r, w):
        if op1 is None:
            P.op(eng, lambda e: e.tensor_scalar(out=out, in0=in0, scalar1=s1, scalar2=None, op0=op0), r, w)
        else:
            P.op(eng, lambda e: e.tensor_scalar(out=out, in0=in0, scalar1=s1, scalar2=s2, op0=op0, op1=op1), r, w)

    def STT(out, in0, scalar, in1, op0, op1, r, w):
        P.op("vector", lambda e: e.scalar_tensor_tensor(out=out, in0=in0, scalar=scalar, in1=in1,
                                                        op0=op0, op1=op1), r, w)

    def CP(eng, out, in_, r, w):
        P.op(eng, lambda e: e.tensor_copy(out=out, in_=in_), r, w)

    def MEMSET(eng, ap, val, w):
        P.op(eng, lambda e: e.memset(ap, val), (), w)

    def DMA(out, in_, tag, r, w, q="sync"):
        P.dma(lambda e: e.dma_start(out=out, in_=in_), tag, r, w, q=q)

    def RECIP(out, in_, r, w):
        P.op("vector", lambda e: e.reciprocal(out=out, in_=in_), r, w)

    MAGIC = 12582912.0
    PI_LO = 3.1415925
    CW1 = 6.28125
    CW2 = 2.0 * PI - 6.28125

    def sincos(x, sn, cs, kx, ksn, kcs):
        ACT(cs, x, AF.Identity, [kx], [kcs], scale=1.0 / (2.0 * PI), bias=MAGIC)
        ACT(cs, cs, AF.Identity, [kcs], [kcs], bias=-MAGIC)
        STT(sn, cs, -CW1, x, ALU.mult, ALU.add, [kcs, kx], [ksn])
        STT(sn, cs, -CW2, sn, ALU.mult, ALU.add, [kcs, ksn], [ksn])
        TS("vector", sn, sn, -PI_LO, PI_LO, ALU.max, ALU.min, [ksn], [ksn])
        STT(cs, sn, -1.0, sn, ALU.mult, ALU.max, [ksn], [kcs])
        ACT(cs, cs, AF.Sin, [kcs, "small"], [kcs], scale=-1.0, bias=halfpi)
        ACT(sn, sn, AF.Sin, [ksn], [ksn])

    DMA(vecs[:, :], vecs_d[:, :], "vecs", [], ["vecs"])
    DMA(consts[:, 0:256], consts_d[:, 0:256], "consts", [], ["const"])
    DMA(consts[:, 256:768], consts_d[:, 640:1152], "consts", [], ["const"])
    cstage = arena[:, 0:384]
    DMA(cstage, consts_d[:, 256:640], "cstage", [], ["cstage"])
    tri01_f = cstage[:, 0:128]
    nti_f = cstage[:, 128:256]
    negbig_f = cstage[:, 256:384]
    for c in range(8):
        DMA(xT[:, c, :], xT_d[:, c, :], "xload%d" % c, [], ["x%d_%d" % (c, n) for n in range(NB)])
    CP("gpsimd", ident16, ident_f, ["const"], ["c16"])
    TS("gpsimd", negones16, ones_f, -1.0, None, ALU.mult, None, ["const"], ["c16"])
    CP("gpsimd", tri01_16, tri01_f, ["cstage"], ["c16"])
    CP("gpsimd", nti16, nti_f, ["cstage"], ["c16"])
    CP("gpsimd", negbig16, negbig_f, ["cstage"], ["c16"])
    MEMSET("gpsimd", zeros16, 0.0, ["c16"])
    CP("gpsimd", ones16, ones_f, ["const"], ["c16"])
    TS("gpsimd", halfid16, ident_f, 0.5, None, ALU.mult, None, ["const"], ["c16"])
    MEMSET("gpsimd", small[:, :], 0.0, ["small", "e512"])
    sc = small[:, 0:8]
    mods = [small[:, 8:32], small[:, 32:56]]
    gsv = [small[:, 56:64], small[:, 64:72]]
    negpi = small[:, 72:73]
    onec = small[:, 73:74]
    MEMSET("gpsimd", negpi, -PI, ["small"])
    MEMSET("gpsimd", onec, 1.0, ["small"])
    halfpi = small[:, 74:75]
    MEMSET("gpsimd", halfpi, PI / 2.0, ["small"])
    ACT(sc, V("cT"), AF.Silu, ["vecs", "small"], ["small"])

    s5 = sb("s5", [128, 16 * 16], F32)

    def S(i):
        return s5[:, i * 16:(i + 1) * 16]
    dt_, xr, th, rr, asn, acs, sn_, cs_, are, aim, den, cr, ci, t1_, t2_, nci = [S(i) for i in range(16)]
    K5 = ["s5"]
    c512 = small[:, 144:160]
    s512 = small[:, 160:176]
    ns512 = small[:, 176:192]
    tabd = nc.dram_tensor("tabd", [16, 128, 1024], F32)
    if do_l0:
        ACT(dt_, V("logdt"), AF.Exp, ["vecs"], K5)
        TT("vector", xr, V("lamre"), dt_, ALU.mult, K5 + ["vecs"], K5)
        TT("vector", th, V("lamim"), dt_, ALU.mult, K5 + ["vecs"], K5)
        ACT(rr, xr, AF.Exp, K5, K5)
        TS("vector", asn, th, 512.0, None, ALU.mult, None, K5, K5)
        sincos(asn, s512, c512, "s5", "e512", "e512")
        TS("vector", ns512, s512, -1.0, None, ALU.mult, None, ["e512"], ["e512"])

    def gen_table(q):
        if True:
            base, kb_ = pshort.next()
            sn, ksn = plong.next()
            cs, kcs = plong.next()
            ACT(base, iota_f, AF.Identity, ["const", "s5"], [kb_], scale=th[:, q:q + 1])
            sincos(base, sn, cs, kb_, ksn, kcs)
            DMA(tabd[q, :, 0:512], cs, "to" + kcs, [kcs], ["tabd%d" % q], q="gpsimd")
            DMA(tabd[q, :, 512:1024], sn, "to" + ksn, [ksn], ["tabd%d" % q], q="gpsimd")

    sc16 = sb("sc16", [128, 8], BF16)[:, :]
    CP("vector", sc16, sc, ["small"], ["sc16"])
    for l in range(2 if ADA1_UPFRONT else 1):
        for kc in range(8):
            if do_l0:
                gen_table(l * 8 + kc)
            ps, kp = psum("ada", [2])
            for g in range(6):
                st, ks = tmpf.next()
                DMA(st, wada_d[l][kc, :, g * 512:(g + 1) * 512], "d" + ks, [], [ks])
                w16_, k16 = tmph.next()
                CP("vector", w16_, st, [ks], [k16])
                for j in range(4):
                    MM(ps[:, g * 4 + j:g * 4 + j + 1], w16_[:, j * 128:(j + 1) * 128], sc16[:, kc:kc + 1], True, True,
                       [k16, "sc16"], [kp])
            if kc == 0:
                CP("vector", mods[l], ps[:, 0:24], [kp], ["small"])
            else:
                TT("vector", mods[l], mods[l], ps[:, 0:24], ALU.add, [kp, "small"], ["small"])
        bname = "b_ada%d" % l
        TT("vector", mods[l], mods[l], V(bname), ALU.add, ["small", "vecs"], ["small"])
        TS("vector", gsv[l], mods[l][:, 8:16], 1.0, None, ALU.add, None, ["small"], ["small"])
        TT("vector", gsv[l], gsv[l], V("norm_g%d" % l), ALU.mult, ["small", "vecs"], ["small"])
    if do_l0 and not ADA1_UPFRONT:
        for q_ in range(8, 16):
            gen_table(q_)

    def ada1_ring_tasks():
        tasks = []
        psst = {}

        def mk(kc, g):
            def f():
                i_ = wring.i % 6
                _, ks = wring.next()
                st = wring_f32[i_]
                DMA(st, wada_d[1][kc, :, g * 512:(g + 1) * 512], "d" + ks, [], [ks])
                j_ = wring.i % 6
                w16t, k16 = wring.next()
                w16_ = w16t[:, 0:512]
                CP("vector", w16_, st, [ks], [k16])
                ps, kp = psum("misc", [7])
                for j in range(4):
                    MM(ps[:, j:j + 1], w16_[:, j * 128:(j + 1) * 128], sc16[:, kc:kc + 1], True, True,
                       [k16, "sc16"], [kp])
                dst = mods[1][:, g * 4:(g + 1) * 4]
                if kc == 0:
                    CP("vector", dst, ps[:, 0:4], [kp], ["mod1"])
                else:
                    TT("vector", dst, dst, ps[:, 0:4], ALU.add, [kp, "mod1"], ["mod1"])
            return f
        for kc in range(8):
            for g in range(6):
                tasks.append(mk(kc, g))

        def fin():
            TT("vector", mods[1], mods[1], V("b_ada1"), ALU.add, ["mod1", "vecs"], ["mod1"])
            TS("vector", gsv[1], mods[1][:, 8:16], 1.0, None, ALU.add, None, ["mod1"], ["mod1"])
            TT("vector", gsv[1], gsv[1], V("norm_g1"), ALU.mult, ["mod1", "vecs"], ["mod1"])
        tasks.append(fin)
        return tasks

    def ada1_tasks():
        tasks = []

        def mk(kc, g):
            def f():
                st, ks = wst.next()
                DMA(st, wada_d[1][kc, :, g * 1024:(g + 1) * 1024], ks, [], [ks])
                ps, kp = psum("misc", [7])
                for j in range(8):
                    MM(ps[:, j:j + 1], st[:, j * 128:(j + 1) * 128], sc[:, kc:kc + 1], True, True, [ks], [kp])
                dst = mods[1][:, g * 8:(g + 1) * 8]
                if kc == 0:
                    CP("vector", dst, ps[:, 0:8], [kp], ["mod1"])
                else:
                    TT("vector", dst, dst, ps[:, 0:8], ALU.add, [kp, "mod1"], ["mod1"])
            return f
        for kc in range(8):
            for g in range(3):
                tasks.append(mk(kc, g))

        def fin():
            TT("vector", mods[1], mods[1], V("b_ada1"), ALU.add, ["mod1", "vecs"], ["mod1"])
            TS("vector", gsv[1], mods[1][:, 8:16], 1.0, None, ALU.add, None, ["mod1"], ["mod1"])
            TT("vector", gsv[1], gsv[1], V("norm_g1"), ALU.mult, ["mod1", "vecs"], ["mod1"])
        tasks.append(fin)
        return tasks
    shiftv = [mods[0][:, 0:8], mods[1][:, 0:8]]
    gatev = [mods[0][:, 16:24], mods[1][:, 16:24]]

    def rms_rstd(n, tag):
        blk = slice(n * TB, (n + 1) * TB)
        ps, kp = psum("stat", [2, 3])
        for c in range(8):
            sq, ksq = tmpf.next()
            ACT(sq, xT[:, c, blk], AF.Square, ["x%d_%d" % (c, n)], [ksq])
            MM(ps, ones_f, sq, c == 0, c == 7, [ksq, "const"], [kp])
        rs, krs = rsd[:, :], "rsd"
        TS("vector", rs, ps, 1.0 / D, EPS, ALU.mult, ALU.add, [kp], [krs])
        ACT(rs, rs, AF.Sqrt, [krs], [krs])
        RECIP(rs, rs, [krs], [krs])
        return rs, krs

    def load_w(src, width, eng="scalar", scratch=None, skey=None, reload=False):
        if reload:
            wb, kb = wring.next()
            DMA(wb[:, 0:width], scratch, "d" + kb, [skey], [kb])
            return wb, kb
        st, ks = wst.next()
        DMA(st[:, 0:width], src, ks, [], [ks])
        wb, kb = wbf.next()
        if eng == "scalar":
            ACT(wb[:, 0:width], st[:, 0:width], AF.Copy, [ks], [kb])
        else:
            CP(eng, wb[:, 0:width], st[:, 0:width], [ks], [kb])
        if scratch is not None:
            DMA(scratch, wb[:, 0:width], "ws" + kb, [kb], [skey])
        return wb, kb

    if do_l0:
        hT = arena16[:, 0:4096].rearrange("p (c t) -> p c t", c=8)
        convo = arena16[:, 4096:8192].rearrange("p (c t) -> p c t", c=8)
        y16 = arena16[:, 8192:14336].rearrange("p (c t) -> p c t", c=12)
        u16 = arena16[:, 14336:16384].rearrange("p (c t) -> p c t", c=4)
        yg = arena[:, 8192:10240].rearrange("p (c t) -> p c t", c=4)
        yg16 = arena16[:, 20480:22528].rearrange("p (c t) -> p c t", c=4)
        halo = arena16[:, 22528:22784].rearrange("p (c t) -> p c t", c=8)
        bbT = [arena16[:, 22784:24832].rearrange("p (q m) -> p q m", q=16),
               arena16[:, 24832:26880].rearrange("p (q m) -> p q m", q=16)]
        cT16 = [arena16[:, 26880:28928].rearrange("p (q m) -> p q m", q=16),
                arena16[:, 28928:30976].rearrange("p (q m) -> p q m", q=16)]
        wglu16 = arena16[:, 30976:33024]
        stg = [arena[:, 0:2048], arena[:, 2048:4096], arena[:, 4096:6144], arena[:, 6144:8192]]
        DMA(stg[0], bpre_d[:, :], "stg0", [], ["stg0", "cstage"])
        DMA(stg[1], bpim_d[:, :], "stg1", [], ["stg1"])
        sincos(th, sn_, cs_, "s5", "s5", "s5")
        TT("vector", are, rr, cs_, ALU.mult, K5, K5)
        TT("vector", aim, rr, sn_, ALU.mult, K5, K5)
        TS("vector", are, are, -1.0, None, ALU.add, None, K5, K5)
        TT("vector", den, V("lamre"), V("lamre"), ALU.mult, ["vecs"], K5)
        TT("vector", t1_, V("lamim"), V("lamim"), ALU.mult, ["vecs"], K5)
        TT("vector", den, den, t1_, ALU.add, K5, K5)
        RECIP(den, den, K5, K5)
        TT("vector", t1_, are, V("lamre"), ALU.mult, K5 + ["vecs"], K5)
        TT("vector", t2_, aim, V("lamim"), ALU.mult, K5 + ["vecs"], K5)
        TT("vector", t1_, t1_, t2_, ALU.add, K5, K5)
        TT("vector", cr, t1_, den, ALU.mult, K5, K5)
        TT("vector", t1_, aim, V("lamre"), ALU.mult, K5 + ["vecs"], K5)
        TT("vector", t2_, are, V("lamim"), ALU.mult, K5 + ["vecs"], K5)
        TT("vector", t1_, t1_, t2_, ALU.subtract, K5, K5)
        TT("vector", ci, t1_, den, ALU.mult, K5, K5)
        TS("vector", nci, ci, -1.0, None, ALU.mult, None, K5, K5)
        for q in range(16):
            qs = slice(q * 128, (q + 1) * 128)
            t_a, ka = tmpf.next()
            TS("vector", t_a[:, 0:128], stg[0][:, qs], cr[:, q:q + 1], None, ALU.mult, None, ["stg0"] + K5, [ka])
            STT(t_a[:, 0:128], stg[1][:, qs], nci[:, q:q + 1], t_a[:, 0:128], ALU.mult, ALU.add,
                ["stg1", ka] + K5, [ka])
            TS("vector", t_a[:, 128:256], stg[1][:, qs], cr[:, q:q + 1], None, ALU.mult, None, ["stg1"] + K5, [ka])
            STT(t_a[:, 128:256], stg[0][:, qs], ci[:, q:q + 1], t_a[:, 128:256], ALU.mult, ALU.add,
                ["stg0", ka] + K5, [ka])
            ps, kp = psum("misc", [7])
            for comp in range(2):
                src = t_a[:, comp * 128:(comp + 1) * 128]
                dst = ps[:, comp * 128:(comp + 1) * 128]
                P.op("tensor", lambda e, dst=dst, src=src: e.transpose(dst, src, ident_f), [ka, "const"], [kp])
            CP("vector", bbT[0][:, q, :], ps[:, 0:128], [kp], ["bbT"])
            CP("vector", bbT[1][:, q, :], ps[:, 128:256], [kp], ["bbT"])
        DMA(stg[2], cpre_d[:, :], "stg2", [], ["stg2"])
        DMA(stg[3], cpim_d[:, :], "stg3", [], ["stg3"])
        CP("gpsimd", cT16[0].rearrange("p q m -> p (q m)"), stg[2], ["stg2"], ["cT16"])
        TS("gpsimd", cT16[1].rearrange("p q m -> p (q m)"), stg[3], -1.0, None, ALU.mult, None, ["stg3"], ["cT16"])
        DMA(stg[0], wglu_d[:, :], "stg0", [], ["stg0"])
        CP("gpsimd", wglu16, stg[0], ["stg0"], ["wglu16"])
        P.barrier()

        for c in range(8):
            o0 = _VEC["conv_w"][0] + c * 31
            identb = bass.AP(c16, 896, [[8 * 128, 128], [0, 31], [1, 128]])
            wbc = bass.AP(vecs, o0, [[NV, 128], [1, 31], [0, 128]])
            dg3 = dgt.rearrange("p (k m) -> p k m", k=31)
            TT("vector", dg3, identb, wbc, ALU.mult, ["c16", "vecs"], ["dgA", "dgB"])
            DMA(dgd[c, :, :], dgt, "dgout", ["dgA", "dgB"], ["dgd%d" % c])
        def proj(m):
            wb, kb = load_w(win0_d[m, :, :], 1024, scratch=win0b[m, :, :], skey="win0b%d" % m, reload=(curblk["n"] > 0))
            ps, kp = psum("proj", [0, 1])
            for c in range(8):
                MM(ps, wb[:, c * 128:(c + 1) * 128], hT[:, c, :], c == 0, c == 7, [kb, "hT%d" % c], [kp])
            return ps, kp

        def head_ops(n):
            blk = slice(n * TB, (n + 1) * TB)
            ops = []
            st = {}

            def f_rms():
                st["rs"] = rms_rstd(n, "l0")
            ops.append(f_rms)

            def mk_h(c):
                def f():
                    rs, krs = st["rs"]
                    tmp, kt = tmpf.next()
                    STT(tmp, xT[:, c, blk], gsv[0][:, c:c + 1], rs, ALU.mult, ALU.mult,
                        ["x%d_%d" % (c, n), krs, "small"], [kt])
                    ACT(hT[:, c, :], tmp, AF.Identity, [kt, "small"], ["hT%d" % c], bias=shiftv[0][:, c:c + 1])
                return f
            for c in range(8):
                ops.append(mk_h(c))

            def mk_u(cc):
                def f():
                    psu, kpu = proj(24 + cc)
                    ACT(u16[:, cc, :], psu, AF.Identity, [kpu], ["u16_%d" % cc])
                return f
            for cc in range(4):
                ops.append(mk_u(cc))
            return ops

        tail_prev = []
        curblk = {"n": 0}
        ada1_q = ada1_ring_tasks() if (do_l1 and not ADA1_UPFRONT) else []
        for n in range(NB):
            blk = slice(n * TB, (n + 1) * TB)
            if n == 1:
                while tail_prev:
                    tail_prev.pop(0)()
                P.barrier()
            curblk["n"] = n
            hops = head_ops(n)
            while hops or tail_prev:
                if hops:
                    hops.pop(0)()
                if tail_prev:
                    tail_prev.pop(0)()

            cst = {}

            def conv_c1a(c):
                psg, kg = proj(8 + c)
                sig, ksig = tmpf.next()
                ACT(sig, psg, AF.Tanh, [kg], [ksig], scale=0.5)
                psv, kv = proj(c)
                cst[c] = {"sig": sig, "ksig": ksig, "psv": psv, "kv": kv}

            def conv_c1b(c):
                d = cst[c]
                sig, ksig, psv, kv = d["sig"], d["ksig"], d["psv"], d["kv"]
                DMA(dgt[:, 0:2048], dgd[c, :, 0:2048], "dgA", ["dgd%d" % c], ["dgA"])
                DMA(dgt[:, 2048:3968], dgd[c, :, 2048:3968], "dgB", ["dgd%d" % c], ["dgB"])
                glu, kgl = glup.next()
                if n == 0:
                    MEMSET("vector", glu[:, 0:30], 0.0, [kgl])
                else:
                    CP("vector", glu[:, 0:30], halo[:, c, 0:30], ["halo%d" % c], [kgl])
                STT(glu[:, 30:542], sig, 1.0, psv, ALU.add, ALU.mult, [kv, ksig], [kgl])
                if n < NB - 1:
                    CP("vector", halo[:, c, 0:30], glu[:, 512:542], [kgl], ["halo%d" % c])
                d["glu"] = glu; d["kgl"] = kgl

            def conv_c2(c):
                d = cst.pop(c)
                glu, kgl = d["glu"], d["kgl"]
                cps, kcp = psum("conv", [2, 3])
                for k in range(31):
                    MM(cps, dgt[:, k * 128:(k + 1) * 128], glu[:, k:k + 512], k == 0, k == 30,
                       ["dgA" if k < 16 else "dgB", kgl], [kcp])
                ACT(convo[:, c, :], cps, AF.Identity, [kcp, "vecs"], ["convo%d" % c], bias=V("conv_b", c, c + 1))

            pst = {}
            ysta = {}

            def s5_p2a_act(q):
                sn, ksn = plong.next()
                cs, kcs = plong.next()
                DMA(cs, tabd[q, :, 0:512], "d" + kcs, ["tabd%d" % q], [kcs])
                DMA(sn, tabd[q, :, 512:1024], "d" + ksn, ["tabd%d" % q], [ksn])
                pst[q] = {"sn": sn, "ksn": ksn, "cs": cs, "kcs": kcs}
                if n > 0:
                    st_re = small[:, 96 + q:97 + q]
                    st_im = small[:, 112 + q:113 + q]
                    kst = "st%d" % q
                    i_re = small[:, 192 + q:193 + q]
                    i_im = small[:, 208 + q:209 + q]
                    t_a = small[:, 224 + q:225 + q]
                    t_b = small[:, 240 + q:241 + q]
                    TS("gpsimd", t_a, st_re, c512[:, q:q + 1], None, ALU.mult, None, [kst, "e512"], [kst])
                    TS("gpsimd", t_b, st_im, ns512[:, q:q + 1], None, ALU.mult, None, [kst, "e512"], [kst])
                    TT("gpsimd", i_re, t_a, t_b, ALU.add, [kst], [kst])
                    TS("gpsimd", t_a, st_re, s512[:, q:q + 1], None, ALU.mult, None, [kst, "e512"], [kst])
                    TS("gpsimd", t_b, st_im, c512[:, q:q + 1], None, ALU.mult, None, [kst, "e512"], [kst])
                    TT("gpsimd", i_im, t_a, t_b, ALU.add, [kst], [kst])

            def s5_p2a_dve(q):
                pass

            def s5_p2a_sin(q):
                pass

            def s5_p1(q):
                cc = q // 4
                bre, kbre = psum("s5b", [4, 5])
                bim, kbim = psum("s5b", [4, 5])
                MM(bre, bbT[0][:, q, :], u16[:, cc, :], True, True, ["bbT", "u16_%d" % cc], [kbre])
                MM(bim, bbT[1][:, q, :], u16[:, cc, :], True, True, ["bbT", "u16_%d" % cc], [kbim])
                pst[q].update({"bre": bre, "kbre": kbre, "bim": bim, "kbim": kbim})

            def s5_p2b(q):
                d = pst[q]
                bre, kbre, bim, kbim = d["bre"], d["kbre"], d["bim"], d["kbim"]
                sn, ksn, cs, kcs = d["sn"], d["ksn"], d["cs"], d["kcs"]
                btr, kbtr = plong.next()
                bti, kbti = plong.next()
                m2, km2 = pshort.next()
                TT("vector", btr, bre, cs, ALU.mult, [kbre, kcs], [kbtr])
                TT("vector", m2, bim, sn, ALU.mult, [kbim, ksn], [km2])
                TT("vector", btr, btr, m2, ALU.add, [kbtr, km2], [kbtr])
                TT("vector", bti, bim, cs, ALU.mult, [kbim, kcs], [kbti])
                TT("vector", m2, bre, sn, ALU.mult, [kbre, ksn, km2], [km2])
                TT("vector", bti, bti, m2, ALU.subtract, [kbti, km2], [kbti])
                rb = bass.AP(s5, 3 * 16 + q, [[256, 128], [0, 512]])
                st_re = small[:, 96 + q:97 + q]
                st_im = small[:, 112 + q:113 + q]
                kst = "st%d" % q
                if n > 0:
                    i_re = small[:, 192 + q:193 + q]
                    i_im = small[:, 208 + q:209 + q]
                for (bt, kbt, stv) in ((btr, kbtr, (i_re if n > 0 else None)), (bti, kbti, (i_im if n > 0 else None))):
                    init = 0.0 if n == 0 else stv
                    P.op("vector", lambda e, bt=bt, init=init, rb=rb: e.tensor_tensor_scan(
                        out=bt, data0=rb, data1=bt, initial=init, op0=ALU.mult, op1=ALU.add),
                        [kbt, "s5", kst], [kbt])
                if n < NB - 1:
                    ACT(st_re, btr[:, 511:512], AF.Copy, [kbtr], [kst])
                    ACT(st_im, bti[:, 511:512], AF.Copy, [kbti], [kst])
                d.update({"btr": btr, "kbtr": kbtr, "bti": bti, "kbti": kbti})

            def s5_p3(q):
                d = pst[q]
                sn, ksn, cs, kcs = d["sn"], d["ksn"], d["cs"], d["kcs"]
                btr, kbtr, bti, kbti = d["btr"], d["kbtr"], d["bti"], d["kbti"]
                sre, ksre = tmph.next()
                sim, ksim = tmph.next()
                g1, kg1 = pg.next()
                g2, kg2 = pg.next()
                TT(P3ENG, g1, btr, cs, ALU.mult, [kbtr, kcs], [kg1])
                TT(P3ENG, g2, bti, sn, ALU.mult, [kbti, ksn], [kg2])
                TT(P3ENG, sre, g1, g2, ALU.subtract, [kg1, kg2], [ksre])
                TT(P3ENG, g1, btr, sn, ALU.mult, [kbtr, ksn, kg1], [kg1])
                TT(P3ENG, g2, bti, cs, ALU.mult, [kbti, kcs, kg2], [kg2])
                TT(P3ENG, sim, g1, g2, ALU.add, [kg1, kg2], [ksim])
                d.update({"sre": sre, "ksre": ksre, "sim": sim, "ksim": ksim})

            def s5_p4(q):
                cc = q // 4
                d = pst.pop(q)
                sre, ksre, sim, ksim = d["sre"], d["ksre"], d["sim"], d["ksim"]
                if q % 4 == 0:
                    ysta["y"] = psum("s5y", [6])
                y_ps, ky = ysta["y"]
                MM(y_ps, cT16[0][:, q, :], sre, q % 4 == 0, False, ["cT16", ksre], [ky])
                MM(y_ps, cT16[1][:, q, :], sim, False, q % 4 == 3, ["cT16", ksim], [ky])
                if q % 4 == 3:
                    yb, kyb = tmpf.next()
                    STT(yb, u16[:, cc, :], V("ssm_d", cc, cc + 1), y_ps, ALU.mult, ALU.add,
                        ["u16_%d" % cc, "vecs", ky], [kyb])
                    ACT(yg[:, cc, :], yb, AF.Gelu_apprx_tanh, [kyb], ["yg%d" % cc])
                    ACT(yg16[:, cc, :], yg[:, cc, :], AF.Copy, ["yg%d" % cc], ["yg16_%d" % cc])

            lnst = {}

            def ln_stats():
                s_ps, ks_ = psum("stat", [2, 3])
                q_ps, kq_ = psum("stat", [2, 3])
                for c in range(8):
                    MM(s_ps, ones16, convo[:, c, :], c == 0, c == 7, ["convo%d" % c, "c16"], [ks_])
                for c in range(8):
                    sq, ksq = pg.next()
                    ACT(sq, convo[:, c, :], AF.Square, ["convo%d" % c], [ksq])
                    MM(q_ps, ones_f, sq, c == 0, c == 7, [ksq, "const"], [kq_])
                mean, kmean = lnm[:, :], "lnm"
                TS("vector", mean, s_ps, 1.0 / 1024, None, ALU.mult, None, [ks_], [kmean])
                rl, krl = lnr[:, :], "lnr"
                TT("vector", rl, mean, mean, ALU.mult, [kmean], [krl])
                STT(rl, q_ps, 1.0 / 1024, rl, ALU.mult, ALU.subtract, [kq_, krl], [krl])
                TS("vector", rl, rl, EPS, None, ALU.add, None, [krl], [krl])
                ACT(rl, rl, AF.Sqrt, [krl], [krl])
                RECIP(rl, rl, [krl], [krl])

            nst = {}

            def norm_a(c):
                mean, kmean = lnm[:, :], "lnm"
                rl, krl = lnr[:, :], "lnr"
                t1, k1 = tmpf.next()
                TT("vector", t1, convo[:, c, :], mean, ALU.subtract, ["convo%d" % c, kmean], [k1])
                TT("vector", t1, t1, rl, ALU.mult, [k1, krl], [k1])
                ACT(t1, t1, AF.Silu, [k1, "vecs"], [k1], scale=V("ln_g", c, c + 1), bias=V("ln_b", c, c + 1))
                psa, kpa = proj(16 + c)
                sga, ksga = tmpf.next()
                ACT(sga, psa, AF.Silu, [kpa], [ksga])
                nst[c] = (t1, k1, sga, ksga)

            def norm_b(c):
                t1, k1, sga, ksga = nst.pop(c)
                TT("vector", y16[:, c, :], t1, sga, ALU.mult, [k1, ksga], ["y16_%d" % c])

            for k in range(21):
                if n >= 1 and ada1_q:
                    ada1_q.pop(0)()
                if 0 <= k - 2 < 16:
                    s5_p3(k - 2)
                if k < 16:
                    s5_p2a_act(k)
                if 0 <= k - 1 < 16:
                    s5_p2b(k - 1)
                if 0 <= k - 3 < 16:
                    s5_p4(k - 3)
                if k == 11:
                    ln_stats()
                if 0 <= k - 2 < 8:
                    conv_c2(k - 2)
                if k < 16:
                    s5_p1(k)
                    s5_p2a_dve(k)
                    s5_p2a_sin(k)
                if 0 <= k - 1 < 8:
                    conv_c1b(k - 1)
                if k < 8:
                    conv_c1a(k)
                if 8 <= k < 12:
                    mo_ = k - 8
                    psb_, kpb = proj(28 + mo_)
                    ACT(y16[:, 8 + mo_, :], psb_, AF.Silu, [kpb], ["y16_%d" % (8 + mo_)])
                if 0 <= k - 13 < 8:
                    norm_b(k - 13)
                if 0 <= k - 12 < 8:
                    norm_a(k - 12)
            def mk_glu(mo, n=n):
                def f():
                    ps, kp = psum("misc", [7])
                    for k in range(4):
                        MM(ps, wglu16[:, k * 512 + mo * 128:k * 512 + (mo + 1) * 128], yg16[:, k, :], k == 0, k == 3,
                           ["wglu16", "yg16_%d" % k], [kp])
                    sg, ksg = tmpf.next()
                    ACT(sg, ps, AF.Sigmoid, [kp, "vecs"], [ksg], bias=V("b_glu", mo, mo + 1))
                    TT("vector", sg, sg, yg[:, mo, :], ALU.mult, [ksg, "yg%d" % mo], [ksg])
                    ky_ = "y16_%d" % (8 + mo)
                    TT("vector", y16[:, 8 + mo, :], sg, y16[:, 8 + mo, :], ALU.mult, [ksg, ky_], [ky_])
                return f

            def mk_wout(mo, n=n, blk=blk):
                def f():
                    wb, kb = load_w(wout0_d[mo, :, 0:1024], 1024, scratch=wout0b[mo, :, 0:1024],
                                    skey="wout0bA%d" % mo, reload=(n > 0))
                    wb2, kb2 = load_w(wout0_d[mo, :, 1024:1536], 512, scratch=wout0b[mo, :, 1024:1536],
                                      skey="wout0bB%d" % mo, reload=(n > 0))
                    ps, kp = psum("misc", [7])
                    for k in range(8):
                        MM(ps, wb[:, k * 128:(k + 1) * 128], y16[:, k, :], k == 0, False, [kb, "y16_%d" % k], [kp])
                    for k in range(8, 12):
                        MM(ps, wb2[:, (k - 8) * 128:(k - 7) * 128], y16[:, k, :], False, k == 11,
                           [kb2, "y16_%d" % k], [kp])
                    xk = "x%d_%d" % (mo, n)
                    STT(xT[:, mo, blk], ps, gatev[0][:, mo:mo + 1], xT[:, mo, blk], ALU.mult, ALU.add,
                        [kp, "small", xk], [xk])
                return f
            tail_prev = [mk_glu(mo) for mo in range(4)] + [mk_wout(mo) for mo in range(8)]
        for f in tail_prev:
            f()
        while ada1_q:
            ada1_q.pop(0)()
        P.barrier()

    if do_l1:
        hT1 = arena16[:, 0:16384].rearrange("p (c t) -> p c t", c=8)
        def mkset(base16, o):
            return {
                "qT": base16[:, o:o + 2048],
                "kpad": [base16[:, o + 2048:o + 4096], base16[:, o + 4096:o + 6144]],
                "vpad": [base16[:, o + 6144:o + 8192].rearrange("p (t m) -> p t m", t=16),
                         base16[:, o + 8192:o + 10240].rearrange("p (t m) -> p t m", t=16)],
                "gsl": base16[:, o + 10240:o + 12288],
                "opair": base16[:, o + 12288:o + 14336],
            }
        sets = [mkset(arena16, 16384), mkset(parena16, 0)]
        ssum = [[arena16[:, 30720:31232], arena16[:, 31232:31744]],
                [arena16[:, 31744:32256], arena16[:, 32256:32768]]]
        MEMSET("gpsimd", arena16[:, 16384 + 2048:16384 + 10240], 0.0, ["kpad0_0", "kpad1_0", "vpad0_0", "vpad1_0"])
        MEMSET("gpsimd", parena16[:, 2048:10240], 0.0, ["kpad0_1", "kpad1_1", "vpad0_1", "vpad1_1"])
        for n in range(NB):
            blk = slice(n * TB, (n + 1) * TB)
            rs, krs = rms_rstd(n, "l1")
            for c in range(8):
                tmp, kt = tmpf.next()
                STT(tmp, xT[:, c, blk], gsv[1][:, c:c + 1], rs, ALU.mult, ALU.mult,
                    ["x%d_%d" % (c, n), krs, "small"], [kt])
                ACT(hT1[:, c, blk], tmp, AF.Identity, [kt, "small"], ["h1_%d_%d" % (c, n)],
                    bias=shiftv[1][:, c:c + 1])

        ZB = [0, 1, 2, 3]
        OB = [4, 5]

        def proj_tasks(hp):
            S_ = sets[hp % 2]
            sx = hp % 2
            tasks = []
            wref = {}

            def mk_load(sec):
                def t():
                    wref[sec] = load_w(win1_d[sec * 8 + hp, :, :], 1024, "vector")
                return t

            def mk_grp(sec, n):
                def t():
                    wb, kb = wref[sec]
                    ps, kp = psum("proj1", [6, 7])
                    for c in range(8):
                        MM(ps, wb[:, c * 128:(c + 1) * 128], hT1[:, c, n * TB:(n + 1) * TB], c == 0, c == 7,
                           [kb, "h1_%d_%d" % (c, n)], [kp])
                    cols = slice(n * TB, (n + 1) * TB)
                    if sec == 0:
                        TS("vector", S_["qT"][:, cols], ps, 0.125, None, ALU.mult, None, [kp], ["qT_%d" % sx])
                    elif sec == 1:
                        CP("vector", S_["kpad"][0][0:64, cols], ps[0:64, :], [kp], ["kpad0_%d" % sx])
                        CP("vector", S_["kpad"][1][64:128, cols], ps[64:128, :], [kp], ["kpad1_%d" % sx])
                    else:
                        ACT(S_["gsl"][:, cols], ps, AF.Silu, [kp], ["gsl_%d" % sx])
                return t

            def mk_v(tt):
                def t():
                    wb, kb = wref[2]
                    ps, kp = psum("proj1", [6, 7])
                    n = tt // 4
                    for c in range(8):
                        MM(ps[:, 0:128], hT1[:, c, tt * 128:(tt + 1) * 128], wb[:, c * 128:(c + 1) * 128],
                           c == 0, c == 7, [kb, "h1_%d_%d" % (c, n)], [kp])
                    CP("vector", S_["vpad"][0][:, tt, 0:64], ps[:, 0:64], [kp], ["vpad0_%d" % sx])
                    CP("vector", S_["vpad"][1][:, tt, 64:128], ps[:, 64:128], [kp], ["vpad1_%d" % sx])
                return t

            for sec in (0, 1, 3):
                tasks.append(mk_load(sec))
                for n in range(NB):
                    tasks.append(mk_grp(sec, n))
            tasks.append(mk_load(2))
            for tt in range(16):
                tasks.append(mk_v(tt))
            return tasks

        def wout_tasks(hp):
            S_ = sets[hp % 2]
            sx = hp % 2
            tasks = []
            wref = {}

            def ld():
                wref[0] = load_w(wout1_d[hp, :, :], 1024, "vector")
            tasks.append(ld)

            def mk(mo, n):
                def t():
                    wb, kb = wref[0]
                    blk = slice(n * TB, (n + 1) * TB)
                    ps, kp = psum("proj1", [6, 7])
                    MM(ps, wb[:, mo * 128:(mo + 1) * 128], S_["opair"][:, blk], True, True, [kb, "opair_%d" % sx], [kp])
                    xk = "x%d_%d" % (mo, n)
                    STT(xT[:, mo, blk], ps, gatev[1][:, mo:mo + 1], xT[:, mo, blk], ALU.mult, ALU.add,
                        [kp, "small", xk], [xk])
                return t
            for mo in range(8):
                for n in range(NB):
                    tasks.append(mk(mo, n))
            return tasks

        items = []
        for hp in range(8):
            for g4 in range(4):
                nblk = 4 * g4 + 4
                for b in range(nblk - 1, -1, -1):
                    for hi in range(2):
                        items.append((hp, hi, g4, b, nblk))
        info = {}
        hstate = {0: [0, 0], 1: [0, 0]}
        ostate = {}

        def stA1(it):
            hp, hi, g4, b, nblk = it
            S_ = sets[hp % 2]; sx = hp % 2
            T0 = g4 * TB
            c0 = max(0, b * 128 - T0)
            Z, kz = psum("Z", ZB)
            MM(Z[:, c0:TB], S_["kpad"][hi][:, b * 128:(b + 1) * 128], S_["qT"][:, T0 + c0:T0 + TB], True, True,
               ["kpad%d_%d" % (hi, sx), "qT_%d" % sx], [kz])
            info[it] = {"Z": Z, "kz": kz, "c0": c0}

        def stA2(it):
            hp, hi, g4, b, nblk = it
            d = info[it]
            Z, kz, c0 = d["Z"], d["kz"], d["c0"]
            first = (b == nblk - 1)
            e_, ke = tmpf.next()
            ACT(e_[:, c0:TB], Z[:, c0:TB], AF.Exp, [kz], [ke])
            sp, ksp = tmph.next()
            ACT(sp[:, c0:TB], e_[:, c0:TB], AF.Ln, [ke], [ksp], bias=1.0)
            if b >= 4 * g4:
                TT("vector", sp[:, c0:c0 + 128], sp[:, c0:c0 + 128], tri01_16, ALU.mult, [ksp, "c16"], [ksp])
            if first:
                MEMSET("gpsimd", ssum[hi][0], 0.0, ["ssum%d_0" % hi])
                MEMSET("gpsimd", ssum[hi][1], 0.0, ["ssum%d_1" % hi])
                hstate[hi][0] = 0
            cur = hstate[hi][0]
            d["sp"] = sp; d["ksp"] = ksp; d["cur"] = cur
            if b > 0:
                nxt = 1 - cur
                TT("vector", ssum[hi][nxt][:, c0:TB], ssum[hi][cur][:, c0:TB], sp[:, c0:TB], ALU.add,
                   ["ssum%d_%d" % (hi, cur), ksp], ["ssum%d_%d" % (hi, nxt)])
                hstate[hi][0] = nxt

        def stB1(it):
            hp, hi, g4, b, nblk = it
            d = info[it]
            Z, kz, c0, sp, ksp, cur = d["Z"], d["kz"], d["c0"], d["sp"], d["ksp"], d["cur"]
            first = (b == nblk - 1)
            diag = (b >= 4 * g4)
            kc_ = "ssum%d_%d" % (hi, cur)
            MM(Z[:, c0:TB], nti16, sp[:, c0:TB], False, (first and not diag), [ksp, "c16"], [kz], skip=True)
            if not first:
                MM(Z[:, c0:TB], negones16, ssum[hi][cur][:, c0:TB], False, not diag, [kc_, "c16"], [kz], skip=True)
            if diag:
                MM(Z[:, c0:c0 + 128], ident16, negbig16, False, True, ["c16"], [kz], skip=True)

        def stB2(it):
            d = info[it]
            Z, kz, c0 = d["Z"], d["kz"], d["c0"]
            w16, kw = tmph.next()
            ACT(w16[:, c0:TB], Z[:, c0:TB], AF.Exp, [kz], [kw])
            d["w16"] = w16; d["kw"] = kw

        def stB3(it):
            hp, hi, g4, b, nblk = it
            S_ = sets[hp % 2]; sx = hp % 2
            T0 = g4 * TB
            d = info.pop(it)
            c0, w16, kw = d["c0"], d["w16"], d["kw"]
            if b == nblk - 1:
                O, ko = psum("O%d" % hi, [OB[hi]])
                ostate[hi] = (O, ko)
                MM(O, zeros16, S_["qT"][:, 0:TB], True, False, ["c16", "qT_%d" % sx], [ko])
            O, ko = ostate[hi]
            MM(O[:, c0:TB], S_["vpad"][hi][:, b, :], w16[:, c0:TB], False, b == 0, [kw, "vpad%d_%d" % (hi, sx)], [ko])
            if b == 0:
                rows = slice(hi * 64, (hi + 1) * 64)
                TT("vector", S_["opair"][rows, T0:T0 + TB], O[rows, :], S_["gsl"][rows, T0:T0 + TB], ALU.mult,
                   [ko, "gsl_%d" % sx], ["opair_%d" % sx])

        for t in proj_tasks(0):
            t()
        NI = len(items)
        PER = NI // 8
        queue = []
        for s_ in range(NI + 4):
            p_ = s_ // PER
            r_ = s_ % PER
            if s_ < NI and r_ == 5 and p_ + 1 < 8:
                queue.extend(proj_tasks(p_ + 1))
            if s_ < NI and r_ == 6 and p_ >= 1:
                queue = wout_tasks(p_ - 1) + queue
            if 0 <= s_ - 2 < NI:
                stB1(items[s_ - 2])
            if 0 <= s_ - 4 < NI:
                stB3(items[s_ - 4])
            if s_ < NI:
                if r_ == 0:
                    while queue:
                        queue.pop(0)()
                stA1(items[s_])
            if 0 <= s_ - 1 < NI:
                stA2(items[s_ - 1])
            if 0 <= s_ - 3 < NI:
                stB2(items[s_ - 3])
            if queue:
                left = PER - r_ - 2
                ntask = len(queue) if left <= 0 else -(-len(queue) // left)
                for _ in range(min(ntask, len(queue))):
                    queue.pop(0)()
        while queue:
            queue.pop(0)()
        for t in wout_tasks(7):
            t()
        P.barrier()

    diagG = arena[:, 0:1024].rearrange("p (c m) -> p c m", c=8)
    for c in range(8):
        TS("vector", diagG[:, c, :], ident_f, V("final_g", c, c + 1), None, ALU.mult, None,
           ["const", "vecs"], ["diagG"])
    obuf = [arena[:, 1024:2048], arena[:, 2048:3072]]
    frs = None
    for tt in range(16):
        n = tt // 4
        tsl = slice(tt * 128, (tt + 1) * 128)
        if tt % 4 == 0:
            frs = rms_rstd(n, "fin")
        rs, krs = frs
        ps2, k2 = psum("fin2", [4, 5])
        MM(ps2[:, 0:1], rs[:, (tt % 4) * 128:(tt % 4 + 1) * 128], ident_f[:, 0:1], True, True, [krs, "const"], [k2])
        rt, krt = tmpf.next()
        CP("vector", rt[:, 0:1], ps2[:, 0:1], [k2], [krt])
        ob = obuf[tt % 2]
        kob = "obuf%d" % (tt % 2)
        for half in range(2):
            ps, kp = psum("fin", [0, 1])
            for j in range(4):
                c = half * 4 + j
                MM(ps[:, j * 128:(j + 1) * 128], xT[:, c, tsl], diagG[:, c, :], True, True,
                   ["x%d_%d" % (c, n), "diagG"], [kp])
            TS("vector", ob[:, half * 512:(half + 1) * 512], ps, rt[:, 0:1], None, ALU.mult, None,
               [kp, krt], [kob])
        DMA(out_d[tsl, :], ob, kob + "d", [kob], [kob])
    P.final_wait(["obuf0d", "obuf1d"])
    P.emit()
    return nc


def _host_inputs(inp, b):
    f = np.float32

    def colvec(v):
        v = np.asarray(v, f)
        return v.reshape(-1, 128).T

    vecs = np.zeros((128, NV), f)

    def put(name, arr):
        o, w = _VEC[name]
        assert arr.shape == (128, w), (name, arr.shape)
        vecs[:, o:o + w] = arr
    put("norm_g0", colvec(inp["l0_norm_g"]))
    put("b_ada0", colvec(inp["l0_b_ada"]))
    put("conv_b", colvec(inp["l0_conv_b"]))
    put("ln_g", colvec(inp["l0_conv_ln_g"]))
    put("ln_b", colvec(inp["l0_conv_ln_b"]))
    put("ssm_d", colvec(inp["l0_ssm_d"]))
    put("b_glu", colvec(inp["l0_ssm_b_glu"]))
    put("norm_g1", colvec(inp["l1_norm_g"]))
    put("b_ada1", colvec(inp["l1_b_ada"]))
    put("final_g", colvec(inp["final_norm_g"]))
    put("cT", colvec(inp["c"][b]))
    cw = np.asarray(inp["l0_conv_w"], f)
    put("conv_w", cw.T.reshape(8, 128, 31).transpose(1, 0, 2).reshape(128, 248))

    def pairlay(a):
        return np.asarray(a, f).reshape(16, 2, 64).transpose(1, 2, 0).reshape(128, 16)
    put("lamre", pairlay(inp["l0_ssm_lam_re"]))
    put("lamim", pairlay(inp["l0_ssm_lam_im"]))
    ld = np.asarray(inp["l0_ssm_log_dt"], f)
    put("logdt", pairlay(np.repeat(ld[:, None], 64, axis=1)))

    def kchunks(w, ncol):
        w = np.asarray(w, f)
        K, N = w.shape
        return np.ascontiguousarray(
            w.reshape(K // 128, 128, N // ncol, ncol).transpose(2, 1, 0, 3).reshape(N // ncol, 128, (K // 128) * ncol))

    def bpad(bmat):
        bmat = np.asarray(bmat, f)
        o = np.zeros((2, 64, 16, 128), f)
        for g in range(32):
            q, h = g // 2, g % 2
            o[h, :, q, (g % 8) * 16:(g % 8) * 16 + 16] = bmat[g]
        return o.reshape(128, 2048)

    def cpad(cmat):
        return bpad(np.asarray(cmat, f).transpose(0, 2, 1))

    d = {}
    xb = np.asarray(inp["x"][b], f)
    d["xT"] = np.ascontiguousarray(xb.reshape(L, 8, 128).transpose(2, 1, 0))
    d["vecs"] = vecs
    d["w_ada0"] = np.ascontiguousarray(np.asarray(inp["l0_w_ada"], f).reshape(8, 128, 3072))
    d["w_ada1"] = np.ascontiguousarray(np.asarray(inp["l1_w_ada"], f).reshape(8, 128, 3072))
    d["w_in0"] = kchunks(inp["l0_w_in"], 128)
    d["w_out0"] = kchunks(inp["l0_w_out"], 128)
    d["w_in1"] = kchunks(inp["l1_w_in"], 128)
    w1 = np.asarray(inp["l1_w_out"], f)
    d["w_out1"] = np.ascontiguousarray(w1.reshape(8, 128, 1024))
    wg = np.asarray(inp["l0_ssm_w_glu"], f)
    d["w_glu"] = np.ascontiguousarray(wg.reshape(4, 128, 512).transpose(1, 0, 2).reshape(128, 2048))
    d["bp_re"] = bpad(inp["l0_ssm_b_re"])
    d["bp_im"] = bpad(inp["l0_ssm_b_im"])
    d["cp_re"] = cpad(inp["l0_ssm_c_re"])
    d["cp_im"] = cpad(inp["l0_ssm_c_im"])
    cst = np.zeros((128, 5 * 128 + 512), f)
    i = np.arange(128)
    cst[:, 0:128] = np.eye(128, dtype=f)
    cst[:, 128:256] = 1.0
    cst[:, 256:384] = (i[None, :] > i[:, None]).astype(f)
    cst[:, 384:512] = -(i[:, None] >= i[None, :]).astype(f)
    cst[:, 512:640] = np.where(i[None, :] <= i[:, None], -30000.0, 0.0).astype(f)
    cst[:, 640:1152] = np.arange(512, dtype=f)[None, :]
    d["consts"] = cst
    return d


_NC_CACHE = {}


def kernel(**inputs):
    inp = {k: np.asarray(v) for k, v in inputs.items()}
    if "nc" not in _NC_CACHE:
        _NC_CACHE["nc"] = build_program()
    nc = _NC_CACHE["nc"]
    in_maps = [_host_inputs(inp, b) for b in range(8)]
    res = run_bass_kernel_spmd(nc, in_maps, core_ids=list(range(8)))
    out = np.stack([np.asarray(r["out"], np.float32).reshape(L, D) for r in res.results], axis=0)
    return out
```

```python
import math
import numpy as np
import concourse.bass as bass
import concourse.mybir as mybir
from concourse.bass_utils import run_bass_kernel_spmd

F32 = mybir.dt.float32
BF16 = mybir.dt.bfloat16
AF = mybir.ActivationFunctionType
ALU = mybir.AluOpType

L = 2048
D = 1024
NB = 4
TB = 512
EPS = 1e-6
PI = math.pi
SAME_ENG_SYNC = True
N_FILL = 0
ADA1_UPFRONT = False
P3ENG = "vector"

_VEC = {}
_off = 0
for _n, _w in [("norm_g0", 8), ("b_ada0", 24), ("conv_b", 8), ("ln_g", 8), ("ln_b", 8), ("ssm_d", 4),
               ("b_glu", 4), ("norm_g1", 8), ("b_ada1", 24), ("final_g", 8), ("cT", 8), ("conv_w", 248),
               ("lamre", 16), ("lamim", 16), ("logdt", 16)]:
    _VEC[_n] = (_off, _w)
    _off += _w
NV = _off


class _Op:
    __slots__ = ("eng", "fn", "deps", "signal", "val", "tag", "is_dma")


class Prog:
    ENG = ("sync", "tensor", "vector", "scalar", "gpsimd")

    def __init__(self, nc):
        self.nc = nc
        self.ops = {e: [] for e in self.ENG}
        self.lastw = {}
        self.readers = {}
        self.pend = {e: [] for e in self.ENG}
        self.dma_count = {}
        self.last_dma = {}

    def _add(self, op, r, w):
        deps = set()
        for k in r:
            o = self.lastw.get(k)
            if o is not None:
                deps.add(o)
        for k in w:
            o = self.lastw.get(k)
            if o is not None:
                deps.add(o)
            for o in self.readers.get(k, ()):
                deps.add(o)
        for o in self.pend[op.eng]:
            deps.add(o)
        self.pend[op.eng] = []
        deps.discard(op)
        op.deps = deps
        for d in deps:
            d.signal = True
        for k in r:
            self.readers.setdefault(k, []).append(op)
        for k in w:
            self.lastw[k] = op
            self.readers[k] = []
        self.ops[op.eng].append(op)

    def op(self, eng, fn, r=(), w=()):
        o = _Op()
        o.eng = eng; o.fn = fn; o.signal = False; o.val = 0; o.tag = None; o.is_dma = False
        self._add(o, r, w)
        return o

    def dma(self, fn, tag, r=(), w=(), q="sync"):
        o = _Op()
        o.eng = q; o.fn = fn; o.signal = True; o.tag = tag; o.is_dma = True
        self.dma_count[tag] = self.dma_count.get(tag, 0) + 1
        o.val = 16 * self.dma_count[tag]
        self.last_dma[tag] = o
        self._add(o, r, w)
        return o

    def barrier(self):
        evs = []
        for e in self.ENG:
            if self.ops[e]:
                o = self.ops[e][-1]
                o.signal = True
                evs.append(o)
        for t, o in self.last_dma.items():
            evs.append(o)
        for e in self.ENG:
            self.pend[e] = list(evs)

    def final_wait(self, tags):
        o = _Op()
        o.eng = "sync"; o.fn = (lambda e: None); o.signal = False; o.val = 0; o.tag = None; o.is_dma = False
        o.deps = set(self.last_dma[t] for t in tags)
        self.ops["sync"].append(o)

    def emit(self):
        nc = self.nc
        for e in self.ENG:
            cnt = 0
            for o in self.ops[e]:
                if o.is_dma:
                    continue
                if o.signal:
                    cnt += 1
                    o.val = cnt
        sems = {e: nc.alloc_semaphore("sem_" + e) for e in self.ENG}
        dsems = {t: nc.alloc_semaphore("dsem_" + t) for t in self.dma_count}
        ops = self.ops
        with nc.Block() as block:
            for e in self.ENG:
                def body(eng, e=e):
                    waited = {}
                    for o in ops[e]:
                        need = {}
                        for d in o.deps:
                            if d.is_dma:
                                nm = "d_" + d.tag; sem = dsems[d.tag]
                            else:
                                if d.eng == e and (e == "tensor" or e == "sync" or not SAME_ENG_SYNC):
                                    continue
                                nm = "e_" + d.eng; sem = sems[d.eng]
                            if d.val > need.get(nm, (None, 0))[1]:
                                need[nm] = (sem, d.val)
                        for nm, (sem, val) in need.items():
                            if waited.get(nm, 0) >= val:
                                continue
                            eng.wait_ge(sem, val)
                            waited[nm] = val
                        ins = o.fn(eng)
                        if ins is None:
                            continue
                        if o.is_dma:
                            ins.then_inc(dsems[o.tag], 16)
                        elif o.signal:
                            ins.then_inc(sems[e], 1)
                getattr(block, e)(body)


class Pool:
    def __init__(self, nc, name, n, shape, dtype, tiles=None):
        if tiles is None:
            self.t = [nc.alloc_sbuf_tensor("%s%d" % (name, i), list(shape), dtype)[:, :] for i in range(n)]
        else:
            self.t = list(tiles)
        self.k = ["%s%d" % (name, i) for i in range(len(self.t))]
        self.i = 0

    def next(self):
        j = self.i % len(self.t)
        self.i += 1
        return self.t[j], self.k[j]


def build_program(do_l0=True, do_l1=True):
    nc = bass.Bass("TRN2", target_bir_lowering=False)
    P = Prog(nc)

    def din(name, shape):
        return nc.dram_tensor(name, list(shape), F32, kind="ExternalInput")

    xT_d = din("xT", [128, 8, L])
    vecs_d = din("vecs", [128, NV])
    wada_d = [din("w_ada0", [8, 128, 3072]), din("w_ada1", [8, 128, 3072])]
    win0_d = din("w_in0", [32, 128, 1024])
    wout0_d = din("w_out0", [8, 128, 1536])
    win1_d = din("w_in1", [32, 128, 1024])
    wout1_d = din("w_out1", [8, 128, 1024])
    wglu_d = din("w_glu", [128, 2048])
    bpre_d = din("bp_re", [128, 2048])
    bpim_d = din("bp_im", [128, 2048])
    cpre_d = din("cp_re", [128, 2048])
    cpim_d = din("cp_im", [128, 2048])
    consts_d = din("consts", [128, 5 * 128 + 512])
    out_d = nc.dram_tensor("out", [L, D], F32, kind="ExternalOutput")
    dgd = nc.dram_tensor("dgd", [8, 128, 31 * 128], BF16)

    sb = nc.alloc_sbuf_tensor
    xT = sb("xT_sb", [128, 8, L], F32)
    vecs = sb("vecs_sb", [128, NV], F32)
    consts = sb("consts_sb", [128, 2 * 128 + 512], F32)
    ident_f = consts[:, 0:128]
    ones_f = consts[:, 128:256]
    iota_f = consts[:, 256:768]
    c16 = sb("c16", [128, 8 * 128], BF16)
    ident16 = c16[:, 0:128]
    negones16 = c16[:, 128:256]
    tri01_16 = c16[:, 256:384]
    nti16 = c16[:, 384:512]
    negbig16 = c16[:, 512:640]
    zeros16 = c16[:, 640:768]
    ones16 = c16[:, 768:896]
    halfid16 = c16[:, 896:1024]
    small = sb("small", [128, 256], F32)
    rsd = sb("rsd", [128, 512], F32)
    lnm = sb("lnm", [128, 512], F32)
    lnr = sb("lnr", [128, 512], F32)

    def V(name, j0=0, j1=None):
        o, w = _VEC[name]
        if j1 is None:
            j1 = w
        return vecs[:, o + j0:o + j1]

    arena = sb("arena", [128, 16640], F32)
    arena16 = arena.bitcast(BF16)

    tmpf = Pool(nc, "tf", 4, [128, 512], F32)
    parena = sb("parena", [128, 9728], F32)
    parena16 = parena.bitcast(BF16)
    plong = Pool(nc, "pl", 10, None, None, tiles=[parena[:, i * 512:(i + 1) * 512] for i in range(10)])
    pshort = Pool(nc, "ps_", 2, None, None, tiles=[parena[:, 5120 + i * 512:5120 + (i + 1) * 512] for i in range(2)])
    pg = Pool(nc, "pg", 2, None, None, tiles=[parena[:, 6144 + i * 512:6144 + (i + 1) * 512] for i in range(2)])
    tmph = Pool(nc, "th", 6, [128, 512], BF16)
    wstT = sb("wstT", [128, 2048], F32)
    wst = Pool(nc, "wst", 2, None, None, tiles=[wstT[:, 0:1024], wstT[:, 1024:2048]])
    wbfT = [sb("wbfT%d" % i, [128, 1024], BF16) for i in range(2)]
    wbf = Pool(nc, "wbf", 2, None, None, tiles=[t_[:, :] for t_ in wbfT])
    wstT16 = wstT.bitcast(BF16)
    wring_f32 = [t_.bitcast(F32)[:, 0:512] for t_ in wbfT] + [wstT[:, i * 512:(i + 1) * 512] for i in range(4)]
    wring = Pool(nc, "wring", 6, None, None, tiles=wbf.t + [wstT16[:, i * 1024:(i + 1) * 1024] for i in range(4)])
    wring.k = list(wbf.k) + ["wrx0", "wrx1", "wrx2", "wrx3"]
    win0b = nc.dram_tensor("win0b", [32, 128, 1024], BF16)
    wout0b = nc.dram_tensor("wout0b", [8, 128, 1536], BF16)
    dgt = parena16[:, 14336:14336 + 3968]
    glup = Pool(nc, "glu", 2, None, None, tiles=[parena16[:, 18304 + i * 544:18304 + (i + 1) * 544] for i in range(2)])

    psb = [nc.alloc_psum_tensor("psb%d" % i, [128, 512], F32) for i in range(8)]
    ps_cnt = {}

    def psum(role, banks):
        i = ps_cnt.get(role, 0)
        ps_cnt[role] = i + 1
        b = banks[i % len(banks)]
        return psb[b][:, :], "ps%d" % b

    def ACT(out, in_, func, r, w, **kw):
        P.op("scalar", lambda e: e.activation(out=out, in_=in_, func=func, **kw), r, w)

    def MM(out, lhsT, rhs, start, stop, r, w, skip=False):
        if skip:
            P.op("tensor", lambda e: e.matmul(out, lhsT, rhs, start=start, stop=stop, skip_group_check=True), r, w)
        else:
            P.op("tensor", lambda e: e.matmul(out, lhsT, rhs, start=start, stop=stop), r, w)

    def TT(eng, out, in0, in1, op, r, w):
        P.op(eng, lambda e: e.tensor_tensor(out=out, in0=in0, in1=in1, op=op), r, w)

    def TS(eng, out, in0, s1, s2, op0, op1, r, w):
        if op1 is None:
            P.op(eng, lambda e: e.tensor_scalar(out=out, in0=in0, scalar1=s1, scalar2=None, op0=op0), r, w)
        else:
            P.op(eng, lambda e: e.tensor_scalar(out=out, in0=in0, scalar1=s1, scalar2=s2, op0=op0, op1=op1), r, w)

    def STT(out, in0, scalar, in1, op0, op1, r, w):
        P.op("vector", lambda e: e.scalar_tensor_tensor(out=out, in0=in0, scalar=scalar, in1=in1,
                                                        op0=op0, op1=op1), r, w)

    def CP(eng, out, in_, r, w):
        P.op(eng, lambda e: e.tensor_copy(out=out, in_=in_), r, w)

    def MEMSET(eng, ap, val, w):
        P.op(eng, lambda e: e.memset(ap, val), (), w)

    def DMA(out, in_, tag, r, w, q="sync"):
        P.dma(lambda e: e.dma_start(out=out, in_=in_), tag, r, w, q=q)

    def RECIP(out, in_, r, w):
        P.op("vector", lambda e: e.reciprocal(out=out, in_=in_), r, w)

    MAGIC = 12582912.0
    PI_LO = 3.1415925
    CW1 = 6.28125
    CW2 = 2.0 * PI - 6.28125

    def sincos(x, sn, cs, kx, ksn, kcs):
        ACT(cs, x, AF.Identity, [kx], [kcs], scale=1.0 / (2.0 * PI), bias=MAGIC)
        ACT(cs, cs, AF.Identity, [kcs], [kcs], bias=-MAGIC)
        STT(sn, cs, -CW1, x, ALU.mult, ALU.add, [kcs, kx], [ksn])
        STT(sn, cs, -CW2, sn, ALU.mult, ALU.add, [kcs, ksn], [ksn])
        TS("vector", sn, sn, -PI_LO, PI_LO, ALU.max, ALU.min, [ksn], [ksn])
        STT(cs, sn, -1.0, sn, ALU.mult, ALU.max, [ksn], [kcs])
        ACT(cs, cs, AF.Sin, [kcs, "small"], [kcs], scale=-1.0, bias=halfpi)
        ACT(sn, sn, AF.Sin, [ksn], [ksn])

    DMA(vecs[:, :], vecs_d[:, :], "vecs", [], ["vecs"])
    DMA(consts[:, 0:256], consts_d[:, 0:256], "consts", [], ["const"])
    DMA(consts[:, 256:768], consts_d[:, 640:1152], "consts", [], ["const"])
    cstage = arena[:, 0:384]
    DMA(cstage, consts_d[:, 256:640], "cstage", [], ["cstage"])
    tri01_f = cstage[:, 0:128]
    nti_f = cstage[:, 128:256]
    negbig_f = cstage[:, 256:384]
    for c in range(8):
        DMA(xT[:, c, :], xT_d[:, c, :], "xload%d" % c, [], ["x%d_%d" % (c, n) for n in range(NB)])
    CP("gpsimd", ident16, ident_f, ["const"], ["c16"])
    TS("gpsimd", negones16, ones_f, -1.0, None, ALU.mult, None, ["const"], ["c16"])
    CP("gpsimd", tri01_16, tri01_f, ["cstage"], ["c16"])
    CP("gpsimd", nti16, nti_f, ["cstage"], ["c16"])
    CP("gpsimd", negbig16, negbig_f, ["cstage"], ["c16"])
    MEMSET("gpsimd", zeros16, 0.0, ["c16"])
    CP("gpsimd", ones16, ones_f, ["const"], ["c16"])
    TS("gpsimd", halfid16, ident_f, 0.5, None, ALU.mult, None, ["const"], ["c16"])
    MEMSET("gpsimd", small[:, :], 0.0, ["small", "e512"])
    sc = small[:, 0:8]
    mods = [small[:, 8:32], small[:, 32:56]]
    gsv = [small[:, 56:64], small[:, 64:72]]
    negpi = small[:, 72:73]
    onec = small[:, 73:74]
    MEMSET("gpsimd", negpi, -PI, ["small"])
    MEMSET("gpsimd", onec, 1.0, ["small"])
    halfpi = small[:, 74:75]
    MEMSET("gpsimd", halfpi, PI / 2.0, ["small"])
    ACT(sc, V("cT"), AF.Silu, ["vecs", "small"], ["small"])

    s5 = sb("s5", [128, 16 * 16], F32)

    def S(i):
        return s5[:, i * 16:(i + 1) * 16]
    dt_, xr, th, rr, asn, acs, sn_, cs_, are, aim, den, cr, ci, t1_, t2_, nci = [S(i) for i in range(16)]
    K5 = ["s5"]
    c512 = small[:, 144:160]
    s512 = small[:, 160:176]
    ns512 = small[:, 176:192]
    tabd = nc.dram_tensor("tabd", [16, 128, 1024], F32)
    if do_l0:
        ACT(dt_, V("logdt"), AF.Exp, ["vecs"], K5)
        TT("vector", xr, V("lamre"), dt_, ALU.mult, K5 + ["vecs"], K5)
        TT("vector", th, V("lamim"), dt_, ALU.mult, K5 + ["vecs"], K5)
        ACT(rr, xr, AF.Exp, K5, K5)
        TS("vector", asn, th, 512.0, None, ALU.mult, None, K5, K5)
        sincos(asn, s512, c512, "s5", "e512", "e512")
        TS("vector", ns512, s512, -1.0, None, ALU.mult, None, ["e512"], ["e512"])

    def gen_table(q):
        if True:
            base, kb_ = pshort.next()
            sn, ksn = plong.next()
            cs, kcs = plong.next()
            ACT(base, iota_f, AF.Identity, ["const", "s5"], [kb_], scale=th[:, q:q + 1])
            sincos(base, sn, cs, kb_, ksn, kcs)
            DMA(tabd[q, :, 0:512], cs, "to" + kcs, [kcs], ["tabd%d" % q], q="gpsimd")
            DMA(tabd[q, :, 512:1024], sn, "to" + ksn, [ksn], ["tabd%d" % q], q="gpsimd")

    sc16 = sb("sc16", [128, 8], BF16)[:, :]
    CP("vector", sc16, sc, ["small"], ["sc16"])
    for l in range(2 if ADA1_UPFRONT else 1):
        for kc in range(8):
            if do_l0:
                gen_table(l * 8 + kc)
            ps, kp = psum("ada", [2])
            for g in range(6):
                st, ks = tmpf.next()
                DMA(st, wada_d[l][kc, :, g * 512:(g + 1) * 512], "d" + ks, [], [ks])
                w16_, k16 = tmph.next()
                CP("vector", w16_, st, [ks], [k16])
                for j in range(4):
                    MM(ps[:, g * 4 + j:g * 4 + j + 1], w16_[:, j * 128:(j + 1) * 128], sc16[:, kc:kc + 1], True, True,
                       [k16, "sc16"], [kp])
            if kc == 0:
                CP("vector", mods[l], ps[:, 0:24], [kp], ["small"])
            else:
                TT("vector", mods[l], mods[l], ps[:, 0:24], ALU.add, [kp, "small"], ["small"])
        bname = "b_ada%d" % l
        TT("vector", mods[l], mods[l], V(bname), ALU.add, ["small", "vecs"], ["small"])
        TS("vector", gsv[l], mods[l][:, 8:16], 1.0, None, ALU.add, None, ["small"], ["small"])
        TT("vector", gsv[l], gsv[l], V("norm_g%d" % l), ALU.mult, ["small", "vecs"], ["small"])
    if do_l0 and not ADA1_UPFRONT:
        for q_ in range(8, 16):
            gen_table(q_)

    def ada1_ring_tasks():
        tasks = []
        psst = {}

        def mk(kc, g):
            def f():
                i_ = wring.i % 6
                _, ks = wring.next()
                st = wring_f32[i_]
                DMA(st, wada_d[1][kc, :, g * 512:(g + 1) * 512], "d" + ks, [], [ks])
                j_ = wring.i % 6
                w16t, k16 = wring.next()
                w16_ = w16t[:, 0:512]
                CP("vector", w16_, st, [ks], [k16])
                ps, kp = psum("misc", [7])
                for j in range(4):
                    MM(ps[:, j:j + 1], w16_[:, j * 128:(j + 1) * 128], sc16[:, kc:kc + 1], True, True,
                       [k16, "sc16"], [kp])
                dst = mods[1][:, g * 4:(g + 1) * 4]
                if kc == 0:
                    CP("vector", dst, ps[:, 0:4], [kp], ["mod1"])
                else:
                    TT("vector", dst, dst, ps[:, 0:4], ALU.add, [kp, "mod1"], ["mod1"])
            return f
        for kc in range(8):
            for g in range(6):
                tasks.append(mk(kc, g))

        def fin():
            TT("vector", mods[1], mods[1], V("b_ada1"), ALU.add, ["mod1", "vecs"], ["mod1"])
            TS("vector", gsv[1], mods[1][:, 8:16], 1.0, None, ALU.add, None, ["mod1"], ["mod1"])
            TT("vector", gsv[1], gsv[1], V("norm_g1"), ALU.mult, ["mod1", "vecs"], ["mod1"])
        tasks.append(fin)
        return tasks

    def ada1_tasks():
        tasks = []

        def mk(kc, g):
            def f():
                st, ks = wst.next()
                DMA(st, wada_d[1][kc, :, g * 1024:(g + 1) * 1024], ks, [], [ks])
                ps, kp = psum("misc", [7])
                for j in range(8):
                    MM(ps[:, j:j + 1], st[:, j * 128:(j + 1) * 128], sc[:, kc:kc + 1], True, True, [ks], [kp])
                dst = mods[1][:, g * 8:(g + 1) * 8]
                if kc == 0:
                    CP("vector", dst, ps[:, 0:8], [kp], ["mod1"])
                else:
                    TT("vector", dst, dst, ps[:, 0:8], ALU.add, [kp, "mod1"], ["mod1"])
            return f
        for kc in range(8):
            for g in range(3):
                tasks.append(mk(kc, g))

        def fin():
            TT("vector", mods[1], mods[1], V("b_ada1"), ALU.add, ["mod1", "vecs"], ["mod1"])
            TS("vector", gsv[1], mods[1][:, 8:16], 1.0, None, ALU.add, None, ["mod1"], ["mod1"])
            TT("vector", gsv[1], gsv[1], V("norm_g1"), ALU.mult, ["mod1", "vecs"], ["mod1"])
        tasks.append(fin)
        return tasks
    shiftv = [mods[0][:, 0:8], mods[1][:, 0:8]]
    gatev = [mods[0][:, 16:24], mods[1][:, 16:24]]

    def rms_rstd(n, tag):
        blk = slice(n * TB, (n + 1) * TB)
        ps, kp = psum("stat", [2, 3])
        for c in range(8):
            sq, ksq = tmpf.next()
            ACT(sq, xT[:, c, blk], AF.Square, ["x%d_%d" % (c, n)], [ksq])
            MM(ps, ones_f, sq, c == 0, c == 7, [ksq, "const"], [kp])
        rs, krs = rsd[:, :], "rsd"
        TS("vector", rs, ps, 1.0 / D, EPS, ALU.mult, ALU.add, [kp], [krs])
        ACT(rs, rs, AF.Ln, [krs], [krs])
        ACT(rs, rs, AF.Exp, [krs], [krs], scale=-0.5)
        return rs, krs

    def load_w(src, width, eng="scalar", scratch=None, skey=None, reload=False):
        if reload:
            wb, kb = wring.next()
            DMA(wb[:, 0:width], scratch, "d" + kb, [skey], [kb])
            return wb, kb
        st, ks = wst.next()
        DMA(st[:, 0:width], src, ks, [], [ks])
        wb, kb = wbf.next()
        if eng == "scalar":
            ACT(wb[:, 0:width], st[:, 0:width], AF.Copy, [ks], [kb])
        else:
            CP(eng, wb[:, 0:width], st[:, 0:width], [ks], [kb])
        if scratch is not None:
            DMA(scratch, wb[:, 0:width], "ws" + kb, [kb], [skey])
        return wb, kb

    if do_l0:
        hT = arena16[:, 0:4096].rearrange("p (c t) -> p c t", c=8)
        convo = arena16[:, 4096:8192].rearrange("p (c t) -> p c t", c=8)
        y16 = arena16[:, 8192:14336].rearrange("p (c t) -> p c t", c=12)
        u16 = arena16[:, 14336:16384].rearrange("p (c t) -> p c t", c=4)
        yg = arena[:, 8192:10240].rearrange("p (c t) -> p c t", c=4)
        yg16 = arena16[:, 20480:22528].rearrange("p (c t) -> p c t", c=4)
        halo = arena16[:, 22528:22784].rearrange("p (c t) -> p c t", c=8)
        bbT = [arena16[:, 22784:24832].rearrange("p (q m) -> p q m", q=16),
               arena16[:, 24832:26880].rearrange("p (q m) -> p q m", q=16)]
        cT16 = [arena16[:, 26880:28928].rearrange("p (q m) -> p q m", q=16),
                arena16[:, 28928:30976].rearrange("p (q m) -> p q m", q=16)]
        wglu16 = arena16[:, 30976:33024]
        stg = [arena[:, 0:2048], arena[:, 2048:4096], arena[:, 4096:6144], arena[:, 6144:8192]]
        DMA(stg[0], bpre_d[:, :], "stg0", [], ["stg0", "cstage"])
        DMA(stg[1], bpim_d[:, :], "stg1", [], ["stg1"])
        sincos(th, sn_, cs_, "s5", "s5", "s5")
        TT("vector", are, rr, cs_, ALU.mult, K5, K5)
        TT("vector", aim, rr, sn_, ALU.mult, K5, K5)
        TS("vector", are, are, -1.0, None, ALU.add, None, K5, K5)
        TT("vector", den, V("lamre"), V("lamre"), ALU.mult, ["vecs"], K5)
        TT("vector", t1_, V("lamim"), V("lamim"), ALU.mult, ["vecs"], K5)
        TT("vector", den, den, t1_, ALU.add, K5, K5)
        RECIP(den, den, K5, K5)
        TT("vector", t1_, are, V("lamre"), ALU.mult, K5 + ["vecs"], K5)
        TT("vector", t2_, aim, V("lamim"), ALU.mult, K5 + ["vecs"], K5)
        TT("vector", t1_, t1_, t2_, ALU.add, K5, K5)
        TT("vector", cr, t1_, den, ALU.mult, K5, K5)
        TT("vector", t1_, aim, V("lamre"), ALU.mult, K5 + ["vecs"], K5)
        TT("vector", t2_, are, V("lamim"), ALU.mult, K5 + ["vecs"], K5)
        TT("vector", t1_, t1_, t2_, ALU.subtract, K5, K5)
        TT("vector", ci, t1_, den, ALU.mult, K5, K5)
        TS("vector", nci, ci, -1.0, None, ALU.mult, None, K5, K5)
        for q in range(16):
            qs = slice(q * 128, (q + 1) * 128)
            t_a, ka = tmpf.next()
            TS("vector", t_a[:, 0:128], stg[0][:, qs], cr[:, q:q + 1], None, ALU.mult, None, ["stg0"] + K5, [ka])
            STT(t_a[:, 0:128], stg[1][:, qs], nci[:, q:q + 1], t_a[:, 0:128], ALU.mult, ALU.add,
                ["stg1", ka] + K5, [ka])
            TS("vector", t_a[:, 128:256], stg[1][:, qs], cr[:, q:q + 1], None, ALU.mult, None, ["stg1"] + K5, [ka])
            STT(t_a[:, 128:256], stg[0][:, qs], ci[:, q:q + 1], t_a[:, 128:256], ALU.mult, ALU.add,
                ["stg0", ka] + K5, [ka])
            ps, kp = psum("misc", [7])
            for comp in range(2):
                src = t_a[:, comp * 128:(comp + 1) * 128]
                dst = ps[:, comp * 128:(comp + 1) * 128]
                P.op("tensor", lambda e, dst=dst, src=src: e.transpose(dst, src, ident_f), [ka, "const"], [kp])
            CP("vector", bbT[0][:, q, :], ps[:, 0:128], [kp], ["bbT"])
            CP("vector", bbT[1][:, q, :], ps[:, 128:256], [kp], ["bbT"])
        DMA(stg[2], cpre_d[:, :], "stg2", [], ["stg2"])
        DMA(stg[3], cpim_d[:, :], "stg3", [], ["stg3"])
        CP("gpsimd", cT16[0].rearrange("p q m -> p (q m)"), stg[2], ["stg2"], ["cT16"])
        TS("gpsimd", cT16[1].rearrange("p q m -> p (q m)"), stg[3], -1.0, None, ALU.mult, None, ["stg3"], ["cT16"])
        DMA(stg[0], wglu_d[:, :], "stg0", [], ["stg0"])
        CP("gpsimd", wglu16, stg[0], ["stg0"], ["wglu16"])
        P.barrier()

        for c in range(8):
            o0 = _VEC["conv_w"][0] + c * 31
            identb = bass.AP(c16, 896, [[8 * 128, 128], [0, 31], [1, 128]])
            wbc = bass.AP(vecs, o0, [[NV, 128], [1, 31], [0, 128]])
            dg3 = dgt.rearrange("p (k m) -> p k m", k=31)
            TT("vector", dg3, identb, wbc, ALU.mult, ["c16", "vecs"], ["dgA", "dgB"])
            DMA(dgd[c, :, :], dgt, "dgout", ["dgA", "dgB"], ["dgd%d" % c])
        def proj(m):
            wb, kb = load_w(win0_d[m, :, :], 1024, scratch=win0b[m, :, :], skey="win0b%d" % m, reload=(curblk["n"] > 0))
            ps, kp = psum("proj", [0, 1])
            for c in range(8):
                MM(ps, wb[:, c * 128:(c + 1) * 128], hT[:, c, :], c == 0, c == 7, [kb, "hT%d" % c], [kp])
            return ps, kp

        def head_ops(n):
            blk = slice(n * TB, (n + 1) * TB)
            ops = []
            st = {}

            def f_rms():
                st["rs"] = rms_rstd(n, "l0")
            ops.append(f_rms)

            def mk_h(c):
                def f():
                    rs, krs = st["rs"]
                    tmp, kt = tmpf.next()
                    STT(tmp, xT[:, c, blk], gsv[0][:, c:c + 1], rs, ALU.mult, ALU.mult,
                        ["x%d_%d" % (c, n), krs, "small"], [kt])
                    ACT(hT[:, c, :], tmp, AF.Identity, [kt, "small"], ["hT%d" % c], bias=shiftv[0][:, c:c + 1])
                return f
            for c in range(8):
                ops.append(mk_h(c))

            def mk_u(cc):
                def f():
                    psu, kpu = proj(24 + cc)
                    ACT(u16[:, cc, :], psu, AF.Identity, [kpu], ["u16_%d" % cc])
                return f
            for cc in range(4):
                ops.append(mk_u(cc))
            return ops

        tail_prev = []
        curblk = {"n": 0}
        ada1_q = ada1_ring_tasks() if (do_l1 and not ADA1_UPFRONT) else []
        for n in range(NB):
            blk = slice(n * TB, (n + 1) * TB)
            if n == 1:
                while tail_prev:
                    tail_prev.pop(0)()
                P.barrier()
            curblk["n"] = n
            hops = head_ops(n)
            while hops or tail_prev:
                if hops:
                    hops.pop(0)()
                if tail_prev:
                    tail_prev.pop(0)()

            cst = {}

            def conv_c1a(c):
                psg, kg = proj(8 + c)
                sig, ksig = tmpf.next()
                ACT(sig, psg, AF.Tanh, [kg], [ksig], scale=0.5)
                psv, kv = proj(c)
                cst[c] = {"sig": sig, "ksig": ksig, "psv": psv, "kv": kv}

            def conv_c1b(c):
                d = cst[c]
                sig, ksig, psv, kv = d["sig"], d["ksig"], d["psv"], d["kv"]
                DMA(dgt[:, 0:2048], dgd[c, :, 0:2048], "dgA", ["dgd%d" % c], ["dgA"])
                DMA(dgt[:, 2048:3968], dgd[c, :, 2048:3968], "dgB", ["dgd%d" % c], ["dgB"])
                glu, kgl = glup.next()
                if n == 0:
                    MEMSET("gpsimd", glu[:, 0:30], 0.0, [kgl])
                else:
                    CP("gpsimd", glu[:, 0:30], halo[:, c, 0:30], ["halo%d" % c], [kgl])
                STT(glu[:, 30:542], sig, 1.0, psv, ALU.add, ALU.mult, [kv, ksig], [kgl])
                if n < NB - 1:
                    CP("gpsimd", halo[:, c, 0:30], glu[:, 512:542], [kgl], ["halo%d" % c])
                d["glu"] = glu; d["kgl"] = kgl

            def conv_c2(c):
                d = cst.pop(c)
                glu, kgl = d["glu"], d["kgl"]
                cps, kcp = psum("conv", [2, 3])
                for k in range(31):
                    MM(cps, dgt[:, k * 128:(k + 1) * 128], glu[:, k:k + 512], k == 0, k == 30,
                       ["dgA" if k < 16 else "dgB", kgl], [kcp])
                ACT(convo[:, c, :], cps, AF.Identity, [kcp, "vecs"], ["convo%d" % c], bias=V("conv_b", c, c + 1))

            pst = {}
            ysta = {}

            def s5_p2a_act(q):
                sn, ksn = plong.next()
                cs, kcs = plong.next()
                DMA(cs, tabd[q, :, 0:512], "d" + kcs, ["tabd%d" % q], [kcs])
                DMA(sn, tabd[q, :, 512:1024], "d" + ksn, ["tabd%d" % q], [ksn])
                pst[q] = {"sn": sn, "ksn": ksn, "cs": cs, "kcs": kcs}
                if n > 0:
                    st_re = small[:, 96 + q:97 + q]
                    st_im = small[:, 112 + q:113 + q]
                    kst = "st%d" % q
                    i_re = small[:, 192 + q:193 + q]
                    i_im = small[:, 208 + q:209 + q]
                    t_a = small[:, 224 + q:225 + q]
                    t_b = small[:, 240 + q:241 + q]
                    TS("gpsimd", t_a, st_re, c512[:, q:q + 1], None, ALU.mult, None, [kst, "e512"], [kst])
                    TS("gpsimd", t_b, st_im, ns512[:, q:q + 1], None, ALU.mult, None, [kst, "e512"], [kst])
                    TT("gpsimd", i_re, t_a, t_b, ALU.add, [kst], [kst])
                    TS("gpsimd", t_a, st_re, s512[:, q:q + 1], None, ALU.mult, None, [kst, "e512"], [kst])
                    TS("gpsimd", t_b, st_im, c512[:, q:q + 1], None, ALU.mult, None, [kst, "e512"], [kst])
                    TT("gpsimd", i_im, t_a, t_b, ALU.add, [kst], [kst])

            def s5_p2a_dve(q):
                pass

            def s5_p2a_sin(q):
                pass

            def s5_p1(q):
                cc = q // 4
                bre, kbre = psum("s5b", [4, 5])
                bim, kbim = psum("s5b", [4, 5])
                MM(bre, bbT[0][:, q, :], u16[:, cc, :], True, True, ["bbT", "u16_%d" % cc], [kbre])
                MM(bim, bbT[1][:, q, :], u16[:, cc, :], True, True, ["bbT", "u16_%d" % cc], [kbim])
                pst[q].update({"bre": bre, "kbre": kbre, "bim": bim, "kbim": kbim})

            def s5_p2b(q):
                d = pst[q]
                bre, kbre, bim, kbim = d["bre"], d["kbre"], d["bim"], d["kbim"]
                sn, ksn, cs, kcs = d["sn"], d["ksn"], d["cs"], d["kcs"]
                btr, kbtr = plong.next()
                bti, kbti = plong.next()
                m2, km2 = pshort.next()
                TT("vector", btr, bre, cs, ALU.mult, [kbre, kcs], [kbtr])
                TT("vector", m2, bim, sn, ALU.mult, [kbim, ksn], [km2])
                TT("vector", btr, btr, m2, ALU.add, [kbtr, km2], [kbtr])
                TT("vector", bti, bim, cs, ALU.mult, [kbim, kcs], [kbti])
                TT("vector", m2, bre, sn, ALU.mult, [kbre, ksn, km2], [km2])
                TT("vector", bti, bti, m2, ALU.subtract, [kbti, km2], [kbti])
                rb = bass.AP(s5, 3 * 16 + q, [[256, 128], [0, 512]])
                st_re = small[:, 96 + q:97 + q]
                st_im = small[:, 112 + q:113 + q]
                kst = "st%d" % q
                if n > 0:
                    i_re = small[:, 192 + q:193 + q]
                    i_im = small[:, 208 + q:209 + q]
                for (bt, kbt, stv) in ((btr, kbtr, (i_re if n > 0 else None)), (bti, kbti, (i_im if n > 0 else None))):
                    init = 0.0 if n == 0 else stv
                    P.op("vector", lambda e, bt=bt, init=init, rb=rb: e.tensor_tensor_scan(
                        out=bt, data0=rb, data1=bt, initial=init, op0=ALU.mult, op1=ALU.add),
                        [kbt, "s5", kst], [kbt])
                if n < NB - 1:
                    ACT(st_re, btr[:, 511:512], AF.Copy, [kbtr], [kst])
                    ACT(st_im, bti[:, 511:512], AF.Copy, [kbti], [kst])
                d.update({"btr": btr, "kbtr": kbtr, "bti": bti, "kbti": kbti})

            def s5_p3(q):
                d = pst[q]
                sn, ksn, cs, kcs = d["sn"], d["ksn"], d["cs"], d["kcs"]
                btr, kbtr, bti, kbti = d["btr"], d["kbtr"], d["bti"], d["kbti"]
                sre, ksre = tmph.next()
                sim, ksim = tmph.next()
                g1, kg1 = pg.next()
                g2, kg2 = pg.next()
                TT(P3ENG, g1, btr, cs, ALU.mult, [kbtr, kcs], [kg1])
                TT(P3ENG, g2, bti, sn, ALU.mult, [kbti, ksn], [kg2])
                TT(P3ENG, sre, g1, g2, ALU.subtract, [kg1, kg2], [ksre])
                TT(P3ENG, g1, btr, sn, ALU.mult, [kbtr, ksn, kg1], [kg1])
                TT(P3ENG, g2, bti, cs, ALU.mult, [kbti, kcs, kg2], [kg2])
                TT(P3ENG, sim, g1, g2, ALU.add, [kg1, kg2], [ksim])
                d.update({"sre": sre, "ksre": ksre, "sim": sim, "ksim": ksim})

            def s5_p4(q):
                cc = q // 4
                d = pst.pop(q)
                sre, ksre, sim, ksim = d["sre"], d["ksre"], d["sim"], d["ksim"]
                if q % 4 == 0:
                    ysta["y"] = psum("s5y", [6])
                y_ps, ky = ysta["y"]
                MM(y_ps, cT16[0][:, q, :], sre, q % 4 == 0, False, ["cT16", ksre], [ky])
                MM(y_ps, cT16[1][:, q, :], sim, False, q % 4 == 3, ["cT16", ksim], [ky])
                if q % 4 == 3:
                    yb, kyb = tmpf.next()
                    STT(yb, u16[:, cc, :], V("ssm_d", cc, cc + 1), y_ps, ALU.mult, ALU.add,
                        ["u16_%d" % cc, "vecs", ky], [kyb])
                    ACT(yg[:, cc, :], yb, AF.Gelu_apprx_tanh, [kyb], ["yg%d" % cc])
                    ACT(yg16[:, cc, :], yg[:, cc, :], AF.Copy, ["yg%d" % cc], ["yg16_%d" % cc])

            lnst = {}

            def ln_stats():
                s_ps, ks_ = psum("stat", [2, 3])
                q_ps, kq_ = psum("stat", [2, 3])
                for c in range(8):
                    MM(s_ps, ones16, convo[:, c, :], c == 0, c == 7, ["convo%d" % c, "c16"], [ks_])
                for c in range(8):
                    sq, ksq = pg.next()
                    ACT(sq, convo[:, c, :], AF.Square, ["convo%d" % c], [ksq])
                    MM(q_ps, ones_f, sq, c == 0, c == 7, [ksq, "const"], [kq_])
                mean, kmean = lnm[:, :], "lnm"
                TS("vector", mean, s_ps, 1.0 / 1024, None, ALU.mult, None, [ks_], [kmean])
                rl, krl = lnr[:, :], "lnr"
                TT("vector", rl, mean, mean, ALU.mult, [kmean], [krl])
                STT(rl, q_ps, 1.0 / 1024, rl, ALU.mult, ALU.subtract, [kq_, krl], [krl])
                TS("vector", rl, rl, EPS, None, ALU.add, None, [krl], [krl])
                ACT(rl, rl, AF.Ln, [krl], [krl])
                ACT(rl, rl, AF.Exp, [krl], [krl], scale=-0.5)

            nst = {}

            def norm_a(c):
                mean, kmean = lnm[:, :], "lnm"
                rl, krl = lnr[:, :], "lnr"
                t1, k1 = tmpf.next()
                TT("vector", t1, convo[:, c, :], mean, ALU.subtract, ["convo%d" % c, kmean], [k1])
                TT("vector", t1, t1, rl, ALU.mult, [k1, krl], [k1])
                ACT(t1, t1, AF.Silu, [k1, "vecs"], [k1], scale=V("ln_g", c, c + 1), bias=V("ln_b", c, c + 1))
                psa, kpa = proj(16 + c)
                sga, ksga = tmpf.next()
                ACT(sga, psa, AF.Silu, [kpa], [ksga])
                nst[c] = (t1, k1, sga, ksga)

            def norm_b(c):
                t1, k1, sga, ksga = nst.pop(c)
                TT("vector", y16[:, c, :], t1, sga, ALU.mult, [k1, ksga], ["y16_%d" % c])

            for k in range(21):
                if n >= 1 and ada1_q:
                    ada1_q.pop(0)()
                if 0 <= k - 2 < 16:
                    s5_p3(k - 2)
                if k < 16:
                    s5_p2a_act(k)
                if 0 <= k - 1 < 16:
                    s5_p2b(k - 1)
                if 0 <= k - 3 < 16:
                    s5_p4(k - 3)
                if k == 11:
                    ln_stats()
                if 0 <= k - 2 < 8:
                    conv_c2(k - 2)
                if k < 16:
                    s5_p1(k)
                    s5_p2a_dve(k)
                    s5_p2a_sin(k)
                if 0 <= k - 1 < 8:
                    conv_c1b(k - 1)
                if k < 8:
                    conv_c1a(k)
                if 8 <= k < 12:
                    mo_ = k - 8
                    psb_, kpb = proj(28 + mo_)
                    ACT(y16[:, 8 + mo_, :], psb_, AF.Silu, [kpb], ["y16_%d" % (8 + mo_)])
                if 0 <= k - 13 < 8:
                    norm_b(k - 13)
                if 0 <= k - 12 < 8:
                    norm_a(k - 12)
            def mk_glu(mo, n=n):
                def f():
                    ps, kp = psum("misc", [7])
                    for k in range(4):
                        MM(ps, wglu16[:, k * 512 + mo * 128:k * 512 + (mo + 1) * 128], yg16[:, k, :], k == 0, k == 3,
                           ["wglu16", "yg16_%d" % k], [kp])
                    sg, ksg = tmpf.next()
                    ACT(sg, ps, AF.Sigmoid, [kp, "vecs"], [ksg], bias=V("b_glu", mo, mo + 1))
                    TT("vector", sg, sg, yg[:, mo, :], ALU.mult, [ksg, "yg%d" % mo], [ksg])
                    ky_ = "y16_%d" % (8 + mo)
                    TT("vector", y16[:, 8 + mo, :], sg, y16[:, 8 + mo, :], ALU.mult, [ksg, ky_], [ky_])
                return f

            def mk_wout(mo, n=n, blk=blk):
                def f():
                    wb, kb = load_w(wout0_d[mo, :, 0:1024], 1024, scratch=wout0b[mo, :, 0:1024],
                                    skey="wout0bA%d" % mo, reload=(n > 0))
                    wb2, kb2 = load_w(wout0_d[mo, :, 1024:1536], 512, scratch=wout0b[mo, :, 1024:1536],
                                      skey="wout0bB%d" % mo, reload=(n > 0))
                    ps, kp = psum("misc", [7])
                    for k in range(8):
                        MM(ps, wb[:, k * 128:(k + 1) * 128], y16[:, k, :], k == 0, False, [kb, "y16_%d" % k], [kp])
                    for k in range(8, 12):
                        MM(ps, wb2[:, (k - 8) * 128:(k - 7) * 128], y16[:, k, :], False, k == 11,
                           [kb2, "y16_%d" % k], [kp])
                    xk = "x%d_%d" % (mo, n)
                    STT(xT[:, mo, blk], ps, gatev[0][:, mo:mo + 1], xT[:, mo, blk], ALU.mult, ALU.add,
                        [kp, "small", xk], [xk])
                return f
            tail_prev = [mk_glu(mo) for mo in range(4)] + [mk_wout(mo) for mo in range(8)]
        for f in tail_prev:
            f()
        while ada1_q:
            ada1_q.pop(0)()
        P.barrier()

    if do_l1:
        hT1 = arena16[:, 0:16384].rearrange("p (c t) -> p c t", c=8)
        def mkset(base16, o):
            return {
                "qT": base16[:, o:o + 2048],
                "kpad": [base16[:, o + 2048:o + 4096], base16[:, o + 4096:o + 6144]],
                "vpad": [base16[:, o + 6144:o + 8192].rearrange("p (t m) -> p t m", t=16),
                         base16[:, o + 8192:o + 10240].rearrange("p (t m) -> p t m", t=16)],
                "gsl": base16[:, o + 10240:o + 12288],
                "opair": base16[:, o + 12288:o + 14336],
            }
        sets = [mkset(arena16, 16384), mkset(parena16, 0)]
        ssum = [[arena16[:, 30720:31232], arena16[:, 31232:31744]],
                [arena16[:, 31744:32256], arena16[:, 32256:32768]]]
        MEMSET("gpsimd", arena16[:, 16384 + 2048:16384 + 10240], 0.0, ["kpad0_0", "kpad1_0", "vpad0_0", "vpad1_0"])
        MEMSET("gpsimd", parena16[:, 2048:10240], 0.0, ["kpad0_1", "kpad1_1", "vpad0_1", "vpad1_1"])
        for n in range(NB):
            blk = slice(n * TB, (n + 1) * TB)
            rs, krs = rms_rstd(n, "l1")
            for c in range(8):
                tmp, kt = tmpf.next()
                STT(tmp, xT[:, c, blk], gsv[1][:, c:c + 1], rs, ALU.mult, ALU.mult,
                    ["x%d_%d" % (c, n), krs, "small"], [kt])
                ACT(hT1[:, c, blk], tmp, AF.Identity, [kt, "small"], ["h1_%d_%d" % (c, n)],
                    bias=shiftv[1][:, c:c + 1])

        ZB = [0, 1, 2, 3]
        OB = [4, 5]

        def proj_tasks(hp):
            S_ = sets[hp % 2]
            sx = hp % 2
            tasks = []
            wref = {}

            def mk_load(sec):
                def t():
                    wref[sec] = load_w(win1_d[sec * 8 + hp, :, :], 1024, "vector")
                return t

            def mk_grp(sec, n):
                def t():
                    wb, kb = wref[sec]
                    ps, kp = psum("proj1", [6, 7])
                    for c in range(8):
                        MM(ps, wb[:, c * 128:(c + 1) * 128], hT1[:, c, n * TB:(n + 1) * TB], c == 0, c == 7,
                           [kb, "h1_%d_%d" % (c, n)], [kp])
                    cols = slice(n * TB, (n + 1) * TB)
                    if sec == 0:
                        TS("vector", S_["qT"][:, cols], ps, 0.125, None, ALU.mult, None, [kp], ["qT_%d" % sx])
                    elif sec == 1:
                        CP("vector", S_["kpad"][0][0:64, cols], ps[0:64, :], [kp], ["kpad0_%d" % sx])
                        CP("vector", S_["kpad"][1][64:128, cols], ps[64:128, :], [kp], ["kpad1_%d" % sx])
                    else:
                        ACT(S_["gsl"][:, cols], ps, AF.Silu, [kp], ["gsl_%d" % sx])
                return t

            def mk_v(tt):
                def t():
                    wb, kb = wref[2]
                    ps, kp = psum("proj1", [6, 7])
                    n = tt // 4
                    for c in range(8):
                        MM(ps[:, 0:128], hT1[:, c, tt * 128:(tt + 1) * 128], wb[:, c * 128:(c + 1) * 128],
                           c == 0, c == 7, [kb, "h1_%d_%d" % (c, n)], [kp])
                    CP("vector", S_["vpad"][0][:, tt, 0:64], ps[:, 0:64], [kp], ["vpad0_%d" % sx])
                    CP("vector", S_["vpad"][1][:, tt, 64:128], ps[:, 64:128], [kp], ["vpad1_%d" % sx])
                return t

            for sec in (0, 1, 3):
                tasks.append(mk_load(sec))
                for n in range(NB):
                    tasks.append(mk_grp(sec, n))
            tasks.append(mk_load(2))
            for tt in range(16):
                tasks.append(mk_v(tt))
            return tasks

        def wout_tasks(hp):
            S_ = sets[hp % 2]
            sx = hp % 2
            tasks = []
            wref = {}

            def ld():
                wref[0] = load_w(wout1_d[hp, :, :], 1024, "vector")
            tasks.append(ld)

            def mk(mo, n):
                def t():
                    wb, kb = wref[0]
                    blk = slice(n * TB, (n + 1) * TB)
                    ps, kp = psum("proj1", [6, 7])
                    MM(ps, wb[:, mo * 128:(mo + 1) * 128], S_["opair"][:, blk], True, True, [kb, "opair_%d" % sx], [kp])
                    xk = "x%d_%d" % (mo, n)
                    STT(xT[:, mo, blk], ps, gatev[1][:, mo:mo + 1], xT[:, mo, blk], ALU.mult, ALU.add,
                        [kp, "small", xk], [xk])
                return t
            for mo in range(8):
                for n in range(NB):
                    tasks.append(mk(mo, n))
            return tasks

        items = []
        for hp in range(8):
            for g4 in range(4):
                nblk = 4 * g4 + 4
                for b in range(nblk - 1, -1, -1):
                    for hi in range(2):
                        items.append((hp, hi, g4, b, nblk))
        info = {}
        hstate = {0: [0, 0], 1: [0, 0]}
        ostate = {}

        def stA1(it):
            hp, hi, g4, b, nblk = it
            S_ = sets[hp % 2]; sx = hp % 2
            T0 = g4 * TB
            c0 = max(0, b * 128 - T0)
            Z, kz = psum("Z", ZB)
            MM(Z[:, c0:TB], S_["kpad"][hi][:, b * 128:(b + 1) * 128], S_["qT"][:, T0 + c0:T0 + TB], True, True,
               ["kpad%d_%d" % (hi, sx), "qT_%d" % sx], [kz])
            info[it] = {"Z": Z, "kz": kz, "c0": c0}

        def stA2(it):
            hp, hi, g4, b, nblk = it
            d = info[it]
            Z, kz, c0 = d["Z"], d["kz"], d["c0"]
            first = (b == nblk - 1)
            e_, ke = tmpf.next()
            ACT(e_[:, c0:TB], Z[:, c0:TB], AF.Exp, [kz], [ke])
            sp, ksp = tmph.next()
            ACT(sp[:, c0:TB], e_[:, c0:TB], AF.Ln, [ke], [ksp], bias=1.0)
            if b >= 4 * g4:
                TT("vector", sp[:, c0:c0 + 128], sp[:, c0:c0 + 128], tri01_16, ALU.mult, [ksp, "c16"], [ksp])
            if first:
                MEMSET("gpsimd", ssum[hi][0], 0.0, ["ssum%d_0" % hi])
                MEMSET("gpsimd", ssum[hi][1], 0.0, ["ssum%d_1" % hi])
                hstate[hi][0] = 0
            cur = hstate[hi][0]
            d["sp"] = sp; d["ksp"] = ksp; d["cur"] = cur
            if b > 0:
                nxt = 1 - cur
                TT("vector", ssum[hi][nxt][:, c0:TB], ssum[hi][cur][:, c0:TB], sp[:, c0:TB], ALU.add,
                   ["ssum%d_%d" % (hi, cur), ksp], ["ssum%d_%d" % (hi, nxt)])
                hstate[hi][0] = nxt

        def stB1(it):
            hp, hi, g4, b, nblk = it
            d = info[it]
            Z, kz, c0, sp, ksp, cur = d["Z"], d["kz"], d["c0"], d["sp"], d["ksp"], d["cur"]
            first = (b == nblk - 1)
            diag = (b >= 4 * g4)
            kc_ = "ssum%d_%d" % (hi, cur)
            MM(Z[:, c0:TB], nti16, sp[:, c0:TB], False, (first and not diag), [ksp, "c16"], [kz], skip=True)
            if not first:
                MM(Z[:, c0:TB], negones16, ssum[hi][cur][:, c0:TB], False, not diag, [kc_, "c16"], [kz], skip=True)
            if diag:
                MM(Z[:, c0:c0 + 128], ident16, negbig16, False, True, ["c16"], [kz], skip=True)

        def stB2(it):
            d = info[it]
            Z, kz, c0 = d["Z"], d["kz"], d["c0"]
            w16, kw = tmph.next()
            ACT(w16[:, c0:TB], Z[:, c0:TB], AF.Exp, [kz], [kw])
            d["w16"] = w16; d["kw"] = kw

        def stB3(it):
            hp, hi, g4, b, nblk = it
            S_ = sets[hp % 2]; sx = hp % 2
            T0 = g4 * TB
            d = info.pop(it)
            c0, w16, kw = d["c0"], d["w16"], d["kw"]
            if b == nblk - 1:
                O, ko = psum("O%d" % hi, [OB[hi]])
                ostate[hi] = (O, ko)
                MM(O, zeros16, S_["qT"][:, 0:TB], True, False, ["c16", "qT_%d" % sx], [ko])
            O, ko = ostate[hi]
            MM(O[:, c0:TB], S_["vpad"][hi][:, b, :], w16[:, c0:TB], False, b == 0, [kw, "vpad%d_%d" % (hi, sx)], [ko])
            if b == 0:
                rows = slice(hi * 64, (hi + 1) * 64)
                TT("vector", S_["opair"][rows, T0:T0 + TB], O[rows, :], S_["gsl"][rows, T0:T0 + TB], ALU.mult,
                   [ko, "gsl_%d" % sx], ["opair_%d" % sx])

        for t in proj_tasks(0):
            t()
        NI = len(items)
        PER = NI // 8
        queue = []
        for s_ in range(NI + 4):
            p_ = s_ // PER
            r_ = s_ % PER
            if s_ < NI and r_ == 5 and p_ + 1 < 8:
                queue.extend(proj_tasks(p_ + 1))
            if s_ < NI and r_ == 6 and p_ >= 1:
                queue = wout_tasks(p_ - 1) + queue
            if 0 <= s_ - 2 < NI:
                stB1(items[s_ - 2])
            if 0 <= s_ - 4 < NI:
                stB3(items[s_ - 4])
            if s_ < NI:
                if r_ == 0:
                    while queue:
                        queue.pop(0)()
                stA1(items[s_])
            if 0 <= s_ - 1 < NI:
                stA2(items[s_ - 1])
            if 0 <= s_ - 3 < NI:
                stB2(items[s_ - 3])
            if queue:
                left = PER - r_ - 2
                ntask = len(queue) if left <= 0 else -(-len(queue) // left)
                for _ in range(min(ntask, len(queue))):
                    queue.pop(0)()
        while queue:
            queue.pop(0)()
        for t in wout_tasks(7):
            t()
        P.barrier()

    diagG = arena[:, 0:1024].rearrange("p (c m) -> p c m", c=8)
    for c in range(8):
        TS("vector", diagG[:, c, :], ident_f, V("final_g", c, c + 1), None, ALU.mult, None,
           ["const", "vecs"], ["diagG"])
    obuf = [arena[:, 1024:2048], arena[:, 2048:3072]]
    frs = None
    for tt in range(16):
        n = tt // 4
        tsl = slice(tt * 128, (tt + 1) * 128)
        if tt % 4 == 0:
            frs = rms_rstd(n, "fin")
        rs, krs = frs
        ps2, k2 = psum("fin2", [4, 5])
        MM(ps2[:, 0:1], rs[:, (tt % 4) * 128:(tt % 4 + 1) * 128], ident_f[:, 0:1], True, True, [krs, "const"], [k2])
        rt, krt = tmpf.next()
        CP("vector", rt[:, 0:1], ps2[:, 0:1], [k2], [krt])
        ob = obuf[tt % 2]
        kob = "obuf%d" % (tt % 2)
        for half in range(2):
            ps, kp = psum("fin", [0, 1])
            for j in range(4):
                c = half * 4 + j
                MM(ps[:, j * 128:(j + 1) * 128], xT[:, c, tsl], diagG[:, c, :], True, True,
                   ["x%d_%d" % (c, n), "diagG"], [kp])
            TS("vector", ob[:, half * 512:(half + 1) * 512], ps, rt[:, 0:1], None, ALU.mult, None,
               [kp, krt], [kob])
        DMA(out_d[tsl, :], ob, kob + "d", [kob], [kob])
    P.final_wait(["obuf0d", "obuf1d"])
    P.emit()
    return nc


def _host_inputs(inp, b):
    f = np.float32

    def colvec(v):
        v = np.asarray(v, f)
        return v.reshape(-1, 128).T

    vecs = np.zeros((128, NV), f)

    def put(name, arr):
        o, w = _VEC[name]
        assert arr.shape == (128, w), (name, arr.shape)
        vecs[:, o:o + w] = arr
    put("norm_g0", colvec(inp["l0_norm_g"]))
    put("b_ada0", colvec(inp["l0_b_ada"]))
    put("conv_b", colvec(inp["l0_conv_b"]))
    put("ln_g", colvec(inp["l0_conv_ln_g"]))
    put("ln_b", colvec(inp["l0_conv_ln_b"]))
    put("ssm_d", colvec(inp["l0_ssm_d"]))
    put("b_glu", colvec(inp["l0_ssm_b_glu"]))
    put("norm_g1", colvec(inp["l1_norm_g"]))
    put("b_ada1", colvec(inp["l1_b_ada"]))
    put("final_g", colvec(inp["final_norm_g"]))
    put("cT", colvec(inp["c"][b]))
    cw = np.asarray(inp["l0_conv_w"], f)
    put("conv_w", cw.T.reshape(8, 128, 31).transpose(1, 0, 2).reshape(128, 248))

    def pairlay(a):
        return np.asarray(a, f).reshape(16, 2, 64).transpose(1, 2, 0).reshape(128, 16)
    put("lamre", pairlay(inp["l0_ssm_lam_re"]))
    put("lamim", pairlay(inp["l0_ssm_lam_im"]))
    ld = np.asarray(inp["l0_ssm_log_dt"], f)
    put("logdt", pairlay(np.repeat(ld[:, None], 64, axis=1)))

    def kchunks(w, ncol):
        w = np.asarray(w, f)
        K, N = w.shape
        return np.ascontiguousarray(
            w.reshape(K // 128, 128, N // ncol, ncol).transpose(2, 1, 0, 3).reshape(N // ncol, 128, (K // 128) * ncol))

    def bpad(bmat):
        bmat = np.asarray(bmat, f)
        o = np.zeros((2, 64, 16, 128), f)
        for g in range(32):
            q, h = g // 2, g % 2
            o[h, :, q, (g % 8) * 16:(g % 8) * 16 + 16] = bmat[g]
        return o.reshape(128, 2048)

    def cpad(cmat):
        return bpad(np.asarray(cmat, f).transpose(0, 2, 1))

    d = {}
    xb = np.asarray(inp["x"][b], f)
    d["xT"] = np.ascontiguousarray(xb.reshape(L, 8, 128).transpose(2, 1, 0))
    d["vecs"] = vecs
    d["w_ada0"] = np.ascontiguousarray(np.asarray(inp["l0_w_ada"], f).reshape(8, 128, 3072))
    d["w_ada1"] = np.ascontiguousarray(np.asarray(inp["l1_w_ada"], f).reshape(8, 128, 3072))
    d["w_in0"] = kchunks(inp["l0_w_in"], 128)
    d["w_out0"] = kchunks(inp["l0_w_out"], 128)
    d["w_in1"] = kchunks(inp["l1_w_in"], 128)
    w1 = np.asarray(inp["l1_w_out"], f)
    d["w_out1"] = np.ascontiguousarray(w1.reshape(8, 128, 1024))
    wg = np.asarray(inp["l0_ssm_w_glu"], f)
    d["w_glu"] = np.ascontiguousarray(wg.reshape(4, 128, 512).transpose(1, 0, 2).reshape(128, 2048))
    d["bp_re"] = bpad(inp["l0_ssm_b_re"])
    d["bp_im"] = bpad(inp["l0_ssm_b_im"])
    d["cp_re"] = cpad(inp["l0_ssm_c_re"])
    d["cp_im"] = cpad(inp["l0_ssm_c_im"])
    cst = np.zeros((128, 5 * 128 + 512), f)
    i = np.arange(128)
    cst[:, 0:128] = np.eye(128, dtype=f)
    cst[:, 128:256] = 1.0
    cst[:, 256:384] = (i[None, :] > i[:, None]).astype(f)
    cst[:, 384:512] = -(i[:, None] >= i[None, :]).astype(f)
    cst[:, 512:640] = np.where(i[None, :] <= i[:, None], -30000.0, 0.0).astype(f)
    cst[:, 640:1152] = np.arange(512, dtype=f)[None, :]
    d["consts"] = cst
    return d


_NC_CACHE = {}


def kernel(**inputs):
    inp = {k: np.asarray(v) for k, v in inputs.items()}
    if "nc" not in _NC_CACHE:
        _NC_CACHE["nc"] = build_program()
    nc = _NC_CACHE["nc"]
    in_maps = [_host_inputs(inp, b) for b in range(8)]
    res = run_bass_kernel_spmd(nc, in_maps, core_ids=list(range(8)))
    out = np.stack([np.asarray(r["out"], np.float32).reshape(L, D) for r in res.results], axis=0)
    return out
```
